# Optimizing a Trainium2 kernel written in Bass

```python
import math
import jax, jax.numpy as jnp
from jax import lax
import numpy as np


D_MODEL = 2048
BATCH = 4
SEQ = 4096
DEPTH = 1

N_META = 16
POOL_WINDOWS = (2, 4, 8, 16)
POOL_GROUP = 256
POOL_WIDTH = POOL_GROUP * len(POOL_WINDOWS)
MLA_HEADS = 16
Q_LORA = 512
KV_LORA = 512
QK_NOPE = 128
QK_ROPE = 64
V_DIM = 128
QK_DIM = QK_NOPE + QK_ROPE
MLA_WIDTH = MLA_HEADS * V_DIM
ROPE_THETA = 10000.0
SOFTMAX_SCALE = QK_DIM ** -0.5
D_FF = 5632
Q_BLOCK = 128
EPS = 1e-6
SPLITS = (POOL_WIDTH,
          POOL_WIDTH + Q_LORA,
          POOL_WIDTH + Q_LORA + KV_LORA,
          POOL_WIDTH + Q_LORA + KV_LORA + QK_ROPE,
          POOL_WIDTH + Q_LORA + KV_LORA + QK_ROPE + D_MODEL)
IN_COLS = POOL_WIDTH + Q_LORA + KV_LORA + QK_ROPE + 2 * D_MODEL

kernel_name = 'hybrid_pool_mla_macaron_block'


def _rmsnorm(x, gain):
    x32 = x.astype(jnp.float32)
    y = x32 * lax.rsqrt(jnp.mean(x32 * x32, axis=-1, keepdims=True) + EPS)
    return (y * gain.astype(jnp.float32)).astype(x.dtype)


def _swiglu(h, w_gu, w_down):
    g, u = jnp.split(h @ w_gu, 2, axis=-1)
    return (jax.nn.silu(g) * u) @ w_down


def _rope_tables(L, dtype):
    pos = jnp.arange(L, dtype=jnp.float32)
    inv = ROPE_THETA ** (-jnp.arange(0, QK_ROPE, 2, dtype=jnp.float32) / QK_ROPE)
    ang = pos[:, None] * inv[None, :]
    ang = jnp.concatenate([ang, ang], axis=-1)
    return jnp.cos(ang).astype(dtype), jnp.sin(ang).astype(dtype)


def _rotate(x, cos, sin):
    x1, x2 = jnp.split(x, 2, axis=-1)
    return x * cos + jnp.concatenate([-x2, x1], axis=-1) * sin


def _multiscale_pool(u, pool_w, pool_scale):
    B, L, _ = u.shape
    u32 = u.astype(jnp.float32)
    cs = jnp.concatenate([jnp.zeros_like(u32[:, :1]), jnp.cumsum(u32, axis=1)], axis=1)
    hi = jnp.arange(1, L + 1)
    outs = []
    for g, w in enumerate(POOL_WINDOWS):
        csg = cs[..., g * POOL_GROUP:(g + 1) * POOL_GROUP]
        lo = jnp.maximum(hi - w, 0)
        cnt = (hi - lo).astype(jnp.float32)[None, :, None]
        mean = (csg[:, hi] - csg[:, lo]) / cnt
        outs.append(mean - u32[..., g * POOL_GROUP:(g + 1) * POOL_GROUP])
    d = jnp.stack(outs, axis=2).astype(u.dtype)
    y = jnp.einsum('blgc,gcd->blgd', d, pool_w).reshape(B, L, POOL_WIDTH)
    return y * pool_scale


def _attend_block(q_blk, q_pos, k, v, k_pos):
    s = jnp.einsum('bqhd,bkhd->bhqk', q_blk, k, preferred_element_type=jnp.float32) * SOFTMAX_SCALE
    mask = k_pos[None, :] <= q_pos[:, None]
    s = jnp.where(mask[None, None], s, jnp.float32(-1e30))
    p = jax.nn.softmax(s, axis=-1).astype(v.dtype)
    return jnp.einsum('bhqk,bkhd->bqhd', p, v)


def _mla(c_q, c_kv, k_rope, q_a_norm, w_q_b, kv_a_norm, w_kv_b, cos, sin):
    B, L, _ = c_q.shape
    q = (_rmsnorm(c_q, q_a_norm) @ w_q_b).reshape(B, L, MLA_HEADS, QK_DIM)
    q_nope, q_pe = jnp.split(q, [QK_NOPE], axis=-1)
    q_pe = _rotate(q_pe, cos[:, None, :], sin[:, None, :])
    kv = (_rmsnorm(c_kv, kv_a_norm) @ w_kv_b).reshape(B, L, MLA_HEADS, QK_NOPE + V_DIM)
    k_nope, v = jnp.split(kv, [QK_NOPE], axis=-1)
    k_pe = _rotate(k_rope, cos, sin)
    q = jnp.concatenate([q_nope, q_pe], axis=-1)
    k = jnp.concatenate([k_nope, jnp.broadcast_to(k_pe[:, :, None, :], (B, L, MLA_HEADS, QK_ROPE))], axis=-1)
    pos = jnp.arange(L)
    o_meta = _attend_block(q[:, :N_META], pos[:N_META], k[:, :N_META], v[:, :N_META], pos[:N_META])
    n_blk = (L - N_META) // Q_BLOCK
    q_real = q[:, N_META:].reshape(B, n_blk, Q_BLOCK, MLA_HEADS, QK_DIM).transpose(1, 0, 2, 3, 4)
    pos_real = pos[N_META:].reshape(n_blk, Q_BLOCK)
    o_real = lax.map(lambda a: _attend_block(a[0], a[1], k, v, pos), (q_real, pos_real))
    o_real = o_real.transpose(1, 0, 2, 3, 4).reshape(B, L - N_META, MLA_HEADS, V_DIM)
    o = jnp.concatenate([o_meta, o_real], axis=1)
    return o.reshape(B, L, MLA_WIDTH)


def _hybrid_mixer(h, w_in, pool_w, pool_scale, w_pool_o, q_a_norm, w_q_b, kv_a_norm, w_kv_b,
                  w_mla_o, w_out, cos, sin):
    z = h @ w_in
    u_pool, c_q, c_kv, k_rope, g_pool, g_mla = jnp.split(z, SPLITS, axis=-1)
    y_pool = _multiscale_pool(u_pool, pool_w, pool_scale) @ w_pool_o
    y_mla = _mla(c_q, c_kv, k_rope, q_a_norm, w_q_b, kv_a_norm, w_kv_b, cos, sin) @ w_mla_o
    y = jax.nn.sigmoid(g_pool) * y_pool + jax.nn.sigmoid(g_mla) * y_mla
    return y @ w_out


def setup_inputs(seed: int = 0) -> dict:
    key = jax.random.key(seed)
    ks = jax.random.split(key, 32)

    def dense(k, shape, fan_in):
        return jax.random.normal(k, shape, jnp.float32) * (fan_in ** -0.5)

    def gain(k, shape):
        return 1.0 + 0.1 * jax.random.normal(k, shape, jnp.float32)

    return {
        'x': jax.random.normal(ks[0], (BATCH, SEQ, D_MODEL), jnp.float32),
        'meta_tokens': jax.random.normal(ks[1], (N_META, D_MODEL), jnp.float32),
        'norm_ffn1_pre': gain(ks[2], (DEPTH, D_MODEL)),
        'norm_ffn1_post': gain(ks[3], (DEPTH, D_MODEL)),
        'ffn1_w_gu': dense(ks[4], (DEPTH, D_MODEL, 2 * D_FF), D_MODEL),
        'ffn1_w_down': dense(ks[5], (DEPTH, D_FF, D_MODEL), D_FF),
        'norm_mix_pre': gain(ks[6], (DEPTH, D_MODEL)),
        'norm_mix_post': gain(ks[7], (DEPTH, D_MODEL)),
        'w_in': dense(ks[8], (DEPTH, D_MODEL, IN_COLS), D_MODEL),
        'pool_w': dense(ks[9], (DEPTH, len(POOL_WINDOWS), POOL_GROUP, POOL_GROUP), POOL_GROUP),
        'pool_scale': gain(ks[10], (DEPTH, POOL_WIDTH)),
        'w_pool_o': dense(ks[11], (DEPTH, POOL_WIDTH, D_MODEL), POOL_WIDTH),
        'q_a_norm': gain(ks[12], (DEPTH, Q_LORA)),
        'w_q_b': dense(ks[13], (DEPTH, Q_LORA, MLA_HEADS * QK_DIM), Q_LORA),
        'kv_a_norm': gain(ks[14], (DEPTH, KV_LORA)),
        'w_kv_b': dense(ks[15], (DEPTH, KV_LORA, MLA_HEADS * (QK_NOPE + V_DIM)), KV_LORA),
        'w_mla_o': dense(ks[16], (DEPTH, MLA_WIDTH, D_MODEL), MLA_WIDTH),
        'w_out': dense(ks[17], (DEPTH, D_MODEL, D_MODEL), D_MODEL),
        'norm_ffn2_pre': gain(ks[18], (DEPTH, D_MODEL)),
        'norm_ffn2_post': gain(ks[19], (DEPTH, D_MODEL)),
        'ffn2_w_gu': dense(ks[20], (DEPTH, D_MODEL, 2 * D_FF), D_MODEL),
        'ffn2_w_down': dense(ks[21], (DEPTH, D_FF, D_MODEL), D_FF),
    }


def reference(x, meta_tokens, norm_ffn1_pre, norm_ffn1_post, ffn1_w_gu, ffn1_w_down,
              norm_mix_pre, norm_mix_post, w_in, pool_w, pool_scale, w_pool_o,
              q_a_norm, w_q_b, kv_a_norm, w_kv_b, w_mla_o, w_out,
              norm_ffn2_pre, norm_ffn2_post, ffn2_w_gu, ffn2_w_down):
    B = x.shape[0]
    meta = jnp.broadcast_to(meta_tokens.astype(x.dtype)[None], (B, N_META, D_MODEL))
    h = jnp.concatenate([meta, x], axis=1)
    L = h.shape[1]
    cos, sin = _rope_tables(L, h.dtype)
    for i in range(DEPTH):
        h = h + 0.5 * _rmsnorm(_swiglu(_rmsnorm(h, norm_ffn1_pre[i]), ffn1_w_gu[i], ffn1_w_down[i]),
                               norm_ffn1_post[i])
        m = _hybrid_mixer(_rmsnorm(h, norm_mix_pre[i]), w_in[i], pool_w[i], pool_scale[i], w_pool_o[i],
                          q_a_norm[i], w_q_b[i], kv_a_norm[i], w_kv_b[i], w_mla_o[i], w_out[i], cos, sin)
        h = h + _rmsnorm(m, norm_mix_post[i])
        h = h + 0.5 * _rmsnorm(_swiglu(_rmsnorm(h, norm_ffn2_pre[i]), ffn2_w_gu[i], ffn2_w_down[i]),
                               norm_ffn2_post[i])
    return h[:, N_META:]
```

```python
import contextlib
import numpy as np
import concourse.bass as bass
import concourse.mybir as mybir
from concourse.bass_utils import run_bass_kernel_spmd

F32 = mybir.dt.float32
BF16 = mybir.dt.bfloat16
AF = mybir.ActivationFunctionType
ALU = mybir.AluOpType

D = 2048
SEQ = 4096
NMETA = 16
DFF = 5632
NFC = DFF // 128
QL = 512
KVL = 512
ROPE = 64
NOPE = 128
VD = 128
NH = 16
QKD = NOPE + ROPE
POOLW = 1024
IN_COLS = 6208
EPS = 1e-6
SCALE = QKD ** -0.5
NT_OWN = 16
NSLOT = 32
LK = NMETA + SEQ
ENGS = ("pe", "act", "dve", "pool", "sp")


class Res:
    __slots__ = ("name", "w_eng", "w_dma", "r_eng", "r_dma")

    def __init__(self, name=""):
        self.name = name
        self.w_eng = {}
        self.w_dma = []
        self.r_eng = {}
        self.r_dma = []


class Prog:
    NRING = 12

    def __init__(self, nc, es):
        self.nc = nc
        self.lists = {e: [] for e in ENGS}
        self.cnt = {e: 0 for e in ENGS}
        self.known = {e: {e2: 0 for e2 in ENGS} for e in ENGS}
        self.kdma = {e: {} for e in ENGS}
        self.snaps = {e: [None] for e in ENGS}
        self.sem = {e: es.enter_context(nc.semaphore("c_" + e)) for e in ENGS}
        self.ring = {}
        self.ring_val = {}
        self.ring_pos = {}
        for q in ("sp", "pool"):
            self.ring[q] = [es.enter_context(nc.semaphore(f"d_{q}{i}")) for i in range(self.NRING)]
            self.ring_val[q] = [0] * self.NRING
            self.ring_pos[q] = 0

    def _deps(self, eng, reads, writes):
        waits_e = {}
        waits_d = {}

        def need_e(e2, idx, raw):
            if e2 == eng and eng == "pe":
                return
            if self.known[eng][e2] >= idx:
                return
            if waits_e.get(e2, 0) < idx:
                waits_e[e2] = idx

        def need_d(tok):
            key, val = tok
            if self.kdma[eng].get(key, 0) >= val:
                return
            if waits_d.get(key, 0) < val:
                waits_d[key] = val

        for r in reads:
            for e2, idx in r.w_eng.items():
                need_e(e2, idx, True)
            for t in r.w_dma:
                need_d(t)
        for w in writes:
            for e2, idx in w.w_eng.items():
                need_e(e2, idx, False)
            for t in w.w_dma:
                need_d(t)
            for e2, idx in w.r_eng.items():
                need_e(e2, idx, False)
            for t in w.r_dma:
                need_d(t)
        out = []
        for e2, idx in waits_e.items():
            out.append((self.sem[e2], idx))
            kn = self.known[eng]
            if kn[e2] < idx:
                kn[e2] = idx
            sn = self.snaps[e2][idx]
            if sn is not None:
                for e3, v in zip(ENGS, sn):
                    if kn[e3] < v:
                        kn[e3] = v
        for key, val in waits_d.items():
            out.append((key, val))
            self.kdma[eng][key] = val
        return out

    def op(self, eng, fn, reads=(), writes=()):
        waits = self._deps(eng, reads, writes)
        self.cnt[eng] += 1
        idx = self.cnt[eng]
        self.snaps[eng].append(tuple(self.known[eng][e] for e in ENGS))
        self.lists[eng].append((waits, fn, self.sem[eng], 1))
        for w in writes:
            w.w_eng = {eng: idx}
            w.w_dma = []
            w.r_eng = {}
            w.r_dma = []
        for r in reads:
            if r.r_eng.get(eng, 0) < idx:
                r.r_eng[eng] = idx
        return idx

    def dma(self, q, out, in_, reads=(), writes=(), append=False):
        waits = [] if append else self._deps(q, reads, writes)
        pos = self.ring_pos[q]
        self.ring_pos[q] = (pos + 1) % self.NRING
        sem = self.ring[q][pos]
        prev = self.ring_val[q][pos]
        if prev and self.kdma[q].get(sem, 0) < prev:
            waits.append((sem, prev))
            self.kdma[q][sem] = prev
        val = prev + 16
        self.ring_val[q][pos] = val
        tok = (sem, val)
        self.lists[q].append((waits, lambda e: e.dma_start(out=out, in_=in_), sem, 16))
        for w in writes:
            if append:
                w.w_dma.append(tok)
                continue
            w.w_eng = {}
            w.w_dma = [tok]
            w.r_eng = {}
            w.r_dma = []
        for r in reads:
            r.r_dma.append(tok)
        return tok

    def barrier(self):
        tgt = dict(self.cnt)
        dm = []
        for q in self.ring:
            for s, v in zip(self.ring[q], self.ring_val[q]):
                if v:
                    dm.append((s, v))
        for e in ENGS:
            waits = []
            for e2 in ENGS:
                if e2 != e and self.known[e][e2] < tgt[e2]:
                    waits.append((self.sem[e2], tgt[e2]))
                    self.known[e][e2] = tgt[e2]
            for s, v in dm:
                if self.kdma[e].get(s, 0) < v:
                    waits.append((s, v))
                    self.kdma[e][s] = v
            if waits:
                self.lists[e].append((waits, None, None, 0))

    def emit(self, block):
        def mk(e):
            lst = self.lists[e]

            def body(eng):
                for waits, fn, sem, inc in lst:
                    for s, v in waits:
                        eng.wait_ge(s, v)
                    if fn is not None:
                        ins = fn(eng)
                        ins.then_inc(sem, inc)
            return body
        block.tensor(mk("pe"))
        block.scalar(mk("act"))
        block.vector(mk("dve"))
        block.gpsimd(mk("pool"))
        block.sync(mk("sp"))


def build_program(debug=None):
    nc = bass.Bass("TRN2", target_bir_lowering=False)
    es = contextlib.ExitStack()

    def din(name, shape, dt=F32):
        return nc.dram_tensor(name, list(shape), dt, kind="ExternalInput").ap()

    xs = din("xs", [SEQ, D])
    meta = din("meta", [NMETA, D])
    gains = din("gains", [6, D])
    gainsT = din("gainsT", [128, 6 * 16])
    w_gu = [din("ffn1_w_gu", [D, 2 * DFF]), din("ffn2_w_gu", [D, 2 * DFF])]
    w_dn = [din("ffn1_w_down", [DFF, D]), din("ffn2_w_down", [DFF, D])]
    w_in = din("w_in", [D, IN_COLS])
    pool_w = din("pool_w", [4, 256, 256])
    pool_scale = din("pool_scale", [POOLW])
    w_pool_o = din("w_pool_o", [POOLW, D])
    q_a_norm = din("q_a_norm", [QL])
    w_q_b = din("w_q_b", [QL, NH * QKD])
    kv_a_norm = din("kv_a_norm", [KVL])
    w_kv_b = din("w_kv_b", [KVL, NH * (NOPE + VD)])
    w_mla_o = din("w_mla_o", [NH * VD, D])
    w_out = din("w_out", [D, D])
    cosT = din("cosT", [ROPE, LK])
    sinT = din("sinT", [ROPE, LK])
    ident_d = din("ident", [128, 128])
    tri_d = din("tri", [128, 128])
    sel_d = din("sel", [128, 4])
    latgT = din("latgT", [128, 8])
    pscT = din("pscT", [128, 8])
    out_d = nc.dram_tensor("out", [NT_OWN * 128, D], F32, kind="ExternalOutput").ap()

    h1_d = nc.dram_tensor("h1_d", [SEQ + NMETA, D], F32).ap()
    kvnT_d = nc.dram_tensor("kvnT_d", [128, 4, LK], BF16).ap()
    kpeT_d = nc.dram_tensor("kpeT_d", [ROPE, LK], BF16).ap()
    cqnT_d = nc.dram_tensor("cqnT_d", [128, 4, NT_OWN * 128], BF16).ap()
    oT_d = nc.dram_tensor("oT_d", [128, NH, NT_OWN * 128], BF16).ap()
    h2_d = nc.dram_tensor("h2_d", [NT_OWN * 128, D], F32).ap()
    dbg = None
    if debug is not None:
        dbg = nc.dram_tensor("dbg", list(debug[:2]), F32, kind="ExternalOutput").ap()

    P = Prog(nc, es)
    stop_after = debug[2] if (debug is not None and len(debug) > 2) else "all"
    only = debug[3] if (debug is not None and len(debug) > 3) else None
    import os
    phases = set((os.environ.get("K_PHASES") or "a1,a2,b,c1,c2").split(","))
    psum = [es.enter_context(nc.psum_tensor(f"ps{i}", [128, 512], F32)) for i in range(8)]
    psr = [Res(f"ps{i}") for i in range(8)]

    ident = es.enter_context(nc.sbuf_tensor("identb", [128, 128], BF16))
    gT = es.enter_context(nc.sbuf_tensor("gT", [128, 6, 16], F32))
    r_const = Res("const")
    epsb = es.enter_context(nc.sbuf_tensor("epsb", [128, 1], F32))
    P.op("dve", lambda e: e.memset(epsb[:], EPS), writes=[r_const])
    P.dma("pool", ident[:], ident_d[:, :], writes=[r_const])
    P.dma("sp", gT[:].rearrange("p g k -> p (g k)"), gainsT[:, :], writes=[r_const])

    def wview(w, r0, nr, c0, ncol):
        return w[r0:r0 + nr, c0:c0 + ncol].rearrange("(kc p) f -> p kc f", p=128)

    def norm_transpose(src, r_src, nt, xnT, r_xnT, col0, gsrc, sq_junk, xs_bf, stat, r_tmp, nkc=16):
        width = nkc * 128
        P.op("act", lambda e: e.activation(out=sq_junk[:nt, :width], in_=src, func=AF.Square,
                                           accum_out=stat[:nt, 0:1]),
             reads=list(r_src), writes=[r_tmp])
        P.op("act", lambda e: e.activation(out=stat[:nt, 1:2], in_=stat[:nt, 0:1], func=AF.Sqrt,
                                           scale=1.0 / width, bias=epsb[:nt, 0:1]),
             reads=[r_tmp, r_const], writes=[r_tmp])
        P.op("dve", lambda e: e.reciprocal(out=stat[:nt, 2:3], in_=stat[:nt, 1:2]),
             reads=[r_tmp], writes=[r_tmp])
        P.op("act", lambda e: e.activation(out=xs_bf[:nt, :width], in_=src, func=AF.Copy,
                                           scale=stat[:nt, 2:3]),
             reads=list(r_src) + [r_tmp], writes=[r_tmp])
        for k0 in range(0, nkc, 8):
            kn = min(8, nkc - k0)
            pb = tr_banks[tr_state[0] % len(tr_banks)]
            tr_state[0] += 1
            pv = psum[pb][:].bitcast(BF16)

            def tr(e, k0=k0, kn=kn, pv=pv):
                ins = None
                for k in range(kn):
                    ins = e.transpose(pv[:, k * 128:k * 128 + nt], xs_bf[:nt, (k0 + k) * 128:(k0 + k + 1) * 128],
                                      ident[:nt, :nt])
                return ins
            P.op("pe", tr, reads=[r_tmp, r_const], writes=[psr[pb]])
            P.op("dve", lambda e, k0=k0, kn=kn, pv=pv: e.tensor_tensor(
                out=xnT[:, k0:k0 + kn, col0:col0 + nt],
                in0=pv[:, :kn * 128].rearrange("p (k t) -> p k t", k=kn)[:, :, :nt],
                in1=gsrc[:, k0:k0 + kn].unsqueeze(2).to_broadcast([128, kn, nt]),
                op=ALU.mult),
                reads=[psr[pb], r_const], writes=[r_xnT])

    tr_banks = [6, 7]
    tr_state = [0]

    GROUPS = [(0, 12), (12, 12), (24, 12), (36, 8)]

    def alloc_ffn(ph, ntile, pfx, T=528, nxst=1):
        def sb_t(name, shape, dt):
            return ph.enter_context(nc.sbuf_tensor(pfx + name, shape, dt))
        sb = {}
        sb["xnT"] = sb_t("xnT", [128, 16, T], BF16)
        sb["r_xnT"] = Res("xnT")
        sb["actT"] = [sb_t(f"actT{i}", [128, 12, T], BF16) for i in range(2)]
        sb["r_act"] = [Res("act0"), Res("act1")]
        sb["ysb"] = [sb_t(f"ysb{i}", [128, D], F32) for i in range(ntile)]
        sb["r_ysb"] = [[Res(f"y{i}q{q}") for q in range(4)] for i in range(ntile)]
        sb["wg"] = [sb_t(f"wg{i}", [128, 16, 256], BF16) for i in range(2)]
        sb["wu"] = [sb_t(f"wu{i}", [128, 16, 256], BF16) for i in range(2)]
        sb["r_wgu"] = [Res(f"wgu{i}") for i in range(2)]
        sb["wd"] = [sb_t(f"wd{i}", [128, 12, 512], BF16) for i in range(2)]
        sb["r_wd"] = [Res("wd0"), Res("wd1")]
        sb["sg"] = [sb_t(f"sg{i}", [128, 512], F32) for i in range(2)]
        sb["r_sg"] = [Res("sg0"), Res("sg1")]
        sb["junk"] = [sb_t(f"junk{i}", [128, D], BF16) for i in range(2)]
        sb["xs_bf"] = [sb_t(f"xs_bf{i}", [128, D], BF16) for i in range(2)]
        sb["stat"] = [sb_t(f"stat{i}", [128, 4], F32) for i in range(2)]
        sb["r_tmp"] = [Res("tmp0"), Res("tmp1")]
        sb["grow"] = sb_t("grow", [128, D], F32)
        sb["r_grow"] = Res("grow")
        sb["xst"] = [sb_t(f"xst{i}", [128, D], F32) for i in range(nxst)]
        sb["r_xst"] = [Res(f"xst{i}") for i in range(nxst)]
        sb["cnt"] = dict(gu=0, bank=0, sg=0, wd=0, dn=0, xst=0)
        sb["gu_banks"] = [(0, 1), (2, 3)]
        sb["dn_banks"] = [4, 5, 6, 7]
        return sb

    def load_grow(sb, g_post):
        P.dma("sp", sb["grow"][:, :], gains[g_post:g_post + 1, :].partition_broadcast(128), writes=[sb["r_grow"]])
        P.op("dve", lambda e: e.tensor_scalar(out=sb["grow"][:, :], in0=sb["grow"][:, :], scalar1=0.5, scalar2=None,
                                              op0=ALU.mult), reads=[sb["r_grow"]], writes=[sb["r_grow"]])

    def ffn_pass(which, tiles, sb):
        wgu, wdn = w_gu[which], w_dn[which]
        g_pre = 0 if which == 0 else 4
        xnT, actT, ysb = sb["xnT"], sb["actT"], sb["ysb"]
        r_xnT = sb["r_xnT"]
        cnt = sb["cnt"]
        T = sum(t["nt"] for t in tiles)
        cols = []
        c = 0
        for t in tiles:
            cols.append(c)
            c += t["nt"]
        for ti, (t, c0) in enumerate(zip(tiles, cols)):
            nt = t["nt"]
            if t.get("x_ap") is None:
                t["load"](ysb[ti][:nt, :], sb["r_ysb"][ti])
                src, rs = ysb[ti][:nt, :], sb["r_ysb"][ti]
            else:
                src, rs = t["x_ap"], [t["r_x"]]
            norm_transpose(src, rs, nt, xnT, r_xnT, c0, gT[:, g_pre, :], sb["junk"][ti % 2], sb["xs_bf"][ti % 2],
                           sb["stat"][ti % 2], sb["r_tmp"][ti % 2])
        nb_ = -(-T // 512)
        bsz = -(-T // (nb_ * 16)) * 16
        nblocks = [(b0, min(bsz, T - b0)) for b0 in range(0, T, bsz)]
        for gi, (f0, gn) in enumerate(GROUPS):
            ab = gi % 2
            r_act = sb["r_act"][ab]
            for fp in range(gn // 2):
                st = cnt["gu"] % 2
                cnt["gu"] += 1
                wg_t, wu_t, r_w = sb["wg"][st], sb["wu"][st], sb["r_wgu"][st]
                fcol = (f0 + 2 * fp) * 128
                P.dma("pool", wg_t[:], wview(wgu, 0, D, fcol, 256), writes=[r_w])
                P.dma("pool", wu_t[:], wview(wgu, 0, D, DFF + fcol, 256), writes=[r_w], append=True)
                for sub in range(2):
                    fi = 2 * fp + sub
                    for (b0, bn) in nblocks:
                        bg, bu = sb["gu_banks"][cnt["bank"] % 2]
                        cnt["bank"] += 1

                        def mm(e, wt, bank, b0=b0, bn=bn, sub=sub):
                            ins = None
                            for kc in range(16):
                                ins = e.matmul(psum[bank][:, :bn], wt[:, kc, sub * 128:(sub + 1) * 128],
                                               xnT[:, kc, b0:b0 + bn], start=(kc == 0), stop=(kc == 15))
                            return ins
                        P.op("pe", lambda e, wt=wg_t, bank=bg, mm=mm: mm(e, wt, bank), reads=[r_w, r_xnT],
                             writes=[psr[bg]])
                        P.op("pe", lambda e, wt=wu_t, bank=bu, mm=mm: mm(e, wt, bank), reads=[r_w, r_xnT],
                             writes=[psr[bu]])
                        sgi = cnt["sg"] % 2
                        cnt["sg"] += 1
                        sg, r_sg = sb["sg"][sgi], sb["r_sg"][sgi]
                        P.op("act", lambda e, bg=bg, bn=bn, sg=sg: e.activation(
                            out=sg[:, :bn], in_=psum[bg][:, :bn], func=AF.Silu),
                            reads=[psr[bg]], writes=[r_sg])
                        P.op("dve", lambda e, bu=bu, b0=b0, bn=bn, sg=sg, fi=fi, ab=ab: e.tensor_tensor(
                            out=actT[ab][:, fi, b0:b0 + bn], in0=psum[bu][:, :bn], in1=sg[:, :bn], op=ALU.mult),
                            reads=[psr[bu], r_sg], writes=[r_act])
            for q in range(4):
                st = cnt["wd"] % 2
                cnt["wd"] += 1
                wd_t, r_wd = sb["wd"][st], sb["r_wd"][st]
                P.dma("pool", wd_t[:, :gn, :], wview(wdn, f0 * 128, gn * 128, q * 512, 512), writes=[r_wd])
                for ti, (t, c0) in enumerate(zip(tiles, cols)):
                    nt = t["nt"]
                    pd = sb["dn_banks"][cnt["dn"] % 4]
                    cnt["dn"] += 1

                    def mmd(e, pd=pd, c0=c0, nt=nt, wd_t=wd_t, ab=ab, gn=gn):
                        ins = None
                        for fi in range(gn):
                            ins = e.matmul(psum[pd][:nt, :], actT[ab][:, fi, c0:c0 + nt], wd_t[:, fi, :],
                                           start=(fi == 0), stop=(fi == gn - 1))
                        return ins
                    P.op("pe", mmd, reads=[r_act, r_wd], writes=[psr[pd]])
                    r_y = sb["r_ysb"][ti][q]
                    dst = ysb[ti][:nt, q * 512:(q + 1) * 512]
                    if gi == 0:
                        P.op("act", lambda e, dst=dst, pd=pd, nt=nt: e.activation(out=dst, in_=psum[pd][:nt, :],
                                                                                 func=AF.Copy),
                             reads=[psr[pd]], writes=[r_y])
                    else:
                        P.op("dve", lambda e, dst=dst, pd=pd, nt=nt: e.tensor_tensor(
                            out=dst, in0=psum[pd][:nt, :], in1=dst, op=ALU.add),
                            reads=[psr[pd], r_y], writes=[r_y])
        for ti, t in enumerate(tiles):
            nt = t["nt"]
            stat = sb["stat"][ti % 2]
            r_t2 = sb["r_tmp"][ti % 2]
            junk_t = sb["junk"][ti % 2]
            ry = sb["r_ysb"][ti]
            if t.get("x_ap") is None:
                k = cnt["xst"] % len(sb["xst"])
                cnt["xst"] += 1
                t["load"](sb["xst"][k][:nt, :], [sb["r_xst"][k]])
                xap, rx = sb["xst"][k][:nt, :], sb["r_xst"][k]
            else:
                xap, rx = t["x_ap"], t["r_x"]
            P.op("act", lambda e, ti=ti, nt=nt, stat=stat, junk_t=junk_t: e.activation(
                out=junk_t[:nt, :], in_=ysb[ti][:nt, :], func=AF.Square, accum_out=stat[:nt, 0:1]),
                 reads=ry, writes=[r_t2])
            P.op("act", lambda e, nt=nt, stat=stat: e.activation(out=stat[:nt, 1:2], in_=stat[:nt, 0:1], func=AF.Sqrt,
                                                                 scale=1.0 / D, bias=epsb[:nt, 0:1]),
                 reads=[r_t2, r_const], writes=[r_t2])
            P.op("dve", lambda e, nt=nt, stat=stat: e.reciprocal(out=stat[:nt, 2:3], in_=stat[:nt, 1:2]),
                 reads=[r_t2], writes=[r_t2])
            P.op("dve", lambda e, ti=ti, nt=nt, stat=stat: e.scalar_tensor_tensor(
                out=ysb[ti][:nt, :], in0=ysb[ti][:nt, :], scalar=stat[:nt, 2:3], in1=sb["grow"][:nt, :],
                op0=ALU.mult, op1=ALU.mult),
                reads=[r_t2, sb["r_grow"]] + ry, writes=ry)
            P.op("dve", lambda e, ti=ti, nt=nt, xap=xap: e.tensor_tensor(out=ysb[ti][:nt, :], in0=ysb[ti][:nt, :],
                                                                         in1=xap, op=ALU.add),
                 reads=ry + [rx], writes=ry)
            t["done"](ti, ysb[ti][:nt, :], ry)

    r_h1d = [Res(f"h1d{i}") for i in range(NSLOT + 1)]
    with contextlib.ExitStack() as ph:
        sb = alloc_ffn(ph, 6, 'a1_', T=768)
        load_grow(sb, 1)
        pass_slots = [list(range(0, 6)), list(range(6, 12)), list(range(12, 17)), list(range(17, 22)),
                      list(range(22, 27)), list(range(27, 32)) + [NSLOT]]
        if debug is not None and stop_after in ('a1', 'a2'):
            pass_slots = [[0, 1, 2, 3, NSLOT]]
        if only is not None or "a1" not in phases:
            pass_slots = []
        for slots_ in pass_slots:
            tiles = []
            for s in slots_:
                if s == NSLOT:
                    tiles.append(dict(nt=NMETA, slot=NSLOT, load=(lambda dst, rd: P.dma(
                        "sp", dst, meta[:, :], writes=rd))))
                else:
                    tiles.append(dict(nt=128, slot=s, load=(lambda dst, rd, s=s: P.dma(
                        "sp", dst, xs[s * 128:(s + 1) * 128, :], writes=rd))))

            def done(ti, y_ap, ry, tiles=tiles):
                s = tiles[ti]["slot"]
                nt = tiles[ti]["nt"]
                P.dma("sp", h1_d[s * 128:s * 128 + nt, :], y_ap, reads=ry, writes=[r_h1d[s]])
                if dbg is not None and stop_after == "a1":
                    P.dma("sp", dbg[ti * 128:ti * 128 + nt, :], y_ap, reads=ry, writes=[Res()])
            for t in tiles:
                t["done"] = done
            ffn_pass(0, tiles, sb)
        P.barrier()


    def dump(ap_sb, reads, r0, nrow, c0, ncol):
        P.dma("sp", dbg[r0:r0 + nrow, c0:c0 + ncol], ap_sb, reads=reads, writes=[Res()])

    r_kvn_d = [Res(f"kvnd{i}") for i in range(NSLOT + 1)]
    r_kpe_d = [Res(f"kped{i}") for i in range(NSLOT + 1)]
    r_cqn_d = [Res(f"cqnd{i}") for i in range(NT_OWN)]
    if stop_after not in ("a1",) and only is None and "a2" in phases:
      with contextlib.ExitStack() as ph:
        def sb_t(name, shape, dt):
            return ph.enter_context(nc.sbuf_tensor('a2_' + name, shape, dt))
        w_lat = sb_t("w_lat", [128, 16, 1152], BF16)
        r_wlat = Res("wlat")
        P.dma("pool", w_lat[:, :, 0:544], wview(w_in, 0, D, 1024, 544), writes=[r_wlat])
        P.dma("pool", w_lat[:, :, 544:1088], wview(w_in, 0, D, 1568, 544), writes=[r_wlat], append=True)
        P.op("dve", lambda e: e.tensor_scalar(out=w_lat[:, :, 1088:1120], in0=w_lat[:, :, 1056:1088], scalar1=-1.0,
                                              scalar2=None, op0=ALU.mult), reads=[r_wlat], writes=[r_wlat])
        P.op("dve", lambda e: e.tensor_copy(out=w_lat[:, :, 1120:1152], in_=w_lat[:, :, 1024:1056]),
             reads=[r_wlat], writes=[r_wlat])
        cos_sb = sb_t("cos_sb", [ROPE, LK], F32)
        sin_sb = sb_t("sin_sb", [ROPE, LK], F32)
        latg = sb_t("latg", [128, 8], F32)
        r_tab = Res("tab")
        P.dma("sp", cos_sb[:, :], cosT[:, :], writes=[r_tab])
        P.dma("sp", sin_sb[:, :], sinT[:, :], writes=[r_tab], append=True)
        P.dma("sp", latg[:, :], latgT[:, :], writes=[r_tab], append=True)
        hbuf = [sb_t(f"hbuf{i}", [128, D], F32) for i in range(2)]
        r_hbuf = [Res("hb0"), Res("hb1")]
        xn2 = [sb_t(f"xn2_{i}", [128, 16, 128], BF16) for i in range(2)]
        r_xn2 = [Res("xn2_0"), Res("xn2_1")]
        junk_l = [sb_t(f"junk{i}", [128, D], BF16) for i in range(2)]
        xs_bf_l = [sb_t(f"xs_bf{i}", [128, D], BF16) for i in range(2)]
        stat_l = [sb_t(f"stat{i}", [128, 4], F32) for i in range(2)]
        r_tmp_l = [Res("tmp0"), Res("tmp1")]
        junk2_l = [sb_t(f"junk2_{i}", [128, 512], BF16) for i in range(4)]
        xs_bf2_l = [sb_t(f"xs_bf2_{i}", [128, 512], BF16) for i in range(4)]
        stat2_l = [sb_t(f"stat2_{i}", [128, 4], F32) for i in range(4)]
        r_tmp2_l = [Res(f"tmp2_{i}") for i in range(4)]
        t1_l = [sb_t(f"t1_{i}", [ROPE, 128], F32) for i in range(2)]
        t2_l = [sb_t(f"t2_{i}", [ROPE, 128], F32) for i in range(2)]
        r_t12_l = [Res("t12_0"), Res("t12_1")]
        kpe_t = [sb_t(f"kpe_t{i}", [ROPE, 128], BF16) for i in range(2)]
        r_kpe_t = [Res("kpet0"), Res("kpet1")]
        kvn_t = [sb_t(f"kvn_t{i}", [128, 4, 128], BF16) for i in range(2)]
        r_kvn_t = [Res("kvnt0"), Res("kvnt1")]
        cqn_t = [sb_t(f"cqn_t{i}", [128, 4, 128], BF16) for i in range(2)]
        r_cqn_t = [Res("cqnt0"), Res("cqnt1")]
        ntile_a2 = NSLOT + 1 if debug is None or stop_after != "a2" else 5
        for tix in range(ntile_a2):
            pb = tix % 2
            junk, xs_bf, stat, r_tmp = junk_l[pb], xs_bf_l[pb], stat_l[pb], r_tmp_l[pb]
            t1, t2, r_t12 = t1_l[pb], t2_l[pb], r_t12_l[pb]
            if tix == 0:
                nt, row0, kcol, own, s = NMETA, SEQ, 0, False, NSLOT
            else:
                s = tix - 1
                nt, row0, kcol, own = 128, s * 128, NMETA + s * 128, s < NT_OWN
            hb = hbuf[pb]
            P.dma("sp", hb[:nt, :], h1_d[row0:row0 + nt, :], reads=[r_h1d[s]], writes=[r_hbuf[pb]])
            norm_transpose(hb[:nt, :], [r_hbuf[pb]], nt, xn2[pb], r_xn2[pb], 0, gT[:, 2, :], junk, xs_bf, stat, r_tmp)

            def mm_tok(e, bank, c0, nt=nt, pb=pb):
                ins = None
                for kc in range(16):
                    ins = e.matmul(psum[bank][:nt, :], xn2[pb][:, kc, :nt], w_lat[:, kc, c0:c0 + 512],
                                   start=(kc == 0), stop=(kc == 15))
                return ins

            def mm_feat(e, bank, c0, nt=nt, pb=pb):
                ins = None
                for kc in range(16):
                    ins = e.matmul(psum[bank][:ROPE, :nt], w_lat[:, kc, c0:c0 + ROPE], xn2[pb][:, kc, :nt],
                                   start=(kc == 0), stop=(kc == 15))
                return ins
            P.op("pe", lambda e, f=mm_tok: f(e, 1, 512), reads=[r_xn2[pb], r_wlat], writes=[psr[1]])
            P.op("pe", lambda e, f=mm_feat: f(e, 2, 1024), reads=[r_xn2[pb], r_wlat], writes=[psr[2]])
            P.op("pe", lambda e, f=mm_feat: f(e, 3, 1088), reads=[r_xn2[pb], r_wlat], writes=[psr[3]])
            if own:
                P.op("pe", lambda e, f=mm_tok: f(e, 0, 0), reads=[r_xn2[pb], r_wlat], writes=[psr[0]])
            P.op("dve", lambda e, nt=nt, kcol=kcol, t1=t1: e.tensor_tensor(out=t1[:, :nt], in0=psum[2][:ROPE, :nt],
                                                                  in1=cos_sb[:, kcol:kcol + nt], op=ALU.mult),
                 reads=[psr[2], r_tab], writes=[r_t12])
            P.op("dve", lambda e, nt=nt, kcol=kcol, t2=t2: e.tensor_tensor(out=t2[:, :nt], in0=psum[3][:ROPE, :nt],
                                                                  in1=sin_sb[:, kcol:kcol + nt], op=ALU.mult),
                 reads=[psr[3], r_tab], writes=[r_t12])
            P.op("dve", lambda e, nt=nt, pb=pb, t1=t1, t2=t2: e.tensor_tensor(out=kpe_t[pb][:, :nt], in0=t1[:, :nt], in1=t2[:, :nt],
                                                              op=ALU.add),
                 reads=[r_t12], writes=[r_kpe_t[pb]])
            P.dma("sp", kpeT_d[:, kcol:kcol + nt], kpe_t[pb][:, :nt], reads=[r_kpe_t[pb]], writes=[r_kpe_d[tix]])
            norm_transpose(psum[1][:nt, :], [psr[1]], nt, kvn_t[pb], r_kvn_t[pb], 0, latg[:, 4:8], junk2_l[2 * pb],
                           xs_bf2_l[2 * pb], stat2_l[2 * pb], r_tmp2_l[2 * pb], nkc=4)
            P.dma("sp", kvnT_d[:, :, kcol:kcol + nt], kvn_t[pb][:, :, :nt], reads=[r_kvn_t[pb]], writes=[r_kvn_d[tix]])
            if stop_after == "a2":
                for kc in range(4):
                    P.dma("pool", dbg[0:128, kc * 1024 + kcol:kc * 1024 + kcol + nt], kvn_t[pb][:, kc, :nt],
                          reads=[r_kvn_t[pb]], writes=[Res()])
                P.dma("pool", dbg[128:192, kcol:kcol + nt], kpe_t[pb][:, :nt], reads=[r_kpe_t[pb]], writes=[Res()])
                if own:
                    for kc in range(4):
                        P.dma("pool", dbg[256:384, kc * 1024 + s * 128:kc * 1024 + s * 128 + nt], cqn_t[pb][:, kc, :nt],
                              reads=[r_cqn_t[pb]], writes=[Res()])
            if own:
                norm_transpose(psum[0][:nt, :], [psr[0]], nt, cqn_t[pb], r_cqn_t[pb], 0, latg[:, 0:4],
                               junk2_l[2 * pb + 1], xs_bf2_l[2 * pb + 1], stat2_l[2 * pb + 1], r_tmp2_l[2 * pb + 1],
                               nkc=4)
                P.dma("sp", cqnT_d[:, :, s * 128:(s + 1) * 128], cqn_t[pb][:, :, :], reads=[r_cqn_t[pb]],
                      writes=[r_cqn_d[s]])
        P.barrier()

    r_oT_d = [Res(f"oTd{h}") for h in range(NH)]
    if stop_after not in ("a1", "a2", "a2f") and "b" in phases:
      with contextlib.ExitStack() as ph:
        def sb_t(name, shape, dt):
            return ph.enter_context(nc.sbuf_tensor('b_' + name, shape, dt))
        NQ = NT_OWN * 128
        kvnT = sb_t("kvnT", [128, 4, LK], BF16)
        kpeT = sb_t("kpeT", [ROPE, LK], BF16)
        cqnT = sb_t("cqnT", [128, 4, NQ], BF16)
        cosq = sb_t("cosq", [ROPE, NQ], F32)
        sinq = sb_t("sinq", [ROPE, NQ], F32)
        tri = sb_t("tri", [128, 128], BF16)
        ones = sb_t("ones", [128, 128], BF16)
        sel = sb_t("sel", [128, 4], F32)
        r_lat = Res("lat")
        P.dma("sp", kvnT[:, :, :], kvnT_d[:, :, :], reads=r_kvn_d, writes=[r_lat])
        P.dma("sp", kpeT[:, :], kpeT_d[:, :], reads=r_kpe_d, writes=[r_lat], append=True)
        P.dma("sp", cqnT[:, :, :], cqnT_d[:, :, :], reads=r_cqn_d, writes=[r_lat], append=True)
        P.dma("sp", cosq[:, :], cosT[:, NMETA:NMETA + NQ], writes=[r_lat], append=True)
        P.dma("sp", sinq[:, :], sinT[:, NMETA:NMETA + NQ], writes=[r_lat], append=True)
        P.dma("sp", sel[:, :], sel_d[:, :], writes=[r_lat], append=True)
        r_msk = Res("msk")
        P.dma("pool", tri[:, :], tri_d[:, :], writes=[r_msk])
        P.op("dve", lambda e: e.memset(ones[:], 1.0), writes=[r_msk])
        hbufs = []
        for i in range(2):
            hbufs.append(dict(
                wkv=sb_t(f"wkv{i}", [128, 4, 256], BF16), wq=sb_t(f"wq{i}", [128, 4, 256], BF16),
                KT=sb_t(f"KT{i}", [128, LK], BF16), V=sb_t(f"V{i}", [128, NSLOT + 1, 128], BF16),
                qT=sb_t(f"qT{i}", [128, NQ], BF16), qpe=sb_t(f"qpe{i}", [ROPE, NQ], BF16),
                oT=sb_t(f"oT{i}", [128, NQ], BF16),
                r_w=Res(f"hw{i}"), r_KT=Res(f"KT{i}"), r_V=Res(f"V{i}"), r_qT=Res(f"qT{i}"), r_qpe=Res(f"qpe{i}"),
                r_oT=Res(f"oT{i}")))
        pT = [sb_t(f"pT{i}", [128, 512], BF16) for i in range(3)]
        r_pT = [Res(f"pT{i}") for i in range(3)]
        rec = sb_t("rec", [128, 512], F32)
        r_rec = Res("rec")
        rt1 = sb_t("rt1", [ROPE, 512], F32)
        rt2 = sb_t("rt2", [ROPE, 512], F32)
        r_rt = Res("rt")
        pj = [6, 7, 0, 1, 2, 3, 4]
        pjc = [0]

        def pjbank():
            b = pj[pjc[0] % len(pj)]
            pjc[0] += 1
            return b

        def proj(h, hb):
            B = hbufs[hb]
            wkv, wq = B["wkv"], B["wq"]
            P.dma("pool", wkv[:, :, :], wview(w_kv_b, 0, KVL, h * 256, 256), writes=[B["r_w"]])
            P.dma("pool", wq[:, :, 0:QKD], wview(w_q_b, 0, QL, h * QKD, QKD), writes=[B["r_w"]], append=True)
            P.op("dve", lambda e: e.tensor_scalar(out=wq[:, :, 192:224], in0=wq[:, :, 160:192], scalar1=-1.0,
                                                  scalar2=None, op0=ALU.mult), reads=[B["r_w"]], writes=[B["r_w"]])
            P.op("dve", lambda e: e.tensor_copy(out=wq[:, :, 224:256], in_=wq[:, :, 128:160]),
                 reads=[B["r_w"]], writes=[B["r_w"]])
            for b0 in range(0, LK, 512):
                bn = min(512, LK - b0)
                bk = pjbank()

                def mmk(e, bk=bk, b0=b0, bn=bn):
                    ins = None
                    for kc in range(4):
                        ins = e.matmul(psum[bk][:, :bn], wkv[:, kc, 0:128], kvnT[:, kc, b0:b0 + bn],
                                       start=(kc == 0), stop=(kc == 3))
                    return ins
                P.op("pe", mmk, reads=[B["r_w"], r_lat], writes=[psr[bk]])
                P.op("act", lambda e, bk=bk, b0=b0, bn=bn: e.activation(out=B["KT"][:, b0:b0 + bn],
                                                                       in_=psum[bk][:, :bn], func=AF.Copy),
                     reads=[psr[bk]], writes=[B["r_KT"]])
            for g0 in range(0, NSLOT + 1, 4):
                gn = min(4, NSLOT + 1 - g0)
                bk = pjbank()

                def mmv(e, bk=bk, g0=g0, gn=gn):
                    ins = None
                    for j in range(gn):
                        kt = g0 + j
                        nk = NMETA if kt == 0 else 128
                        kc0 = 0 if kt == 0 else NMETA + (kt - 1) * 128
                        for kc in range(4):
                            ins = e.matmul(psum[bk][:nk, j * 128:(j + 1) * 128], kvnT[:, kc, kc0:kc0 + nk],
                                           wkv[:, kc, 128:256], start=(kc == 0), stop=(kc == 3))
                    return ins
                P.op("pe", mmv, reads=[B["r_w"], r_lat], writes=[psr[bk]])
                if g0 == 0:
                    P.op("dve", lambda e, bk=bk: e.tensor_copy(out=B["V"][:NMETA, 0, :], in_=psum[bk][:NMETA, 0:128]),
                         reads=[psr[bk]], writes=[B["r_V"]])
                    P.op("dve", lambda e, bk=bk, gn=gn: e.tensor_copy(
                        out=B["V"][:, 1:gn, :], in_=psum[bk][:, 128:gn * 128].rearrange("p (k c) -> p k c", c=128)),
                        reads=[psr[bk]], writes=[B["r_V"]])
                else:
                    P.op("dve", lambda e, bk=bk, g0=g0, gn=gn: e.tensor_copy(
                        out=B["V"][:, g0:g0 + gn, :], in_=psum[bk][:, :gn * 128].rearrange("p (k c) -> p k c", c=128)),
                        reads=[psr[bk]], writes=[B["r_V"]])
            for qb in range(NQ // 512):
                bk = pjbank()

                def mmq(e, bk=bk, qb=qb):
                    ins = None
                    for kc in range(4):
                        ins = e.matmul(psum[bk][:, :], wq[:, kc, 0:128], cqnT[:, kc, qb * 512:(qb + 1) * 512],
                                       start=(kc == 0), stop=(kc == 3))
                    return ins
                P.op("pe", mmq, reads=[B["r_w"], r_lat], writes=[psr[bk]])
                P.op("act", lambda e, bk=bk, qb=qb: e.activation(out=B["qT"][:, qb * 512:(qb + 1) * 512],
                                                               in_=psum[bk][:, :], func=AF.Copy),
                     reads=[psr[bk]], writes=[B["r_qT"]])
                ba, bb = pjbank(), pjbank()

                def mmp(e, bank, c0, qb=qb):
                    ins = None
                    for kc in range(4):
                        ins = e.matmul(psum[bank][:ROPE, :], wq[:, kc, c0:c0 + ROPE], cqnT[:, kc, qb * 512:(qb + 1) * 512],
                                       start=(kc == 0), stop=(kc == 3))
                    return ins
                P.op("pe", lambda e, f=mmp, ba=ba: f(e, ba, 128), reads=[B["r_w"], r_lat], writes=[psr[ba]])
                P.op("pe", lambda e, f=mmp, bb=bb: f(e, bb, 192), reads=[B["r_w"], r_lat], writes=[psr[bb]])
                P.op("dve", lambda e, ba=ba, qb=qb: e.tensor_tensor(out=rt1[:, :], in0=psum[ba][:ROPE, :],
                                                                  in1=cosq[:, qb * 512:(qb + 1) * 512], op=ALU.mult),
                     reads=[psr[ba], r_lat], writes=[r_rt])
                P.op("dve", lambda e, bb=bb, qb=qb: e.tensor_tensor(out=rt2[:, :], in0=psum[bb][:ROPE, :],
                                                                  in1=sinq[:, qb * 512:(qb + 1) * 512], op=ALU.mult),
                     reads=[psr[bb], r_lat], writes=[r_rt])
                P.op("dve", lambda e, qb=qb: e.tensor_tensor(out=B["qpe"][:, qb * 512:(qb + 1) * 512], in0=rt1[:, :],
                                                            in1=rt2[:, :], op=ALU.add),
                     reads=[r_rt], writes=[B["r_qpe"]])

        sbank = [0]
        qcount = [0]

        def attn(h, hb):
            B = hbufs[hb]
            for qb in range(NQ // 512):
                tl = [(0, NMETA, 0, 0, False, False)]
                for ip in range(4 * qb + 4):
                    q0 = max(0, ip - 4 * qb) * 128
                    tl.append((1 + ip, 128, NMETA + ip * 128, q0, ip >= 4 * qb, False))
                    tl.append((1 + NT_OWN + ip, 128, NMETA + (NT_OWN + ip) * 128, q0, False, ip == 0))
                tl = [tl[0]] + [x for x in tl[1:] if x[3] > 0] + [x for x in tl[1:] if x[3] == 0]
                ob = 3 + (qcount[0] % 2)
                db = 5 + (qcount[0] % 2)
                qcount[0] += 1
                n = len(tl)
                sb_of = {}

                def qk(j):
                    kt, nk, kc0, q0, diag, b16 = tl[j]
                    s = sbank[0] % 3
                    sbank[0] += 1
                    sb_of[j] = s

                    def f(e, s=s, nk=nk, kc0=kc0, q0=q0, qb=qb):
                        e.matmul(psum[s][:nk, q0:512], B["KT"][:, kc0:kc0 + nk],
                                 B["qT"][:, qb * 512 + q0:(qb + 1) * 512], start=True, stop=False)
                        return e.matmul(psum[s][:nk, q0:512], kpeT[:, kc0:kc0 + nk],
                                        B["qpe"][:, qb * 512 + q0:(qb + 1) * 512], start=False, stop=True)
                    P.op("pe", f, reads=[B["r_KT"], B["r_qT"], B["r_qpe"], r_lat], writes=[psr[s]])
                    bias = sel[:nk, 2:3] if b16 else 0.0
                    P.op("act", lambda e, s=s, nk=nk, q0=q0, bias=bias: e.activation(
                        out=pT[s][:nk, q0:512], in_=psum[s][:nk, q0:512], func=AF.Exp, bias=bias, scale=SCALE),
                        reads=[psr[s], r_lat], writes=[r_pT[s]])
                    if diag:
                        P.op("dve", lambda e, s=s, q0=q0: e.tensor_tensor(out=pT[s][:, q0:q0 + 128],
                                                                        in0=pT[s][:, q0:q0 + 128], in1=tri[:, :],
                                                                        op=ALU.mult),
                             reads=[r_pT[s], r_msk], writes=[r_pT[s]])

                def pv(j):
                    kt, nk, kc0, q0, diag, b16 = tl[j]
                    s = sb_of[j]

                    def f(e, s=s, kt=kt, nk=nk, q0=q0, j=j, ob=ob, n=n, db=db):
                        e.matmul(psum[ob][:, q0:512], B["V"][:nk, kt, :], pT[s][:nk, q0:512],
                                 start=(j == 0), stop=(j == n - 1))
                        return e.matmul(psum[db][:, q0:512], ones[:nk, :], pT[s][:nk, q0:512],
                                        start=(j == 0), stop=(j == n - 1))
                    P.op("pe", f, reads=[B["r_V"], r_pT[s], r_msk], writes=[psr[ob], psr[db]])
                qk(0)
                qk(1)
                for j in range(n):
                    if j + 2 < n:
                        qk(j + 2)
                    pv(j)
                P.op("dve", lambda e, db=db: e.reciprocal(out=rec[:, :], in_=psum[db][:, :]), reads=[psr[db]],
                     writes=[r_rec])
                P.op("dve", lambda e, ob=ob, qb=qb: e.tensor_tensor(out=B["oT"][:, qb * 512:(qb + 1) * 512],
                                                                  in0=psum[ob][:, :], in1=rec[:, :], op=ALU.mult),
                     reads=[psr[ob], r_rec], writes=[B["r_oT"]])
            P.dma("sp", oT_d[:, h, :], B["oT"][:, :], reads=[B["r_oT"]], writes=[r_oT_d[h]])
            if stop_after == "b":
                P.dma("pool", dbg[h * 128:(h + 1) * 128, :], B["oT"][:, :], reads=[B["r_oT"]], writes=[Res()])

        nheads = NH if stop_after != "b" else (2 if only is None else 1)
        if debug is not None and len(debug) > 4:
            nheads = debug[4]
        import os
        if not os.environ.get("K_PIPE"):
            for h in range(nheads):
                proj(h, h % 2)
                attn(h, h % 2)
        else:
            proj(0, 0)
            for h in range(nheads):
                if h + 1 < nheads:
                    proj(h + 1, (h + 1) % 2)
                attn(h, h % 2)
        P.barrier()

    r_h2d = [Res(f"h2d{i}") for i in range(NT_OWN)]
    if stop_after not in ("a1", "a2", "a2f", "b") and "c1" in phases:
      with contextlib.ExitStack() as ph:
        def sb_t(name, shape, dt):
            return ph.enter_context(nc.sbuf_tensor('c1_' + name, shape, dt))
        xn2m = sb_t("xn2m", [128, 16, 512], BF16)
        r_xn2m = Res("xn2m")
        xn2h = sb_t("xn2h", [128, 16, 64], BF16)
        r_xn2h = Res("xn2h")
        hst = sb_t("hst", [128, D], F32)
        r_hst = Res("hst")
        halo = sb_t("halo", [NMETA, D], F32)
        r_halo = Res("halo")
        junk = sb_t("junk", [128, D], BF16)
        xs_bf = sb_t("xs_bf", [128, D], BF16)
        stat = sb_t("stat", [128, 4], F32)
        r_tmp = Res("tmp")
        stat2 = sb_t("stat2", [128, 4], F32)
        r_tmp2 = Res("tmp2")
        uext = sb_t("uext", [128, 2, 576], F32)
        r_uext = [Res("uext0"), Res("uext1")]
        ptmp = [sb_t(f"ptmp{i}", [128, 576], F32) for i in range(2)]
        r_ptmp = Res("ptmp")
        P.op("dve", lambda e: e.memset(ptmp[0][:, :], 0.0), writes=[r_ptmp])
        P.op("dve", lambda e: e.memset(ptmp[1][:, :], 0.0), writes=[r_ptmp])
        dext = sb_t("dext", [128, 2, 576], BF16)
        r_dext = Res("dext")
        ypT = sb_t("ypT", [128, 8, 512], BF16)
        r_ypT = Res("ypT")
        oTs = sb_t("oTs", [128, NH, 512], BF16)
        r_oTs = Res("oTs")
        yT = sb_t("yT", [128, 16, 512], BF16)
        r_yT = Res("yT")
        stg = [sb_t(f"stg{i}", [128, 14336], BF16) for i in range(2)]
        r_stg = [Res("stg0"), Res("stg1")]
        wst = []
        for i in range(2):
            wst.append(dict(gp=stg[i][:, 0:4096].rearrange("p (k c) -> p k c", c=256),
                            mo=stg[i][:, 4096:8192].rearrange("p (k c) -> p k c", c=256),
                            gm=stg[i][:, 8192:12288].rearrange("p (k c) -> p k c", c=256),
                            po=stg[i][:, 12288:14336].rearrange("p (k c) -> p k c", c=256),
                            r=r_stg[i]))
        wup = [stg[i][:, 0:4096].rearrange("p (k c) -> p k c", c=256) for i in range(2)]
        r_wup = r_stg
        wo = wup
        r_wo = r_stg
        sgt = [sb_t(f"sgt{i}", [128, 512], F32) for i in range(4)]
        r_sgt = [Res(f"sgt{i}") for i in range(4)]
        m_sb = [sb_t(f"m_sb{i}", [128, D], F32) for i in range(4)]
        r_m = [Res(f"m{i}") for i in range(4)]
        growm = sb_t("growm", [128, D], F32)
        pw = sb_t("pw", [128, 4, 2, 256], BF16)
        psc = sb_t("psc", [128, 8], F32)
        selc = sb_t("selc", [128, 4], F32)
        r_cc = Res("cc")
        P.dma("sp", growm[:, :], gains[3:4, :].partition_broadcast(128), writes=[r_cc])
        P.dma("sp", psc[:, :], pscT[:, :], writes=[r_cc], append=True)
        P.dma("sp", selc[:, :], sel_d[:, :], writes=[r_cc], append=True)
        P.dma("pool", pw[:, :, :, :], pool_w.rearrange("g (k p) o -> p g k o", p=128), writes=[r_cc], append=True)
        stc = [0]
        npass_c = 4 if stop_after != "c1" else 1
        for ps_i in range(npass_c):
            own_tiles = list(range(ps_i * 4, ps_i * 4 + 4))
            P.dma("sp", oTs[:, :, :], oT_d[:, :, ps_i * 512:(ps_i + 1) * 512], reads=r_oT_d, writes=[r_oTs])
            for t, i in enumerate(own_tiles):
                P.dma("sp", m_sb[t][:, :], h1_d[i * 128:(i + 1) * 128, :], reads=[r_h1d[i]], writes=[r_m[t]])
                norm_transpose(m_sb[t][:, :], [r_m[t]], 128, xn2m, r_xn2m, t * 128, gT[:, 2, :], junk, xs_bf, stat, r_tmp)
                if i == 0:
                    P.dma("sp", hst[:NMETA, :], h1_d[SEQ:SEQ + NMETA, :], reads=[r_h1d[NSLOT]], writes=[r_hst])
                    P.dma("sp", halo[:, :], h1_d[NT_OWN * 128 + 112:NT_OWN * 128 + 128, :], reads=[r_h1d[NT_OWN]],
                          writes=[r_halo])
                    P.op("dve", lambda e: e.tensor_scalar(out=hst[:NMETA, :], in0=hst[:NMETA, :],
                                                          scalar1=selc[:NMETA, 0:1], scalar2=None, op0=ALU.mult),
                         reads=[r_hst, r_cc], writes=[r_hst])
                    P.op("dve", lambda e: e.scalar_tensor_tensor(out=halo[:, :], in0=halo[:, :],
                                                                 scalar=selc[:NMETA, 1:2], in1=hst[:NMETA, :],
                                                                 op0=ALU.mult, op1=ALU.add),
                         reads=[r_hst, r_halo, r_cc], writes=[r_halo])
                else:
                    sl = NT_OWN + i
                    P.dma("sp", halo[:, :], h1_d[sl * 128 + 112:sl * 128 + 128, :], reads=[r_h1d[sl]], writes=[r_halo])
                norm_transpose(halo[:, :], [r_halo], NMETA, xn2h, r_xn2h, t * 16, gT[:, 2, :], junk, xs_bf, stat, r_tmp)
            for g in range(4):
                wi = stc[0] % 2
                stc[0] += 1
                P.dma("pool", wup[wi], wview(w_in, 0, D, g * 256, 256), writes=[r_wup[wi]])
                nsteps = g + 1
                for sub in range(2):
                    def mmu(e, bank, rhs, n, sub=sub, wi=wi):
                        ins = None
                        for kc in range(16):
                            ins = e.matmul(psum[bank][:, :n], wup[wi][:, kc, sub * 128:(sub + 1) * 128], rhs(kc),
                                           start=(kc == 0), stop=(kc == 15))
                        return ins
                    ba, bb = (0, 1) if sub == 0 else (2, 3)
                    P.op("pe", lambda e, f=mmu, ba=ba: f(e, ba, lambda kc: xn2m[:, kc, :], 512),
                         reads=[r_wup[wi], r_xn2m], writes=[psr[ba]])
                    P.op("pe", lambda e, f=mmu, bb=bb: f(e, bb, lambda kc: xn2h[:, kc, :], 64),
                         reads=[r_wup[wi], r_xn2h], writes=[psr[bb]])
                    uv = uext[:, sub, :].rearrange("p (s c) -> p s c", c=144)
                    P.op("act", lambda e, ba=ba, uv=uv: e.activation(
                        out=uv[:, :, 16:144], in_=psum[ba][:, :].rearrange("p (s c) -> p s c", c=128), func=AF.Copy),
                        reads=[psr[ba]], writes=[r_uext[sub]])
                    P.op("act", lambda e, bb=bb, uv=uv: e.activation(
                        out=uv[:, :, 0:16], in_=psum[bb][:, :64].rearrange("p (s c) -> p s c", c=16), func=AF.Copy),
                        reads=[psr[bb], r_uext[sub]], writes=[r_uext[sub]])
                    cur = uext[:, sub, :]
                    rcur = r_uext[sub]
                    for k in range(nsteps):
                        sh = 1 << k
                        nx = ptmp[k % 2]
                        P.op("dve", lambda e, cur=cur, nx=nx, sh=sh: e.tensor_tensor(
                            out=nx[:, sh:576], in0=cur[:, sh:576], in1=cur[:, 0:576 - sh], op=ALU.add),
                            reads=[rcur, r_ptmp], writes=[r_ptmp])
                        cur = nx[:, :]
                        rcur = r_ptmp
                    wnd = float(1 << nsteps)
                    P.op("dve", lambda e, cur=cur, sub=sub, wnd=wnd: e.scalar_tensor_tensor(
                        out=dext[:, sub, 16:576], in0=cur[:, 16:576], scalar=1.0 / wnd, in1=uext[:, sub, 16:576],
                        op0=ALU.mult, op1=ALU.subtract),
                        reads=[r_ptmp, r_uext[sub]], writes=[r_dext])
                P.op("dve", lambda e: e.memset(dext[:, :, 0:16], 0.0), reads=[], writes=[r_dext])
                for oc2 in range(2):
                    for blk in range(2):
                        bk = 4 + (2 * oc2 + blk) % 4

                        def mmpw(e, bk=bk, oc2=oc2, blk=blk, g=g):
                            ins = None
                            for k2 in range(2):
                                ins = e.matmul(psum[bk][:, :288], pw[:, g, k2, oc2 * 128:(oc2 + 1) * 128],
                                               dext[:, k2, blk * 288:(blk + 1) * 288], start=(k2 == 0), stop=(k2 == 1))
                            return ins
                        P.op("pe", mmpw, reads=[r_dext, r_cc], writes=[psr[bk]])
                        c = 2 * g + oc2
                        P.op("act", lambda e, bk=bk, c=c, blk=blk: e.activation(
                            out=ypT[:, c, blk * 256:(blk + 1) * 256].rearrange("p (s c) -> p s c", c=128),
                            in_=psum[bk][:, :288].rearrange("p (s c) -> p s c", c=144)[:, :, 16:144],
                            func=AF.Copy, scale=psc[:, c:c + 1]),
                            reads=[psr[bk], r_cc], writes=[r_ypT])
            for oc in range(16):
                sub = oc % 2
                if sub == 0:
                    wi = stc[0] % 2
                    stc[0] += 1
                    W = wst[wi]
                    P.dma("pool", W["po"], wview(w_pool_o, 0, POOLW, oc * 128, 256), writes=[W["r"]])
                    P.dma("pool", W["gp"], wview(w_in, 0, D, 2112 + oc * 128, 256), writes=[W["r"]], append=True)
                    P.dma("pool", W["mo"], wview(w_mla_o, 0, D, oc * 128, 256), writes=[W["r"]], append=True)
                    P.dma("pool", W["gm"], wview(w_in, 0, D, 4160 + oc * 128, 256), writes=[W["r"]], append=True)
                bs = (0, 1, 2, 3) if oc % 2 == 0 else (4, 5, 6, 7)

                def mmc(e, bank, wt, rhs, nk, sub=sub):
                    ins = None
                    for kc in range(nk):
                        ins = e.matmul(psum[bank][:, :], wt[:, kc, sub * 128:(sub + 1) * 128], rhs(kc),
                                       start=(kc == 0), stop=(kc == nk - 1))
                    return ins
                P.op("pe", lambda e, f=mmc, W=W, b=bs[0]: f(e, b, W["po"], lambda kc: ypT[:, kc, :], 8),
                     reads=[W["r"], r_ypT], writes=[psr[bs[0]]])
                P.op("pe", lambda e, f=mmc, W=W, b=bs[1]: f(e, b, W["gp"], lambda kc: xn2m[:, kc, :], 16),
                     reads=[W["r"], r_xn2m], writes=[psr[bs[1]]])
                P.op("pe", lambda e, f=mmc, W=W, b=bs[2]: f(e, b, W["mo"], lambda kc: oTs[:, kc, :], 16),
                     reads=[W["r"], r_oTs], writes=[psr[bs[2]]])
                P.op("pe", lambda e, f=mmc, W=W, b=bs[3]: f(e, b, W["gm"], lambda kc: xn2m[:, kc, :], 16),
                     reads=[W["r"], r_xn2m], writes=[psr[bs[3]]])
                P.op("act", lambda e, b=bs[1]: e.activation(out=sgt[0][:, :], in_=psum[b][:, :], func=AF.Sigmoid),
                     reads=[psr[bs[1]]], writes=[r_sgt[0]])
                P.op("act", lambda e, b=bs[3]: e.activation(out=sgt[1][:, :], in_=psum[b][:, :], func=AF.Sigmoid),
                     reads=[psr[bs[3]]], writes=[r_sgt[1]])
                P.op("dve", lambda e, b=bs[0]: e.tensor_tensor(out=sgt[2][:, :], in0=psum[b][:, :], in1=sgt[0][:, :],
                                                              op=ALU.mult),
                     reads=[psr[bs[0]], r_sgt[0]], writes=[r_sgt[2]])
                P.op("dve", lambda e, b=bs[2]: e.tensor_tensor(out=sgt[3][:, :], in0=psum[b][:, :], in1=sgt[1][:, :],
                                                              op=ALU.mult),
                     reads=[psr[bs[2]], r_sgt[1]], writes=[r_sgt[3]])
                P.op("dve", lambda e, oc=oc: e.tensor_tensor(out=yT[:, oc, :], in0=sgt[2][:, :], in1=sgt[3][:, :],
                                                            op=ALU.add),
                     reads=[r_sgt[2], r_sgt[3]], writes=[r_yT])
            for e8 in range(8):
                wi = stc[0] % 2
                stc[0] += 1
                P.dma("pool", wo[wi], wview(w_out, 0, D, e8 * 256, 256), writes=[r_wo[wi]])
                for t in range(4):
                    bk = (e8 * 4 + t) % 8

                    def mmo(e, bk=bk, t=t, wi=wi):
                        ins = None
                        for kc in range(16):
                            ins = e.matmul(psum[bk][:, :256], yT[:, kc, t * 128:(t + 1) * 128], wo[wi][:, kc, :],
                                           start=(kc == 0), stop=(kc == 15))
                        return ins
                    P.op("pe", mmo, reads=[r_yT, r_wo[wi]], writes=[psr[bk]])
                    P.op("act", lambda e, bk=bk, t=t, e8=e8: e.activation(out=m_sb[t][:, e8 * 256:(e8 + 1) * 256],
                                                                         in_=psum[bk][:, :256], func=AF.Copy),
                         reads=[psr[bk]], writes=[r_m[t]])
            for t, i in enumerate(own_tiles):
                P.op("act", lambda e, t=t: e.activation(out=junk[:, :], in_=m_sb[t][:, :], func=AF.Square,
                                                        accum_out=stat2[:, 0:1]),
                     reads=[r_m[t]], writes=[r_tmp2])
                P.op("act", lambda e: e.activation(out=stat2[:, 1:2], in_=stat2[:, 0:1], func=AF.Sqrt, scale=1.0 / D,
                                                   bias=epsb[:, 0:1]),
                     reads=[r_tmp2, r_const], writes=[r_tmp2])
                P.op("dve", lambda e: e.reciprocal(out=stat2[:, 2:3], in_=stat2[:, 1:2]), reads=[r_tmp2], writes=[r_tmp2])
                P.dma("sp", hst[:, :], h1_d[i * 128:(i + 1) * 128, :], reads=[r_h1d[i]], writes=[r_hst])
                P.op("dve", lambda e, t=t: e.scalar_tensor_tensor(out=m_sb[t][:, :], in0=m_sb[t][:, :],
                                                                 scalar=stat2[:, 2:3], in1=growm[:, :],
                                                                 op0=ALU.mult, op1=ALU.mult),
                     reads=[r_tmp2, r_cc, r_m[t]], writes=[r_m[t]])
                P.op("dve", lambda e, t=t: e.tensor_tensor(out=m_sb[t][:, :], in0=m_sb[t][:, :], in1=hst[:, :], op=ALU.add),
                     reads=[r_m[t], r_hst], writes=[r_m[t]])
                P.dma("sp", h2_d[i * 128:(i + 1) * 128, :], m_sb[t][:, :], reads=[r_m[t]], writes=[r_h2d[i]])
                if stop_after == "c1":
                    dump(m_sb[t][:, :], [r_m[t]], t * 128, 128, 0, D)
        P.barrier()

    if stop_after == "all" and "c2" in phases:
      with contextlib.ExitStack() as ph:
        sb = alloc_ffn(ph, 6, 'c2_', T=768)
        load_grow(sb, 5)
        for own_ in ([0, 1, 2, 3, 4, 5], [6, 7, 8, 9, 10], [11, 12, 13, 14, 15]):
            tiles = []
            for i in own_:
                tiles.append(dict(nt=128, slot=i, load=(lambda dst, rd, i=i: P.dma(
                    "sp", dst, h2_d[i * 128:(i + 1) * 128, :], reads=[r_h2d[i]], writes=rd))))

            def done2(ti, y_ap, ry, tiles=tiles):
                i = tiles[ti]["slot"]
                P.dma("sp", out_d[i * 128:(i + 1) * 128, :], y_ap, reads=ry, writes=[Res()])
            for t in tiles:
                t["done"] = done2
            ffn_pass(1, tiles, sb)
        P.barrier()

    P.barrier()
    with nc.Block() as block:
        P.emit(block)
    es.close()
    return nc


def _prep_inputs(inputs):
    x = np.asarray(inputs["x"], dtype=np.float32)
    B = x.shape[0]
    gains = np.stack([np.asarray(inputs[k], np.float32)[0] for k in
                      ("norm_ffn1_pre", "norm_ffn1_post", "norm_mix_pre", "norm_mix_post",
                       "norm_ffn2_pre", "norm_ffn2_post")], axis=0)
    common = {
        "meta": np.ascontiguousarray(inputs["meta_tokens"], dtype=np.float32),
        "gains": np.ascontiguousarray(gains),
        "gainsT": np.ascontiguousarray(gains.reshape(6, 16, 128).transpose(2, 0, 1).reshape(128, 96)),
        "ffn1_w_gu": np.asarray(inputs["ffn1_w_gu"], np.float32)[0],
        "ffn2_w_gu": np.asarray(inputs["ffn2_w_gu"], np.float32)[0],
        "ffn1_w_down": np.asarray(inputs["ffn1_w_down"], np.float32)[0],
        "ffn2_w_down": np.asarray(inputs["ffn2_w_down"], np.float32)[0],
        "w_in": np.asarray(inputs["w_in"], np.float32)[0],
        "pool_w": np.asarray(inputs["pool_w"], np.float32)[0],
        "pool_scale": np.asarray(inputs["pool_scale"], np.float32)[0],
        "w_pool_o": np.asarray(inputs["w_pool_o"], np.float32)[0],
        "q_a_norm": np.asarray(inputs["q_a_norm"], np.float32)[0],
        "w_q_b": np.asarray(inputs["w_q_b"], np.float32)[0],
        "kv_a_norm": np.asarray(inputs["kv_a_norm"], np.float32)[0],
        "w_kv_b": np.asarray(inputs["w_kv_b"], np.float32)[0],
        "w_mla_o": np.asarray(inputs["w_mla_o"], np.float32)[0],
        "w_out": np.asarray(inputs["w_out"], np.float32)[0],
        "ident": np.eye(128, dtype=np.float32),
        "latgT": np.ascontiguousarray(np.concatenate([
            np.asarray(inputs["q_a_norm"], np.float32)[0].reshape(4, 128).T,
            np.asarray(inputs["kv_a_norm"], np.float32)[0].reshape(4, 128).T], axis=1)),
        "pscT": np.ascontiguousarray(np.asarray(inputs["pool_scale"], np.float32)[0].reshape(8, 128).T),
        "tri": np.triu(np.ones((128, 128), np.float32)),
    }
    inv = (10000.0 ** (-np.arange(0, ROPE, 2, dtype=np.float32) / ROPE)).astype(np.float32)
    in_maps = []
    tile_maps = []
    for core in range(8):
        b, c = core // 2, core % 2
        own = [2 * i + c for i in range(16)]
        if c == 1:
            oth = [2 * i for i in range(16)]
        else:
            oth = [31] + [2 * i - 1 for i in range(1, 16)]
        order = own + oth
        xt = x[b].reshape(32, 128, D)[order].reshape(SEQ, D)
        pos = np.concatenate([np.arange(NMETA)] + [NMETA + j * 128 + np.arange(128) for j in order]).astype(np.float32)
        ang = pos[None, :] * np.concatenate([inv, inv])[:, None]
        sel = np.zeros((128, 4), np.float32)
        if c == 0:
            sel[:, 0] = 1.0
            sel[:, 2] = -30000.0
        else:
            sel[:, 1] = 1.0
        m = dict(common)
        m["xs"] = np.ascontiguousarray(xt)
        m["cosT"] = np.cos(ang).astype(np.float32)
        m["sinT"] = np.sin(ang).astype(np.float32)
        m["sel"] = sel
        in_maps.append(m)
        tile_maps.append(own)
    return in_maps, tile_maps


_NC_CACHE = {}


def kernel(**inputs):
    in_maps, tile_maps = _prep_inputs(inputs)
    if "nc" not in _NC_CACHE:
        _NC_CACHE["nc"] = build_program()
    nc = _NC_CACHE["nc"]
    res = run_bass_kernel_spmd(nc, in_maps, core_ids=list(range(8)))
    out = np.empty((4, SEQ, D), np.float32)
    for core in range(8):
        b = core // 2
        o = res.results[core]["out"].reshape(16, 128, D)
        for i, j in enumerate(tile_maps[core]):
            out[b, j * 128:(j + 1) * 128] = o[i]
    return out
```

```python
import contextlib
import numpy as np
import concourse.bass as bass
import concourse.mybir as mybir
from concourse.bass_utils import run_bass_kernel_spmd

F32 = mybir.dt.float32
BF16 = mybir.dt.bfloat16
AF = mybir.ActivationFunctionType
ALU = mybir.AluOpType

D = 2048
SEQ = 4096
NMETA = 16
DFF = 5632
NFC = DFF // 128
QL = 512
KVL = 512
ROPE = 64
NOPE = 128
VD = 128
NH = 16
QKD = NOPE + ROPE
POOLW = 1024
IN_COLS = 6208
EPS = 1e-6
SCALE = QKD ** -0.5
NT_OWN = 16
NSLOT = 32
LK = NMETA + SEQ
ENGS = ("pe", "act", "dve", "pool", "sp")


class Res:
    __slots__ = ("name", "w_eng", "w_dma", "r_eng", "r_dma")

    def __init__(self, name=""):
        self.name = name
        self.w_eng = {}
        self.w_dma = []
        self.r_eng = {}
        self.r_dma = []


class Prog:
    NRING = 12

    def __init__(self, nc, es):
        self.nc = nc
        self.lists = {e: [] for e in ENGS}
        self.cnt = {e: 0 for e in ENGS}
        self.known = {e: {e2: 0 for e2 in ENGS} for e in ENGS}
        self.kdma = {e: {} for e in ENGS}
        self.snaps = {e: [None] for e in ENGS}
        self.sem = {e: es.enter_context(nc.semaphore("c_" + e)) for e in ENGS}
        self.ring = {}
        self.ring_val = {}
        self.ring_pos = {}
        for q in ("sp", "pool"):
            self.ring[q] = [es.enter_context(nc.semaphore(f"d_{q}{i}")) for i in range(self.NRING)]
            self.ring_val[q] = [0] * self.NRING
            self.ring_pos[q] = 0

    def _deps(self, eng, reads, writes):
        waits_e = {}
        waits_d = {}

        def need_e(e2, idx, raw):
            if e2 == eng and eng == "pe":
                return
            if self.known[eng][e2] >= idx:
                return
            if waits_e.get(e2, 0) < idx:
                waits_e[e2] = idx

        def need_d(tok):
            key, val = tok
            if self.kdma[eng].get(key, 0) >= val:
                return
            if waits_d.get(key, 0) < val:
                waits_d[key] = val

        for r in reads:
            for e2, idx in r.w_eng.items():
                need_e(e2, idx, True)
            for t in r.w_dma:
                need_d(t)
        for w in writes:
            for e2, idx in w.w_eng.items():
                need_e(e2, idx, False)
            for t in w.w_dma:
                need_d(t)
            for e2, idx in w.r_eng.items():
                need_e(e2, idx, False)
            for t in w.r_dma:
                need_d(t)
        out = []
        for e2, idx in waits_e.items():
            out.append((self.sem[e2], idx))
            kn = self.known[eng]
            if kn[e2] < idx:
                kn[e2] = idx
            sn = self.snaps[e2][idx]
            if sn is not None:
                for e3, v in zip(ENGS, sn):
                    if kn[e3] < v:
                        kn[e3] = v
        for key, val in waits_d.items():
            out.append((key, val))
            self.kdma[eng][key] = val
        return out

    def op(self, eng, fn, reads=(), writes=()):
        waits = self._deps(eng, reads, writes)
        self.cnt[eng] += 1
        idx = self.cnt[eng]
        self.snaps[eng].append(tuple(self.known[eng][e] for e in ENGS))
        self.lists[eng].append((waits, fn, self.sem[eng], 1))
        for w in writes:
            w.w_eng = {eng: idx}
            w.w_dma = []
            w.r_eng = {}
            w.r_dma = []
        for r in reads:
            if r.r_eng.get(eng, 0) < idx:
                r.r_eng[eng] = idx
        return idx

    def dma(self, q, out, in_, reads=(), writes=(), append=False):
        waits = [] if append else self._deps(q, reads, writes)
        pos = self.ring_pos[q]
        self.ring_pos[q] = (pos + 1) % self.NRING
        sem = self.ring[q][pos]
        prev = self.ring_val[q][pos]
        if prev and self.kdma[q].get(sem, 0) < prev:
            waits.append((sem, prev))
            self.kdma[q][sem] = prev
        val = prev + 16
        self.ring_val[q][pos] = val
        tok = (sem, val)
        self.lists[q].append((waits, lambda e: e.dma_start(out=out, in_=in_), sem, 16))
        for w in writes:
            if append:
                w.w_dma.append(tok)
                continue
            w.w_eng = {}
            w.w_dma = [tok]
            w.r_eng = {}
            w.r_dma = []
        for r in reads:
            r.r_dma.append(tok)
        return tok

    def barrier(self):
        tgt = dict(self.cnt)
        dm = []
        for q in self.ring:
            for s, v in zip(self.ring[q], self.ring_val[q]):
                if v:
                    dm.append((s, v))
        for e in ENGS:
            waits = []
            for e2 in ENGS:
                if e2 != e and self.known[e][e2] < tgt[e2]:
                    waits.append((self.sem[e2], tgt[e2]))
                    self.known[e][e2] = tgt[e2]
            for s, v in dm:
                if self.kdma[e].get(s, 0) < v:
                    waits.append((s, v))
                    self.kdma[e][s] = v
            if waits:
                self.lists[e].append((waits, None, None, 0))

    def emit(self, block):
        def mk(e):
            lst = self.lists[e]

            def body(eng):
                for waits, fn, sem, inc in lst:
                    for s, v in waits:
                        eng.wait_ge(s, v)
                    if fn is not None:
                        ins = fn(eng)
                        ins.then_inc(sem, inc)
            return body
        block.tensor(mk("pe"))
        block.scalar(mk("act"))
        block.vector(mk("dve"))
        block.gpsimd(mk("pool"))
        block.sync(mk("sp"))


def build_program(debug=None):
    nc = bass.Bass("TRN2", target_bir_lowering=False)
    es = contextlib.ExitStack()

    def din(name, shape, dt=F32):
        return nc.dram_tensor(name, list(shape), dt, kind="ExternalInput").ap()

    xs = din("xs", [SEQ, D])
    meta = din("meta", [NMETA, D])
    gains = din("gains", [6, D])
    gainsT = din("gainsT", [128, 6 * 16])
    w_gu = [din("ffn1_w_gu", [D, 2 * DFF]), din("ffn2_w_gu", [D, 2 * DFF])]
    w_dn = [din("ffn1_w_down", [DFF, D]), din("ffn2_w_down", [DFF, D])]
    w_in = din("w_in", [D, IN_COLS])
    pool_w = din("pool_w", [4, 256, 256])
    pool_scale = din("pool_scale", [POOLW])
    w_pool_o = din("w_pool_o", [POOLW, D])
    q_a_norm = din("q_a_norm", [QL])
    w_q_b = din("w_q_b", [QL, NH * QKD])
    kv_a_norm = din("kv_a_norm", [KVL])
    w_kv_b = din("w_kv_b", [KVL, NH * (NOPE + VD)])
    w_mla_o = din("w_mla_o", [NH * VD, D])
    w_out = din("w_out", [D, D])
    cosT = din("cosT", [ROPE, LK])
    sinT = din("sinT", [ROPE, LK])
    ident_d = din("ident", [128, 128])
    tri_d = din("tri", [128, 128])
    sel_d = din("sel", [128, 4])
    latgT = din("latgT", [128, 8])
    pscT = din("pscT", [128, 8])
    out_d = nc.dram_tensor("out", [NT_OWN * 128, D], F32, kind="ExternalOutput").ap()

    h1_d = nc.dram_tensor("h1_d", [SEQ + NMETA, D], F32).ap()
    kvnT_d = nc.dram_tensor("kvnT_d", [128, 4, LK], BF16).ap()
    kpeT_d = nc.dram_tensor("kpeT_d", [ROPE, LK], BF16).ap()
    cqnT_d = nc.dram_tensor("cqnT_d", [128, 4, NT_OWN * 128], BF16).ap()
    oT_d = nc.dram_tensor("oT_d", [128, NH, NT_OWN * 128], BF16).ap()
    h2_d = nc.dram_tensor("h2_d", [NT_OWN * 128, D], F32).ap()
    dbg = None
    if debug is not None:
        dbg = nc.dram_tensor("dbg", list(debug[:2]), F32, kind="ExternalOutput").ap()

    P = Prog(nc, es)
    stop_after = debug[2] if (debug is not None and len(debug) > 2) else "all"
    only = debug[3] if (debug is not None and len(debug) > 3) else None
    import os
    phases = set((os.environ.get("K_PHASES") or "a1,a2,b,c1,c2").split(","))
    psum = [es.enter_context(nc.psum_tensor(f"ps{i}", [128, 512], F32)) for i in range(8)]
    psr = [Res(f"ps{i}") for i in range(8)]

    ident = es.enter_context(nc.sbuf_tensor("identb", [128, 128], BF16))
    gT = es.enter_context(nc.sbuf_tensor("gT", [128, 6, 16], F32))
    r_const = Res("const")
    epsb = es.enter_context(nc.sbuf_tensor("epsb", [128, 1], F32))
    P.op("dve", lambda e: e.memset(epsb[:], EPS), writes=[r_const])
    P.dma("pool", ident[:], ident_d[:, :], writes=[r_const])
    P.dma("sp", gT[:].rearrange("p g k -> p (g k)"), gainsT[:, :], writes=[r_const])

    def wview(w, r0, nr, c0, ncol):
        return w[r0:r0 + nr, c0:c0 + ncol].rearrange("(kc p) f -> p kc f", p=128)

    def norm_transpose(src, r_src, nt, xnT, r_xnT, col0, gsrc, sq_junk, xs_bf, stat, r_tmp, nkc=16):
        width = nkc * 128
        P.op("act", lambda e: e.activation(out=sq_junk[:nt, :width], in_=src, func=AF.Square,
                                           accum_out=stat[:nt, 0:1]),
             reads=list(r_src), writes=[r_tmp])
        P.op("act", lambda e: e.activation(out=stat[:nt, 1:2], in_=stat[:nt, 0:1], func=AF.Sqrt,
                                           scale=1.0 / width, bias=epsb[:nt, 0:1]),
             reads=[r_tmp, r_const], writes=[r_tmp])
        P.op("dve", lambda e: e.reciprocal(out=stat[:nt, 2:3], in_=stat[:nt, 1:2]),
             reads=[r_tmp], writes=[r_tmp])
        P.op("act", lambda e: e.activation(out=xs_bf[:nt, :width], in_=src, func=AF.Copy,
                                           scale=stat[:nt, 2:3]),
             reads=list(r_src) + [r_tmp], writes=[r_tmp])
        for k0 in range(0, nkc, 8):
            kn = min(8, nkc - k0)
            pb = tr_banks[tr_state[0] % len(tr_banks)]
            tr_state[0] += 1
            pv = psum[pb][:].bitcast(BF16)

            def tr(e, k0=k0, kn=kn, pv=pv):
                ins = None
                for k in range(kn):
                    ins = e.transpose(pv[:, k * 128:k * 128 + nt], xs_bf[:nt, (k0 + k) * 128:(k0 + k + 1) * 128],
                                      ident[:nt, :nt])
                return ins
            P.op("pe", tr, reads=[r_tmp, r_const], writes=[psr[pb]])
            P.op("dve", lambda e, k0=k0, kn=kn, pv=pv: e.tensor_tensor(
                out=xnT[:, k0:k0 + kn, col0:col0 + nt],
                in0=pv[:, :kn * 128].rearrange("p (k t) -> p k t", k=kn)[:, :, :nt],
                in1=gsrc[:, k0:k0 + kn].unsqueeze(2).to_broadcast([128, kn, nt]),
                op=ALU.mult),
                reads=[psr[pb], r_const], writes=[r_xnT])

    tr_banks = [6, 7]
    tr_state = [0]

    GROUPS = [(0, 12), (12, 12), (24, 12), (36, 8)]

    def alloc_ffn(ph, ntile, pfx, T=528, nxst=1):
        def sb_t(name, shape, dt):
            return ph.enter_context(nc.sbuf_tensor(pfx + name, shape, dt))
        sb = {}
        sb["xnT"] = sb_t("xnT", [128, 16, T], BF16)
        sb["r_xnT"] = Res("xnT")
        sb["actT"] = [sb_t(f"actT{i}", [128, 12, T], BF16) for i in range(2)]
        sb["r_act"] = [Res("act0"), Res("act1")]
        sb["ysb"] = [sb_t(f"ysb{i}", [128, D], F32) for i in range(ntile)]
        sb["r_ysb"] = [[Res(f"y{i}q{q}") for q in range(4)] for i in range(ntile)]
        sb["wg"] = [sb_t(f"wg{i}", [128, 16, 256], BF16) for i in range(2)]
        sb["wu"] = [sb_t(f"wu{i}", [128, 16, 256], BF16) for i in range(2)]
        sb["r_wgu"] = [Res(f"wgu{i}") for i in range(2)]
        sb["wd"] = [sb_t(f"wd{i}", [128, 12, 512], BF16) for i in range(2)]
        sb["r_wd"] = [Res("wd0"), Res("wd1")]
        sb["sg"] = [sb_t(f"sg{i}", [128, 512], F32) for i in range(2)]
        sb["r_sg"] = [Res("sg0"), Res("sg1")]
        sb["junk"] = sb_t("junk", [128, D], BF16)
        sb["xs_bf"] = sb_t("xs_bf", [128, D], BF16)
        sb["stat"] = sb_t("stat", [128, 4], F32)
        sb["stat2"] = sb_t("stat2", [128, 4], F32)
        sb["r_tmp"] = Res("tmp")
        sb["r_tmp2"] = Res("tmp2")
        sb["grow"] = sb_t("grow", [128, D], F32)
        sb["r_grow"] = Res("grow")
        sb["xst"] = [sb_t(f"xst{i}", [128, D], F32) for i in range(nxst)]
        sb["r_xst"] = [Res(f"xst{i}") for i in range(nxst)]
        sb["cnt"] = dict(gu=0, bank=0, sg=0, wd=0, dn=0, xst=0)
        sb["gu_banks"] = [(0, 1), (2, 3)]
        sb["dn_banks"] = [4, 5, 6, 7]
        return sb

    def load_grow(sb, g_post):
        P.dma("sp", sb["grow"][:, :], gains[g_post:g_post + 1, :].partition_broadcast(128), writes=[sb["r_grow"]])
        P.op("dve", lambda e: e.tensor_scalar(out=sb["grow"][:, :], in0=sb["grow"][:, :], scalar1=0.5, scalar2=None,
                                              op0=ALU.mult), reads=[sb["r_grow"]], writes=[sb["r_grow"]])

    def ffn_pass(which, tiles, sb):
        wgu, wdn = w_gu[which], w_dn[which]
        g_pre = 0 if which == 0 else 4
        xnT, actT, ysb = sb["xnT"], sb["actT"], sb["ysb"]
        r_xnT = sb["r_xnT"]
        cnt = sb["cnt"]
        T = sum(t["nt"] for t in tiles)
        cols = []
        c = 0
        for t in tiles:
            cols.append(c)
            c += t["nt"]
        for ti, (t, c0) in enumerate(zip(tiles, cols)):
            nt = t["nt"]
            if t.get("x_ap") is None:
                t["load"](ysb[ti][:nt, :], sb["r_ysb"][ti])
                src, rs = ysb[ti][:nt, :], sb["r_ysb"][ti]
            else:
                src, rs = t["x_ap"], [t["r_x"]]
            norm_transpose(src, rs, nt, xnT, r_xnT, c0, gT[:, g_pre, :], sb["junk"], sb["xs_bf"],
                           sb["stat"], sb["r_tmp"])
        nb_ = -(-T // 512)
        bsz = -(-T // (nb_ * 16)) * 16
        nblocks = [(b0, min(bsz, T - b0)) for b0 in range(0, T, bsz)]
        for gi, (f0, gn) in enumerate(GROUPS):
            ab = gi % 2
            r_act = sb["r_act"][ab]
            for fp in range(gn // 2):
                st = cnt["gu"] % 2
                cnt["gu"] += 1
                wg_t, wu_t, r_w = sb["wg"][st], sb["wu"][st], sb["r_wgu"][st]
                fcol = (f0 + 2 * fp) * 128
                P.dma("pool", wg_t[:], wview(wgu, 0, D, fcol, 256), writes=[r_w])
                P.dma("pool", wu_t[:], wview(wgu, 0, D, DFF + fcol, 256), writes=[r_w], append=True)
                for sub in range(2):
                    fi = 2 * fp + sub
                    for (b0, bn) in nblocks:
                        bg, bu = sb["gu_banks"][cnt["bank"] % 2]
                        cnt["bank"] += 1

                        def mm(e, wt, bank, b0=b0, bn=bn, sub=sub):
                            ins = None
                            for kc in range(16):
                                ins = e.matmul(psum[bank][:, :bn], wt[:, kc, sub * 128:(sub + 1) * 128],
                                               xnT[:, kc, b0:b0 + bn], start=(kc == 0), stop=(kc == 15))
                            return ins
                        P.op("pe", lambda e, wt=wg_t, bank=bg, mm=mm: mm(e, wt, bank), reads=[r_w, r_xnT],
                             writes=[psr[bg]])
                        P.op("pe", lambda e, wt=wu_t, bank=bu, mm=mm: mm(e, wt, bank), reads=[r_w, r_xnT],
                             writes=[psr[bu]])
                        sgi = cnt["sg"] % 2
                        cnt["sg"] += 1
                        sg, r_sg = sb["sg"][sgi], sb["r_sg"][sgi]
                        P.op("act", lambda e, bg=bg, bn=bn, sg=sg: e.activation(
                            out=sg[:, :bn], in_=psum[bg][:, :bn], func=AF.Silu),
                            reads=[psr[bg]], writes=[r_sg])
                        P.op("dve", lambda e, bu=bu, b0=b0, bn=bn, sg=sg, fi=fi, ab=ab: e.tensor_tensor(
                            out=actT[ab][:, fi, b0:b0 + bn], in0=psum[bu][:, :bn], in1=sg[:, :bn], op=ALU.mult),
                            reads=[psr[bu], r_sg], writes=[r_act])
            for q in range(4):
                st = cnt["wd"] % 2
                cnt["wd"] += 1
                wd_t, r_wd = sb["wd"][st], sb["r_wd"][st]
                P.dma("pool", wd_t[:, :gn, :], wview(wdn, f0 * 128, gn * 128, q * 512, 512), writes=[r_wd])
                for ti, (t, c0) in enumerate(zip(tiles, cols)):
                    nt = t["nt"]
                    pd = sb["dn_banks"][cnt["dn"] % 4]
                    cnt["dn"] += 1

                    def mmd(e, pd=pd, c0=c0, nt=nt, wd_t=wd_t, ab=ab, gn=gn):
                        ins = None
                        for fi in range(gn):
                            ins = e.matmul(psum[pd][:nt, :], actT[ab][:, fi, c0:c0 + nt], wd_t[:, fi, :],
                                           start=(fi == 0), stop=(fi == gn - 1))
                        return ins
                    P.op("pe", mmd, reads=[r_act, r_wd], writes=[psr[pd]])
                    r_y = sb["r_ysb"][ti][q]
                    dst = ysb[ti][:nt, q * 512:(q + 1) * 512]
                    if gi == 0:
                        P.op("act", lambda e, dst=dst, pd=pd, nt=nt: e.activation(out=dst, in_=psum[pd][:nt, :],
                                                                                 func=AF.Copy),
                             reads=[psr[pd]], writes=[r_y])
                    else:
                        P.op("dve", lambda e, dst=dst, pd=pd, nt=nt: e.tensor_tensor(
                            out=dst, in0=psum[pd][:nt, :], in1=dst, op=ALU.add),
                            reads=[psr[pd], r_y], writes=[r_y])
        for ti, t in enumerate(tiles):
            nt = t["nt"]
            stat = sb["stat2"]
            r_t2 = sb["r_tmp2"]
            ry = sb["r_ysb"][ti]
            if t.get("x_ap") is None:
                k = cnt["xst"] % len(sb["xst"])
                cnt["xst"] += 1
                t["load"](sb["xst"][k][:nt, :], [sb["r_xst"][k]])
                xap, rx = sb["xst"][k][:nt, :], sb["r_xst"][k]
            else:
                xap, rx = t["x_ap"], t["r_x"]
            P.op("act", lambda e, ti=ti, nt=nt: e.activation(out=sb["junk"][:nt, :], in_=ysb[ti][:nt, :],
                                                            func=AF.Square, accum_out=stat[:nt, 0:1]),
                 reads=ry, writes=[r_t2])
            P.op("act", lambda e, nt=nt: e.activation(out=stat[:nt, 1:2], in_=stat[:nt, 0:1], func=AF.Sqrt,
                                                      scale=1.0 / D, bias=epsb[:nt, 0:1]),
                 reads=[r_t2, r_const], writes=[r_t2])
            P.op("dve", lambda e, nt=nt: e.reciprocal(out=stat[:nt, 2:3], in_=stat[:nt, 1:2]),
                 reads=[r_t2], writes=[r_t2])
            P.op("dve", lambda e, ti=ti, nt=nt: e.scalar_tensor_tensor(
                out=ysb[ti][:nt, :], in0=ysb[ti][:nt, :], scalar=stat[:nt, 2:3], in1=sb["grow"][:nt, :],
                op0=ALU.mult, op1=ALU.mult),
                reads=[r_t2, sb["r_grow"]] + ry, writes=ry)
            P.op("dve", lambda e, ti=ti, nt=nt, xap=xap: e.tensor_tensor(out=ysb[ti][:nt, :], in0=ysb[ti][:nt, :],
                                                                         in1=xap, op=ALU.add),
                 reads=ry + [rx], writes=ry)
            t["done"](ti, ysb[ti][:nt, :], ry)

    r_h1d = [Res(f"h1d{i}") for i in range(NSLOT + 1)]
    with contextlib.ExitStack() as ph:
        sb = alloc_ffn(ph, 6, 'a1_', T=768)
        load_grow(sb, 1)
        pass_slots = [list(range(0, 6)), list(range(6, 12)), list(range(12, 17)), list(range(17, 22)),
                      list(range(22, 27)), list(range(27, 32)) + [NSLOT]]
        if debug is not None and stop_after in ('a1', 'a2'):
            pass_slots = [[0, 1, 2, 3, NSLOT]]
        if only is not None or "a1" not in phases:
            pass_slots = []
        for slots_ in pass_slots:
            tiles = []
            for s in slots_:
                if s == NSLOT:
                    tiles.append(dict(nt=NMETA, slot=NSLOT, load=(lambda dst, rd: P.dma(
                        "sp", dst, meta[:, :], writes=rd))))
                else:
                    tiles.append(dict(nt=128, slot=s, load=(lambda dst, rd, s=s: P.dma(
                        "sp", dst, xs[s * 128:(s + 1) * 128, :], writes=rd))))

            def done(ti, y_ap, ry, tiles=tiles):
                s = tiles[ti]["slot"]
                nt = tiles[ti]["nt"]
                P.dma("sp", h1_d[s * 128:s * 128 + nt, :], y_ap, reads=ry, writes=[r_h1d[s]])
                if dbg is not None and stop_after == "a1":
                    P.dma("sp", dbg[ti * 128:ti * 128 + nt, :], y_ap, reads=ry, writes=[Res()])
            for t in tiles:
                t["done"] = done
            ffn_pass(0, tiles, sb)
        P.barrier()


    def dump(ap_sb, reads, r0, nrow, c0, ncol):
        P.dma("sp", dbg[r0:r0 + nrow, c0:c0 + ncol], ap_sb, reads=reads, writes=[Res()])

    r_kvn_d = [Res(f"kvnd{i}") for i in range(NSLOT + 1)]
    r_kpe_d = [Res(f"kped{i}") for i in range(NSLOT + 1)]
    r_cqn_d = [Res(f"cqnd{i}") for i in range(NT_OWN)]
    if stop_after not in ("a1",) and only is None and "a2" in phases:
      with contextlib.ExitStack() as ph:
        def sb_t(name, shape, dt):
            return ph.enter_context(nc.sbuf_tensor('a2_' + name, shape, dt))
        w_lat = sb_t("w_lat", [128, 16, 1152], BF16)
        r_wlat = Res("wlat")
        P.dma("pool", w_lat[:, :, 0:544], wview(w_in, 0, D, 1024, 544), writes=[r_wlat])
        P.dma("pool", w_lat[:, :, 544:1088], wview(w_in, 0, D, 1568, 544), writes=[r_wlat], append=True)
        P.op("dve", lambda e: e.tensor_scalar(out=w_lat[:, :, 1088:1120], in0=w_lat[:, :, 1056:1088], scalar1=-1.0,
                                              scalar2=None, op0=ALU.mult), reads=[r_wlat], writes=[r_wlat])
        P.op("dve", lambda e: e.tensor_copy(out=w_lat[:, :, 1120:1152], in_=w_lat[:, :, 1024:1056]),
             reads=[r_wlat], writes=[r_wlat])
        cos_sb = sb_t("cos_sb", [ROPE, LK], F32)
        sin_sb = sb_t("sin_sb", [ROPE, LK], F32)
        latg = sb_t("latg", [128, 8], F32)
        r_tab = Res("tab")
        P.dma("sp", cos_sb[:, :], cosT[:, :], writes=[r_tab])
        P.dma("sp", sin_sb[:, :], sinT[:, :], writes=[r_tab], append=True)
        P.dma("sp", latg[:, :], latgT[:, :], writes=[r_tab], append=True)
        hbuf = [sb_t(f"hbuf{i}", [128, D], F32) for i in range(2)]
        r_hbuf = [Res("hb0"), Res("hb1")]
        xn2 = [sb_t(f"xn2_{i}", [128, 16, 128], BF16) for i in range(2)]
        r_xn2 = [Res("xn2_0"), Res("xn2_1")]
        junk = sb_t("junk", [128, D], BF16)
        xs_bf = sb_t("xs_bf", [128, D], BF16)
        stat = sb_t("stat", [128, 4], F32)
        r_tmp = Res("tmp")
        junk2 = sb_t("junk2", [128, 512], BF16)
        xs_bf2 = sb_t("xs_bf2", [128, 512], BF16)
        stat2 = sb_t("stat2", [128, 4], F32)
        r_tmp2 = Res("tmp2")
        t1 = sb_t("t1", [ROPE, 128], F32)
        t2 = sb_t("t2", [ROPE, 128], F32)
        r_t12 = Res("t12")
        kpe_t = [sb_t(f"kpe_t{i}", [ROPE, 128], BF16) for i in range(2)]
        r_kpe_t = [Res("kpet0"), Res("kpet1")]
        kvn_t = [sb_t(f"kvn_t{i}", [128, 4, 128], BF16) for i in range(2)]
        r_kvn_t = [Res("kvnt0"), Res("kvnt1")]
        cqn_t = [sb_t(f"cqn_t{i}", [128, 4, 128], BF16) for i in range(2)]
        r_cqn_t = [Res("cqnt0"), Res("cqnt1")]
        ntile_a2 = NSLOT + 1 if debug is None or stop_after != "a2" else 5
        for tix in range(ntile_a2):
            pb = tix % 2
            if tix == 0:
                nt, row0, kcol, own, s = NMETA, SEQ, 0, False, NSLOT
            else:
                s = tix - 1
                nt, row0, kcol, own = 128, s * 128, NMETA + s * 128, s < NT_OWN
            hb = hbuf[pb]
            P.dma("sp", hb[:nt, :], h1_d[row0:row0 + nt, :], reads=[r_h1d[s]], writes=[r_hbuf[pb]])
            norm_transpose(hb[:nt, :], [r_hbuf[pb]], nt, xn2[pb], r_xn2[pb], 0, gT[:, 2, :], junk, xs_bf, stat, r_tmp)

            def mm_tok(e, bank, c0, nt=nt, pb=pb):
                ins = None
                for kc in range(16):
                    ins = e.matmul(psum[bank][:nt, :], xn2[pb][:, kc, :nt], w_lat[:, kc, c0:c0 + 512],
                                   start=(kc == 0), stop=(kc == 15))
                return ins

            def mm_feat(e, bank, c0, nt=nt, pb=pb):
                ins = None
                for kc in range(16):
                    ins = e.matmul(psum[bank][:ROPE, :nt], w_lat[:, kc, c0:c0 + ROPE], xn2[pb][:, kc, :nt],
                                   start=(kc == 0), stop=(kc == 15))
                return ins
            P.op("pe", lambda e, f=mm_tok: f(e, 1, 512), reads=[r_xn2[pb], r_wlat], writes=[psr[1]])
            P.op("pe", lambda e, f=mm_feat: f(e, 2, 1024), reads=[r_xn2[pb], r_wlat], writes=[psr[2]])
            P.op("pe", lambda e, f=mm_feat: f(e, 3, 1088), reads=[r_xn2[pb], r_wlat], writes=[psr[3]])
            if own:
                P.op("pe", lambda e, f=mm_tok: f(e, 0, 0), reads=[r_xn2[pb], r_wlat], writes=[psr[0]])
            P.op("dve", lambda e, nt=nt, kcol=kcol: e.tensor_tensor(out=t1[:, :nt], in0=psum[2][:ROPE, :nt],
                                                                  in1=cos_sb[:, kcol:kcol + nt], op=ALU.mult),
                 reads=[psr[2], r_tab], writes=[r_t12])
            P.op("dve", lambda e, nt=nt, kcol=kcol: e.tensor_tensor(out=t2[:, :nt], in0=psum[3][:ROPE, :nt],
                                                                  in1=sin_sb[:, kcol:kcol + nt], op=ALU.mult),
                 reads=[psr[3], r_tab], writes=[r_t12])
            P.op("dve", lambda e, nt=nt, pb=pb: e.tensor_tensor(out=kpe_t[pb][:, :nt], in0=t1[:, :nt], in1=t2[:, :nt],
                                                              op=ALU.add),
                 reads=[r_t12], writes=[r_kpe_t[pb]])
            P.dma("sp", kpeT_d[:, kcol:kcol + nt], kpe_t[pb][:, :nt], reads=[r_kpe_t[pb]], writes=[r_kpe_d[tix]])
            norm_transpose(psum[1][:nt, :], [psr[1]], nt, kvn_t[pb], r_kvn_t[pb], 0, latg[:, 4:8], junk2, xs_bf2,
                           stat2, r_tmp2, nkc=4)
            P.dma("sp", kvnT_d[:, :, kcol:kcol + nt], kvn_t[pb][:, :, :nt], reads=[r_kvn_t[pb]], writes=[r_kvn_d[tix]])
            if stop_after == "a2":
                for kc in range(4):
                    P.dma("pool", dbg[0:128, kc * 1024 + kcol:kc * 1024 + kcol + nt], kvn_t[pb][:, kc, :nt],
                          reads=[r_kvn_t[pb]], writes=[Res()])
                P.dma("pool", dbg[128:192, kcol:kcol + nt], kpe_t[pb][:, :nt], reads=[r_kpe_t[pb]], writes=[Res()])
                if own:
                    for kc in range(4):
                        P.dma("pool", dbg[256:384, kc * 1024 + s * 128:kc * 1024 + s * 128 + nt], cqn_t[pb][:, kc, :nt],
                              reads=[r_cqn_t[pb]], writes=[Res()])
            if own:
                norm_transpose(psum[0][:nt, :], [psr[0]], nt, cqn_t[pb], r_cqn_t[pb], 0, latg[:, 0:4], junk2, xs_bf2,
                               stat2, r_tmp2, nkc=4)
                P.dma("sp", cqnT_d[:, :, s * 128:(s + 1) * 128], cqn_t[pb][:, :, :], reads=[r_cqn_t[pb]],
                      writes=[r_cqn_d[s]])
        P.barrier()

    r_oT_d = [Res(f"oTd{h}") for h in range(NH)]
    if stop_after not in ("a1", "a2", "a2f") and "b" in phases:
      with contextlib.ExitStack() as ph:
        def sb_t(name, shape, dt):
            return ph.enter_context(nc.sbuf_tensor('b_' + name, shape, dt))
        NQ = NT_OWN * 128
        kvnT = sb_t("kvnT", [128, 4, LK], BF16)
        kpeT = sb_t("kpeT", [ROPE, LK], BF16)
        cqnT = sb_t("cqnT", [128, 4, NQ], BF16)
        cosq = sb_t("cosq", [ROPE, NQ], F32)
        sinq = sb_t("sinq", [ROPE, NQ], F32)
        tri = sb_t("tri", [128, 128], BF16)
        ones = sb_t("ones", [128, 128], BF16)
        sel = sb_t("sel", [128, 4], F32)
        r_lat = Res("lat")
        P.dma("sp", kvnT[:, :, :], kvnT_d[:, :, :], reads=r_kvn_d, writes=[r_lat])
        P.dma("sp", kpeT[:, :], kpeT_d[:, :], reads=r_kpe_d, writes=[r_lat], append=True)
        P.dma("sp", cqnT[:, :, :], cqnT_d[:, :, :], reads=r_cqn_d, writes=[r_lat], append=True)
        P.dma("sp", cosq[:, :], cosT[:, NMETA:NMETA + NQ], writes=[r_lat], append=True)
        P.dma("sp", sinq[:, :], sinT[:, NMETA:NMETA + NQ], writes=[r_lat], append=True)
        P.dma("sp", sel[:, :], sel_d[:, :], writes=[r_lat], append=True)
        r_msk = Res("msk")
        P.dma("pool", tri[:, :], tri_d[:, :], writes=[r_msk])
        P.op("dve", lambda e: e.memset(ones[:], 1.0), writes=[r_msk])
        hbufs = []
        for i in range(2):
            hbufs.append(dict(
                wkv=sb_t(f"wkv{i}", [128, 4, 256], BF16), wq=sb_t(f"wq{i}", [128, 4, 256], BF16),
                KT=sb_t(f"KT{i}", [128, LK], BF16), V=sb_t(f"V{i}", [128, NSLOT + 1, 128], BF16),
                qT=sb_t(f"qT{i}", [128, NQ], BF16), qpe=sb_t(f"qpe{i}", [ROPE, NQ], BF16),
                oT=sb_t(f"oT{i}", [128, NQ], BF16),
                r_w=Res(f"hw{i}"), r_KT=Res(f"KT{i}"), r_V=Res(f"V{i}"), r_qT=Res(f"qT{i}"), r_qpe=Res(f"qpe{i}"),
                r_oT=Res(f"oT{i}")))
        pT = [sb_t(f"pT{i}", [128, 512], BF16) for i in range(3)]
        r_pT = [Res(f"pT{i}") for i in range(3)]
        rec = sb_t("rec", [128, 512], F32)
        r_rec = Res("rec")
        rt1 = sb_t("rt1", [ROPE, 512], F32)
        rt2 = sb_t("rt2", [ROPE, 512], F32)
        r_rt = Res("rt")
        pj = [6, 7, 0, 1, 2, 3, 4]
        pjc = [0]

        def pjbank():
            b = pj[pjc[0] % len(pj)]
            pjc[0] += 1
            return b

        def proj(h, hb):
            B = hbufs[hb]
            wkv, wq = B["wkv"], B["wq"]
            P.dma("pool", wkv[:, :, :], wview(w_kv_b, 0, KVL, h * 256, 256), writes=[B["r_w"]])
            P.dma("pool", wq[:, :, 0:QKD], wview(w_q_b, 0, QL, h * QKD, QKD), writes=[B["r_w"]], append=True)
            P.op("dve", lambda e: e.tensor_scalar(out=wq[:, :, 192:224], in0=wq[:, :, 160:192], scalar1=-1.0,
                                                  scalar2=None, op0=ALU.mult), reads=[B["r_w"]], writes=[B["r_w"]])
            P.op("dve", lambda e: e.tensor_copy(out=wq[:, :, 224:256], in_=wq[:, :, 128:160]),
                 reads=[B["r_w"]], writes=[B["r_w"]])
            for b0 in range(0, LK, 512):
                bn = min(512, LK - b0)
                bk = pjbank()

                def mmk(e, bk=bk, b0=b0, bn=bn):
                    ins = None
                    for kc in range(4):
                        ins = e.matmul(psum[bk][:, :bn], wkv[:, kc, 0:128], kvnT[:, kc, b0:b0 + bn],
                                       start=(kc == 0), stop=(kc == 3))
                    return ins
                P.op("pe", mmk, reads=[B["r_w"], r_lat], writes=[psr[bk]])
                P.op("act", lambda e, bk=bk, b0=b0, bn=bn: e.activation(out=B["KT"][:, b0:b0 + bn],
                                                                       in_=psum[bk][:, :bn], func=AF.Copy),
                     reads=[psr[bk]], writes=[B["r_KT"]])
            for g0 in range(0, NSLOT + 1, 4):
                gn = min(4, NSLOT + 1 - g0)
                bk = pjbank()

                def mmv(e, bk=bk, g0=g0, gn=gn):
                    ins = None
                    for j in range(gn):
                        kt = g0 + j
                        nk = NMETA if kt == 0 else 128
                        kc0 = 0 if kt == 0 else NMETA + (kt - 1) * 128
                        for kc in range(4):
                            ins = e.matmul(psum[bk][:nk, j * 128:(j + 1) * 128], kvnT[:, kc, kc0:kc0 + nk],
                                           wkv[:, kc, 128:256], start=(kc == 0), stop=(kc == 3))
                    return ins
                P.op("pe", mmv, reads=[B["r_w"], r_lat], writes=[psr[bk]])
                if g0 == 0:
                    P.op("dve", lambda e, bk=bk: e.tensor_copy(out=B["V"][:NMETA, 0, :], in_=psum[bk][:NMETA, 0:128]),
                         reads=[psr[bk]], writes=[B["r_V"]])
                    P.op("dve", lambda e, bk=bk, gn=gn: e.tensor_copy(
                        out=B["V"][:, 1:gn, :], in_=psum[bk][:, 128:gn * 128].rearrange("p (k c) -> p k c", c=128)),
                        reads=[psr[bk]], writes=[B["r_V"]])
                else:
                    P.op("dve", lambda e, bk=bk, g0=g0, gn=gn: e.tensor_copy(
                        out=B["V"][:, g0:g0 + gn, :], in_=psum[bk][:, :gn * 128].rearrange("p (k c) -> p k c", c=128)),
                        reads=[psr[bk]], writes=[B["r_V"]])
            for qb in range(NQ // 512):
                bk = pjbank()

                def mmq(e, bk=bk, qb=qb):
                    ins = None
                    for kc in range(4):
                        ins = e.matmul(psum[bk][:, :], wq[:, kc, 0:128], cqnT[:, kc, qb * 512:(qb + 1) * 512],
                                       start=(kc == 0), stop=(kc == 3))
                    return ins
                P.op("pe", mmq, reads=[B["r_w"], r_lat], writes=[psr[bk]])
                P.op("act", lambda e, bk=bk, qb=qb: e.activation(out=B["qT"][:, qb * 512:(qb + 1) * 512],
                                                               in_=psum[bk][:, :], func=AF.Copy),
                     reads=[psr[bk]], writes=[B["r_qT"]])
                ba, bb = pjbank(), pjbank()

                def mmp(e, bank, c0, qb=qb):
                    ins = None
                    for kc in range(4):
                        ins = e.matmul(psum[bank][:ROPE, :], wq[:, kc, c0:c0 + ROPE], cqnT[:, kc, qb * 512:(qb + 1) * 512],
                                       start=(kc == 0), stop=(kc == 3))
                    return ins
                P.op("pe", lambda e, f=mmp, ba=ba: f(e, ba, 128), reads=[B["r_w"], r_lat], writes=[psr[ba]])
                P.op("pe", lambda e, f=mmp, bb=bb: f(e, bb, 192), reads=[B["r_w"], r_lat], writes=[psr[bb]])
                P.op("dve", lambda e, ba=ba, qb=qb: e.tensor_tensor(out=rt1[:, :], in0=psum[ba][:ROPE, :],
                                                                  in1=cosq[:, qb * 512:(qb + 1) * 512], op=ALU.mult),
                     reads=[psr[ba], r_lat], writes=[r_rt])
                P.op("dve", lambda e, bb=bb, qb=qb: e.tensor_tensor(out=rt2[:, :], in0=psum[bb][:ROPE, :],
                                                                  in1=sinq[:, qb * 512:(qb + 1) * 512], op=ALU.mult),
                     reads=[psr[bb], r_lat], writes=[r_rt])
                P.op("dve", lambda e, qb=qb: e.tensor_tensor(out=B["qpe"][:, qb * 512:(qb + 1) * 512], in0=rt1[:, :],
                                                            in1=rt2[:, :], op=ALU.add),
                     reads=[r_rt], writes=[B["r_qpe"]])

        sbank = [0]
        qcount = [0]

        def attn(h, hb):
            B = hbufs[hb]
            for qb in range(NQ // 512):
                tl = [(0, NMETA, 0, 0, False, False)]
                for ip in range(4 * qb + 4):
                    q0 = max(0, ip - 4 * qb) * 128
                    tl.append((1 + ip, 128, NMETA + ip * 128, q0, ip >= 4 * qb, False))
                    tl.append((1 + NT_OWN + ip, 128, NMETA + (NT_OWN + ip) * 128, q0, False, ip == 0))
                tl = [tl[0]] + [x for x in tl[1:] if x[3] > 0] + [x for x in tl[1:] if x[3] == 0]
                ob = 3 + (qcount[0] % 2)
                db = 5 + (qcount[0] % 2)
                qcount[0] += 1
                n = len(tl)
                sb_of = {}

                def qk(j):
                    kt, nk, kc0, q0, diag, b16 = tl[j]
                    s = sbank[0] % 3
                    sbank[0] += 1
                    sb_of[j] = s

                    def f(e, s=s, nk=nk, kc0=kc0, q0=q0, qb=qb):
                        e.matmul(psum[s][:nk, q0:512], B["KT"][:, kc0:kc0 + nk],
                                 B["qT"][:, qb * 512 + q0:(qb + 1) * 512], start=True, stop=False)
                        return e.matmul(psum[s][:nk, q0:512], kpeT[:, kc0:kc0 + nk],
                                        B["qpe"][:, qb * 512 + q0:(qb + 1) * 512], start=False, stop=True)
                    P.op("pe", f, reads=[B["r_KT"], B["r_qT"], B["r_qpe"], r_lat], writes=[psr[s]])
                    bias = sel[:nk, 2:3] if b16 else 0.0
                    P.op("act", lambda e, s=s, nk=nk, q0=q0, bias=bias: e.activation(
                        out=pT[s][:nk, q0:512], in_=psum[s][:nk, q0:512], func=AF.Exp, bias=bias, scale=SCALE),
                        reads=[psr[s], r_lat], writes=[r_pT[s]])
                    if diag:
                        P.op("dve", lambda e, s=s, q0=q0: e.tensor_tensor(out=pT[s][:, q0:q0 + 128],
                                                                        in0=pT[s][:, q0:q0 + 128], in1=tri[:, :],
                                                                        op=ALU.mult),
                             reads=[r_pT[s], r_msk], writes=[r_pT[s]])

                def pv(j):
                    kt, nk, kc0, q0, diag, b16 = tl[j]
                    s = sb_of[j]

                    def f(e, s=s, kt=kt, nk=nk, q0=q0, j=j, ob=ob, n=n, db=db):
                        e.matmul(psum[ob][:, q0:512], B["V"][:nk, kt, :], pT[s][:nk, q0:512],
                                 start=(j == 0), stop=(j == n - 1))
                        return e.matmul(psum[db][:, q0:512], ones[:nk, :], pT[s][:nk, q0:512],
                                        start=(j == 0), stop=(j == n - 1))
                    P.op("pe", f, reads=[B["r_V"], r_pT[s], r_msk], writes=[psr[ob], psr[db]])
                qk(0)
                qk(1)
                for j in range(n):
                    if j + 2 < n:
                        qk(j + 2)
                    pv(j)
                P.op("dve", lambda e, db=db: e.reciprocal(out=rec[:, :], in_=psum[db][:, :]), reads=[psr[db]],
                     writes=[r_rec])
                P.op("dve", lambda e, ob=ob, qb=qb: e.tensor_tensor(out=B["oT"][:, qb * 512:(qb + 1) * 512],
                                                                  in0=psum[ob][:, :], in1=rec[:, :], op=ALU.mult),
                     reads=[psr[ob], r_rec], writes=[B["r_oT"]])
            P.dma("sp", oT_d[:, h, :], B["oT"][:, :], reads=[B["r_oT"]], writes=[r_oT_d[h]])
            if stop_after == "b":
                P.dma("pool", dbg[h * 128:(h + 1) * 128, :], B["oT"][:, :], reads=[B["r_oT"]], writes=[Res()])

        nheads = NH if stop_after != "b" else (2 if only is None else 1)
        if debug is not None and len(debug) > 4:
            nheads = debug[4]
        import os
        if not os.environ.get("K_PIPE"):
            for h in range(nheads):
                proj(h, h % 2)
                attn(h, h % 2)
        else:
            proj(0, 0)
            for h in range(nheads):
                if h + 1 < nheads:
                    proj(h + 1, (h + 1) % 2)
                attn(h, h % 2)
        P.barrier()

    r_h2d = [Res(f"h2d{i}") for i in range(NT_OWN)]
    if stop_after not in ("a1", "a2", "a2f", "b") and "c1" in phases:
      with contextlib.ExitStack() as ph:
        def sb_t(name, shape, dt):
            return ph.enter_context(nc.sbuf_tensor('c1_' + name, shape, dt))
        xn2m = sb_t("xn2m", [128, 16, 512], BF16)
        r_xn2m = Res("xn2m")
        xn2h = sb_t("xn2h", [128, 16, 64], BF16)
        r_xn2h = Res("xn2h")
        hst = sb_t("hst", [128, D], F32)
        r_hst = Res("hst")
        halo = sb_t("halo", [NMETA, D], F32)
        r_halo = Res("halo")
        junk = sb_t("junk", [128, D], BF16)
        xs_bf = sb_t("xs_bf", [128, D], BF16)
        stat = sb_t("stat", [128, 4], F32)
        r_tmp = Res("tmp")
        stat2 = sb_t("stat2", [128, 4], F32)
        r_tmp2 = Res("tmp2")
        uext = sb_t("uext", [128, 2, 576], F32)
        r_uext = [Res("uext0"), Res("uext1")]
        ptmp = [sb_t(f"ptmp{i}", [128, 576], F32) for i in range(2)]
        r_ptmp = Res("ptmp")
        P.op("dve", lambda e: e.memset(ptmp[0][:, :], 0.0), writes=[r_ptmp])
        P.op("dve", lambda e: e.memset(ptmp[1][:, :], 0.0), writes=[r_ptmp])
        dext = sb_t("dext", [128, 2, 576], BF16)
        r_dext = Res("dext")
        ypT = sb_t("ypT", [128, 8, 512], BF16)
        r_ypT = Res("ypT")
        oTs = sb_t("oTs", [128, NH, 512], BF16)
        r_oTs = Res("oTs")
        yT = sb_t("yT", [128, 16, 512], BF16)
        r_yT = Res("yT")
        wst = []
        for i in range(2):
            wst.append(dict(po=sb_t(f"w_po{i}", [128, 8, 128], BF16), gp=sb_t(f"w_gp{i}", [128, 16, 128], BF16),
                            mo=sb_t(f"w_mo{i}", [128, 16, 128], BF16), gm=sb_t(f"w_gm{i}", [128, 16, 128], BF16),
                            r=Res(f"wst{i}")))
        wup = [sb_t(f"wup{i}", [128, 16, 256], BF16) for i in range(2)]
        r_wup = [Res("wup0"), Res("wup1")]
        wo = [sb_t(f"wo{i}", [128, 16, 256], BF16) for i in range(2)]
        r_wo = [Res("wo0"), Res("wo1")]
        sgt = [sb_t(f"sgt{i}", [128, 512], F32) for i in range(4)]
        r_sgt = [Res(f"sgt{i}") for i in range(4)]
        m_sb = [sb_t(f"m_sb{i}", [128, D], F32) for i in range(4)]
        r_m = [Res(f"m{i}") for i in range(4)]
        growm = sb_t("growm", [128, D], F32)
        pw = sb_t("pw", [128, 4, 2, 256], BF16)
        psc = sb_t("psc", [128, 8], F32)
        selc = sb_t("selc", [128, 4], F32)
        r_cc = Res("cc")
        P.dma("sp", growm[:, :], gains[3:4, :].partition_broadcast(128), writes=[r_cc])
        P.dma("sp", psc[:, :], pscT[:, :], writes=[r_cc], append=True)
        P.dma("sp", selc[:, :], sel_d[:, :], writes=[r_cc], append=True)
        P.dma("pool", pw[:, :, :, :], pool_w.rearrange("g (k p) o -> p g k o", p=128), writes=[r_cc], append=True)
        stc = [0]
        npass_c = 4 if stop_after != "c1" else 1
        for ps_i in range(npass_c):
            own_tiles = list(range(ps_i * 4, ps_i * 4 + 4))
            P.dma("sp", oTs[:, :, :], oT_d[:, :, ps_i * 512:(ps_i + 1) * 512], reads=r_oT_d, writes=[r_oTs])
            for t, i in enumerate(own_tiles):
                P.dma("sp", m_sb[t][:, :], h1_d[i * 128:(i + 1) * 128, :], reads=[r_h1d[i]], writes=[r_m[t]])
                norm_transpose(m_sb[t][:, :], [r_m[t]], 128, xn2m, r_xn2m, t * 128, gT[:, 2, :], junk, xs_bf, stat, r_tmp)
                if i == 0:
                    P.dma("sp", hst[:NMETA, :], h1_d[SEQ:SEQ + NMETA, :], reads=[r_h1d[NSLOT]], writes=[r_hst])
                    P.dma("sp", halo[:, :], h1_d[NT_OWN * 128 + 112:NT_OWN * 128 + 128, :], reads=[r_h1d[NT_OWN]],
                          writes=[r_halo])
                    P.op("dve", lambda e: e.tensor_scalar(out=hst[:NMETA, :], in0=hst[:NMETA, :],
                                                          scalar1=selc[:NMETA, 0:1], scalar2=None, op0=ALU.mult),
                         reads=[r_hst, r_cc], writes=[r_hst])
                    P.op("dve", lambda e: e.scalar_tensor_tensor(out=halo[:, :], in0=halo[:, :],
                                                                 scalar=selc[:NMETA, 1:2], in1=hst[:NMETA, :],
                                                                 op0=ALU.mult, op1=ALU.add),
                         reads=[r_hst, r_halo, r_cc], writes=[r_halo])
                else:
                    sl = NT_OWN + i
                    P.dma("sp", halo[:, :], h1_d[sl * 128 + 112:sl * 128 + 128, :], reads=[r_h1d[sl]], writes=[r_halo])
                norm_transpose(halo[:, :], [r_halo], NMETA, xn2h, r_xn2h, t * 16, gT[:, 2, :], junk, xs_bf, stat, r_tmp)
            for g in range(4):
                wi = stc[0] % 2
                stc[0] += 1
                P.dma("pool", wup[wi][:, :, :], wview(w_in, 0, D, g * 256, 256), writes=[r_wup[wi]])
                nsteps = g + 1
                for sub in range(2):
                    def mmu(e, bank, rhs, n, sub=sub, wi=wi):
                        ins = None
                        for kc in range(16):
                            ins = e.matmul(psum[bank][:, :n], wup[wi][:, kc, sub * 128:(sub + 1) * 128], rhs(kc),
                                           start=(kc == 0), stop=(kc == 15))
                        return ins
                    ba, bb = (0, 1) if sub == 0 else (2, 3)
                    P.op("pe", lambda e, f=mmu, ba=ba: f(e, ba, lambda kc: xn2m[:, kc, :], 512),
                         reads=[r_wup[wi], r_xn2m], writes=[psr[ba]])
                    P.op("pe", lambda e, f=mmu, bb=bb: f(e, bb, lambda kc: xn2h[:, kc, :], 64),
                         reads=[r_wup[wi], r_xn2h], writes=[psr[bb]])
                    uv = uext[:, sub, :].rearrange("p (s c) -> p s c", c=144)
                    P.op("act", lambda e, ba=ba, uv=uv: e.activation(
                        out=uv[:, :, 16:144], in_=psum[ba][:, :].rearrange("p (s c) -> p s c", c=128), func=AF.Copy),
                        reads=[psr[ba]], writes=[r_uext[sub]])
                    P.op("act", lambda e, bb=bb, uv=uv: e.activation(
                        out=uv[:, :, 0:16], in_=psum[bb][:, :64].rearrange("p (s c) -> p s c", c=16), func=AF.Copy),
                        reads=[psr[bb], r_uext[sub]], writes=[r_uext[sub]])
                    cur = uext[:, sub, :]
                    rcur = r_uext[sub]
                    for k in range(nsteps):
                        sh = 1 << k
                        nx = ptmp[k % 2]
                        P.op("dve", lambda e, cur=cur, nx=nx, sh=sh: e.tensor_tensor(
                            out=nx[:, sh:576], in0=cur[:, sh:576], in1=cur[:, 0:576 - sh], op=ALU.add),
                            reads=[rcur, r_ptmp], writes=[r_ptmp])
                        cur = nx[:, :]
                        rcur = r_ptmp
                    wnd = float(1 << nsteps)
                    P.op("dve", lambda e, cur=cur, sub=sub, wnd=wnd: e.scalar_tensor_tensor(
                        out=dext[:, sub, 16:576], in0=cur[:, 16:576], scalar=1.0 / wnd, in1=uext[:, sub, 16:576],
                        op0=ALU.mult, op1=ALU.subtract),
                        reads=[r_ptmp, r_uext[sub]], writes=[r_dext])
                P.op("dve", lambda e: e.memset(dext[:, :, 0:16], 0.0), reads=[], writes=[r_dext])
                for oc2 in range(2):
                    for blk in range(2):
                        bk = 4 + (2 * oc2 + blk) % 4

                        def mmpw(e, bk=bk, oc2=oc2, blk=blk, g=g):
                            ins = None
                            for k2 in range(2):
                                ins = e.matmul(psum[bk][:, :288], pw[:, g, k2, oc2 * 128:(oc2 + 1) * 128],
                                               dext[:, k2, blk * 288:(blk + 1) * 288], start=(k2 == 0), stop=(k2 == 1))
                            return ins
                        P.op("pe", mmpw, reads=[r_dext, r_cc], writes=[psr[bk]])
                        c = 2 * g + oc2
                        P.op("act", lambda e, bk=bk, c=c, blk=blk: e.activation(
                            out=ypT[:, c, blk * 256:(blk + 1) * 256].rearrange("p (s c) -> p s c", c=128),
                            in_=psum[bk][:, :288].rearrange("p (s c) -> p s c", c=144)[:, :, 16:144],
                            func=AF.Copy, scale=psc[:, c:c + 1]),
                            reads=[psr[bk], r_cc], writes=[r_ypT])
            for oc in range(16):
                wi = stc[0] % 2
                stc[0] += 1
                W = wst[wi]
                P.dma("pool", W["po"][:, :, :], wview(w_pool_o, 0, POOLW, oc * 128, 128), writes=[W["r"]])
                P.dma("pool", W["gp"][:, :, :], wview(w_in, 0, D, 2112 + oc * 128, 128), writes=[W["r"]], append=True)
                P.dma("pool", W["mo"][:, :, :], wview(w_mla_o, 0, D, oc * 128, 128), writes=[W["r"]], append=True)
                P.dma("pool", W["gm"][:, :, :], wview(w_in, 0, D, 4160 + oc * 128, 128), writes=[W["r"]], append=True)
                bs = (0, 1, 2, 3) if oc % 2 == 0 else (4, 5, 6, 7)

                def mmc(e, bank, wt, rhs, nk):
                    ins = None
                    for kc in range(nk):
                        ins = e.matmul(psum[bank][:, :], wt[:, kc, :], rhs(kc), start=(kc == 0), stop=(kc == nk - 1))
                    return ins
                P.op("pe", lambda e, f=mmc, W=W, b=bs[0]: f(e, b, W["po"], lambda kc: ypT[:, kc, :], 8),
                     reads=[W["r"], r_ypT], writes=[psr[bs[0]]])
                P.op("pe", lambda e, f=mmc, W=W, b=bs[1]: f(e, b, W["gp"], lambda kc: xn2m[:, kc, :], 16),
                     reads=[W["r"], r_xn2m], writes=[psr[bs[1]]])
                P.op("pe", lambda e, f=mmc, W=W, b=bs[2]: f(e, b, W["mo"], lambda kc: oTs[:, kc, :], 16),
                     reads=[W["r"], r_oTs], writes=[psr[bs[2]]])
                P.op("pe", lambda e, f=mmc, W=W, b=bs[3]: f(e, b, W["gm"], lambda kc: xn2m[:, kc, :], 16),
                     reads=[W["r"], r_xn2m], writes=[psr[bs[3]]])
                P.op("act", lambda e, b=bs[1]: e.activation(out=sgt[0][:, :], in_=psum[b][:, :], func=AF.Sigmoid),
                     reads=[psr[bs[1]]], writes=[r_sgt[0]])
                P.op("act", lambda e, b=bs[3]: e.activation(out=sgt[1][:, :], in_=psum[b][:, :], func=AF.Sigmoid),
                     reads=[psr[bs[3]]], writes=[r_sgt[1]])
                P.op("dve", lambda e, b=bs[0]: e.tensor_tensor(out=sgt[2][:, :], in0=psum[b][:, :], in1=sgt[0][:, :],
                                                              op=ALU.mult),
                     reads=[psr[bs[0]], r_sgt[0]], writes=[r_sgt[2]])
                P.op("dve", lambda e, b=bs[2]: e.tensor_tensor(out=sgt[3][:, :], in0=psum[b][:, :], in1=sgt[1][:, :],
                                                              op=ALU.mult),
                     reads=[psr[bs[2]], r_sgt[1]], writes=[r_sgt[3]])
                P.op("dve", lambda e, oc=oc: e.tensor_tensor(out=yT[:, oc, :], in0=sgt[2][:, :], in1=sgt[3][:, :],
                                                            op=ALU.add),
                     reads=[r_sgt[2], r_sgt[3]], writes=[r_yT])
            for e8 in range(8):
                wi = stc[0] % 2
                stc[0] += 1
                P.dma("pool", wo[wi][:, :, :], wview(w_out, 0, D, e8 * 256, 256), writes=[r_wo[wi]])
                for t in range(4):
                    bk = (e8 * 4 + t) % 8

                    def mmo(e, bk=bk, t=t, wi=wi):
                        ins = None
                        for kc in range(16):
                            ins = e.matmul(psum[bk][:, :256], yT[:, kc, t * 128:(t + 1) * 128], wo[wi][:, kc, :],
                                           start=(kc == 0), stop=(kc == 15))
                        return ins
                    P.op("pe", mmo, reads=[r_yT, r_wo[wi]], writes=[psr[bk]])
                    P.op("act", lambda e, bk=bk, t=t, e8=e8: e.activation(out=m_sb[t][:, e8 * 256:(e8 + 1) * 256],
                                                                         in_=psum[bk][:, :256], func=AF.Copy),
                         reads=[psr[bk]], writes=[r_m[t]])
            for t, i in enumerate(own_tiles):
                P.op("act", lambda e, t=t: e.activation(out=junk[:, :], in_=m_sb[t][:, :], func=AF.Square,
                                                        accum_out=stat2[:, 0:1]),
                     reads=[r_m[t]], writes=[r_tmp2])
                P.op("act", lambda e: e.activation(out=stat2[:, 1:2], in_=stat2[:, 0:1], func=AF.Sqrt, scale=1.0 / D,
                                                   bias=epsb[:, 0:1]),
                     reads=[r_tmp2, r_const], writes=[r_tmp2])
                P.op("dve", lambda e: e.reciprocal(out=stat2[:, 2:3], in_=stat2[:, 1:2]), reads=[r_tmp2], writes=[r_tmp2])
                P.dma("sp", hst[:, :], h1_d[i * 128:(i + 1) * 128, :], reads=[r_h1d[i]], writes=[r_hst])
                P.op("dve", lambda e, t=t: e.scalar_tensor_tensor(out=m_sb[t][:, :], in0=m_sb[t][:, :],
                                                                 scalar=stat2[:, 2:3], in1=growm[:, :],
                                                                 op0=ALU.mult, op1=ALU.mult),
                     reads=[r_tmp2, r_cc, r_m[t]], writes=[r_m[t]])
                P.op("dve", lambda e, t=t: e.tensor_tensor(out=m_sb[t][:, :], in0=m_sb[t][:, :], in1=hst[:, :], op=ALU.add),
                     reads=[r_m[t], r_hst], writes=[r_m[t]])
                P.dma("sp", h2_d[i * 128:(i + 1) * 128, :], m_sb[t][:, :], reads=[r_m[t]], writes=[r_h2d[i]])
                if stop_after == "c1":
                    dump(m_sb[t][:, :], [r_m[t]], t * 128, 128, 0, D)
        P.barrier()

    if stop_after == "all" and "c2" in phases:
      with contextlib.ExitStack() as ph:
        sb = alloc_ffn(ph, 6, 'c2_', T=768)
        load_grow(sb, 5)
        for own_ in ([0, 1, 2, 3, 4, 5], [6, 7, 8, 9, 10], [11, 12, 13, 14, 15]):
            tiles = []
            for i in own_:
                tiles.append(dict(nt=128, slot=i, load=(lambda dst, rd, i=i: P.dma(
                    "sp", dst, h2_d[i * 128:(i + 1) * 128, :], reads=[r_h2d[i]], writes=rd))))

            def done2(ti, y_ap, ry, tiles=tiles):
                i = tiles[ti]["slot"]
                P.dma("sp", out_d[i * 128:(i + 1) * 128, :], y_ap, reads=ry, writes=[Res()])
            for t in tiles:
                t["done"] = done2
            ffn_pass(1, tiles, sb)
        P.barrier()

    P.barrier()
    with nc.Block() as block:
        P.emit(block)
    es.close()
    return nc


def _prep_inputs(inputs):
    x = np.asarray(inputs["x"], dtype=np.float32)
    B = x.shape[0]
    gains = np.stack([np.asarray(inputs[k], np.float32)[0] for k in
                      ("norm_ffn1_pre", "norm_ffn1_post", "norm_mix_pre", "norm_mix_post",
                       "norm_ffn2_pre", "norm_ffn2_post")], axis=0)
    common = {
        "meta": np.ascontiguousarray(inputs["meta_tokens"], dtype=np.float32),
        "gains": np.ascontiguousarray(gains),
        "gainsT": np.ascontiguousarray(gains.reshape(6, 16, 128).transpose(2, 0, 1).reshape(128, 96)),
        "ffn1_w_gu": np.asarray(inputs["ffn1_w_gu"], np.float32)[0],
        "ffn2_w_gu": np.asarray(inputs["ffn2_w_gu"], np.float32)[0],
        "ffn1_w_down": np.asarray(inputs["ffn1_w_down"], np.float32)[0],
        "ffn2_w_down": np.asarray(inputs["ffn2_w_down"], np.float32)[0],
        "w_in": np.asarray(inputs["w_in"], np.float32)[0],
        "pool_w": np.asarray(inputs["pool_w"], np.float32)[0],
        "pool_scale": np.asarray(inputs["pool_scale"], np.float32)[0],
        "w_pool_o": np.asarray(inputs["w_pool_o"], np.float32)[0],
        "q_a_norm": np.asarray(inputs["q_a_norm"], np.float32)[0],
        "w_q_b": np.asarray(inputs["w_q_b"], np.float32)[0],
        "kv_a_norm": np.asarray(inputs["kv_a_norm"], np.float32)[0],
        "w_kv_b": np.asarray(inputs["w_kv_b"], np.float32)[0],
        "w_mla_o": np.asarray(inputs["w_mla_o"], np.float32)[0],
        "w_out": np.asarray(inputs["w_out"], np.float32)[0],
        "ident": np.eye(128, dtype=np.float32),
        "latgT": np.ascontiguousarray(np.concatenate([
            np.asarray(inputs["q_a_norm"], np.float32)[0].reshape(4, 128).T,
            np.asarray(inputs["kv_a_norm"], np.float32)[0].reshape(4, 128).T], axis=1)),
        "pscT": np.ascontiguousarray(np.asarray(inputs["pool_scale"], np.float32)[0].reshape(8, 128).T),
        "tri": np.triu(np.ones((128, 128), np.float32)),
    }
    inv = (10000.0 ** (-np.arange(0, ROPE, 2, dtype=np.float32) / ROPE)).astype(np.float32)
    in_maps = []
    tile_maps = []
    for core in range(8):
        b, c = core // 2, core % 2
        own = [2 * i + c for i in range(16)]
        if c == 1:
            oth = [2 * i for i in range(16)]
        else:
            oth = [31] + [2 * i - 1 for i in range(1, 16)]
        order = own + oth
        xt = x[b].reshape(32, 128, D)[order].reshape(SEQ, D)
        pos = np.concatenate([np.arange(NMETA)] + [NMETA + j * 128 + np.arange(128) for j in order]).astype(np.float32)
        ang = pos[None, :] * np.concatenate([inv, inv])[:, None]
        sel = np.zeros((128, 4), np.float32)
        if c == 0:
            sel[:, 0] = 1.0
            sel[:, 2] = -30000.0
        else:
            sel[:, 1] = 1.0
        m = dict(common)
        m["xs"] = np.ascontiguousarray(xt)
        m["cosT"] = np.cos(ang).astype(np.float32)
        m["sinT"] = np.sin(ang).astype(np.float32)
        m["sel"] = sel
        in_maps.append(m)
        tile_maps.append(own)
    return in_maps, tile_maps


_NC_CACHE = {}


def kernel(**inputs):
    in_maps, tile_maps = _prep_inputs(inputs)
    if "nc" not in _NC_CACHE:
        _NC_CACHE["nc"] = build_program()
    nc = _NC_CACHE["nc"]
    res = run_bass_kernel_spmd(nc, in_maps, core_ids=list(range(8)))
    out = np.empty((4, SEQ, D), np.float32)
    for core in range(8):
        b = core // 2
        o = res.results[core]["out"].reshape(16, 128, D)
        for i, j in enumerate(tile_maps[core]):
            out[b, j * 128:(j + 1) * 128] = o[i]
    return out
```

```python
import contextlib
import numpy as np
import concourse.bass as bass
import concourse.mybir as mybir
from concourse.bass_utils import run_bass_kernel_spmd

F32 = mybir.dt.float32
BF16 = mybir.dt.bfloat16
AF = mybir.ActivationFunctionType
ALU = mybir.AluOpType

D = 2048
SEQ = 4096
NMETA = 16
DFF = 5632
NFC = DFF // 128
QL = 512
KVL = 512
ROPE = 64
NOPE = 128
VD = 128
NH = 16
QKD = NOPE + ROPE
POOLW = 1024
IN_COLS = 6208
EPS = 1e-6
SCALE = QKD ** -0.5
NT_OWN = 16
NSLOT = 32
LK = NMETA + SEQ
ENGS = ("pe", "act", "dve", "pool", "sp")


class Res:
    __slots__ = ("name", "w_eng", "w_dma", "r_eng", "r_dma")

    def __init__(self, name=""):
        self.name = name
        self.w_eng = {}
        self.w_dma = []
        self.r_eng = {}
        self.r_dma = []


class Prog:
    NRING = 12

    def __init__(self, nc, es):
        self.nc = nc
        self.lists = {e: [] for e in ENGS}
        self.cnt = {e: 0 for e in ENGS}
        self.known = {e: {e2: 0 for e2 in ENGS} for e in ENGS}
        self.kdma = {e: {} for e in ENGS}
        self.snaps = {e: [None] for e in ENGS}
        self.sem = {e: es.enter_context(nc.semaphore("c_" + e)) for e in ENGS}
        self.ring = {}
        self.ring_val = {}
        self.ring_pos = {}
        for q in ("sp", "pool"):
            self.ring[q] = [es.enter_context(nc.semaphore(f"d_{q}{i}")) for i in range(self.NRING)]
            self.ring_val[q] = [0] * self.NRING
            self.ring_pos[q] = 0

    def _deps(self, eng, reads, writes):
        waits_e = {}
        waits_d = {}

        def need_e(e2, idx, raw):
            if e2 == eng and eng == "pe":
                return
            if self.known[eng][e2] >= idx:
                return
            if waits_e.get(e2, 0) < idx:
                waits_e[e2] = idx

        def need_d(tok):
            key, val = tok
            if self.kdma[eng].get(key, 0) >= val:
                return
            if waits_d.get(key, 0) < val:
                waits_d[key] = val

        for r in reads:
            for e2, idx in r.w_eng.items():
                need_e(e2, idx, True)
            for t in r.w_dma:
                need_d(t)
        for w in writes:
            for e2, idx in w.w_eng.items():
                need_e(e2, idx, False)
            for t in w.w_dma:
                need_d(t)
            for e2, idx in w.r_eng.items():
                need_e(e2, idx, False)
            for t in w.r_dma:
                need_d(t)
        out = []
        for e2, idx in waits_e.items():
            out.append((self.sem[e2], idx))
            kn = self.known[eng]
            if kn[e2] < idx:
                kn[e2] = idx
            sn = self.snaps[e2][idx]
            if sn is not None:
                for e3, v in zip(ENGS, sn):
                    if kn[e3] < v:
                        kn[e3] = v
        for key, val in waits_d.items():
            out.append((key, val))
            self.kdma[eng][key] = val
        return out

    def op(self, eng, fn, reads=(), writes=()):
        waits = self._deps(eng, reads, writes)
        self.cnt[eng] += 1
        idx = self.cnt[eng]
        self.snaps[eng].append(tuple(self.known[eng][e] for e in ENGS))
        self.lists[eng].append((waits, fn, self.sem[eng], 1))
        for w in writes:
            w.w_eng = {eng: idx}
            w.w_dma = []
            w.r_eng = {}
            w.r_dma = []
        for r in reads:
            if r.r_eng.get(eng, 0) < idx:
                r.r_eng[eng] = idx
        return idx

    def dma(self, q, out, in_, reads=(), writes=(), append=False):
        waits = [] if append else self._deps(q, reads, writes)
        pos = self.ring_pos[q]
        self.ring_pos[q] = (pos + 1) % self.NRING
        sem = self.ring[q][pos]
        prev = self.ring_val[q][pos]
        if prev and self.kdma[q].get(sem, 0) < prev:
            waits.append((sem, prev))
            self.kdma[q][sem] = prev
        val = prev + 16
        self.ring_val[q][pos] = val
        tok = (sem, val)
        self.lists[q].append((waits, lambda e: e.dma_start(out=out, in_=in_), sem, 16))
        for w in writes:
            if append:
                w.w_dma.append(tok)
                continue
            w.w_eng = {}
            w.w_dma = [tok]
            w.r_eng = {}
            w.r_dma = []
        for r in reads:
            r.r_dma.append(tok)
        return tok

    def barrier(self):
        tgt = dict(self.cnt)
        dm = []
        for q in self.ring:
            for s, v in zip(self.ring[q], self.ring_val[q]):
                if v:
                    dm.append((s, v))
        for e in ENGS:
            waits = []
            for e2 in ENGS:
                if e2 != e and self.known[e][e2] < tgt[e2]:
                    waits.append((self.sem[e2], tgt[e2]))
                    self.known[e][e2] = tgt[e2]
            for s, v in dm:
                if self.kdma[e].get(s, 0) < v:
                    waits.append((s, v))
                    self.kdma[e][s] = v
            if waits:
                self.lists[e].append((waits, None, None, 0))

    def emit(self, block):
        def mk(e):
            lst = self.lists[e]

            def body(eng):
                for waits, fn, sem, inc in lst:
                    for s, v in waits:
                        eng.wait_ge(s, v)
                    if fn is not None:
                        ins = fn(eng)
                        ins.then_inc(sem, inc)
            return body
        block.tensor(mk("pe"))
        block.scalar(mk("act"))
        block.vector(mk("dve"))
        block.gpsimd(mk("pool"))
        block.sync(mk("sp"))


def build_program(debug=None):
    nc = bass.Bass("TRN2", target_bir_lowering=False)
    es = contextlib.ExitStack()

    def din(name, shape, dt=F32):
        return nc.dram_tensor(name, list(shape), dt, kind="ExternalInput").ap()

    xs = din("xs", [SEQ, D])
    meta = din("meta", [NMETA, D])
    gains = din("gains", [6, D])
    gainsT = din("gainsT", [128, 6 * 16])
    w_gu = [din("ffn1_w_gu", [D, 2 * DFF]), din("ffn2_w_gu", [D, 2 * DFF])]
    w_dn = [din("ffn1_w_down", [DFF, D]), din("ffn2_w_down", [DFF, D])]
    w_in = din("w_in", [D, IN_COLS])
    pool_w = din("pool_w", [4, 256, 256])
    pool_scale = din("pool_scale", [POOLW])
    w_pool_o = din("w_pool_o", [POOLW, D])
    q_a_norm = din("q_a_norm", [QL])
    w_q_b = din("w_q_b", [QL, NH * QKD])
    kv_a_norm = din("kv_a_norm", [KVL])
    w_kv_b = din("w_kv_b", [KVL, NH * (NOPE + VD)])
    w_mla_o = din("w_mla_o", [NH * VD, D])
    w_out = din("w_out", [D, D])
    cosT = din("cosT", [ROPE, LK])
    sinT = din("sinT", [ROPE, LK])
    ident_d = din("ident", [128, 128])
    tri_d = din("tri", [128, 128])
    sel_d = din("sel", [128, 4])
    latgT = din("latgT", [128, 8])
    pscT = din("pscT", [128, 8])
    out_d = nc.dram_tensor("out", [NT_OWN * 128, D], F32, kind="ExternalOutput").ap()

    h1_d = nc.dram_tensor("h1_d", [SEQ + NMETA, D], F32).ap()
    kvnT_d = nc.dram_tensor("kvnT_d", [128, 4, LK], BF16).ap()
    kpeT_d = nc.dram_tensor("kpeT_d", [ROPE, LK], BF16).ap()
    cqnT_d = nc.dram_tensor("cqnT_d", [128, 4, NT_OWN * 128], BF16).ap()
    oT_d = nc.dram_tensor("oT_d", [128, NH, NT_OWN * 128], BF16).ap()
    h2_d = nc.dram_tensor("h2_d", [NT_OWN * 128, D], F32).ap()
    dbg = None
    if debug is not None:
        dbg = nc.dram_tensor("dbg", list(debug[:2]), F32, kind="ExternalOutput").ap()

    P = Prog(nc, es)
    stop_after = debug[2] if (debug is not None and len(debug) > 2) else "all"
    only = debug[3] if (debug is not None and len(debug) > 3) else None
    import os
    phases = set((os.environ.get("K_PHASES") or "a1,a2,b,c1,c2").split(","))
    psum = [es.enter_context(nc.psum_tensor(f"ps{i}", [128, 512], F32)) for i in range(8)]
    psr = [Res(f"ps{i}") for i in range(8)]

    ident = es.enter_context(nc.sbuf_tensor("identb", [128, 128], BF16))
    gT = es.enter_context(nc.sbuf_tensor("gT", [128, 6, 16], F32))
    r_const = Res("const")
    epsb = es.enter_context(nc.sbuf_tensor("epsb", [128, 1], F32))
    P.op("dve", lambda e: e.memset(epsb[:], EPS), writes=[r_const])
    P.dma("pool", ident[:], ident_d[:, :], writes=[r_const])
    P.dma("sp", gT[:].rearrange("p g k -> p (g k)"), gainsT[:, :], writes=[r_const])

    def wview(w, r0, nr, c0, ncol):
        return w[r0:r0 + nr, c0:c0 + ncol].rearrange("(kc p) f -> p kc f", p=128)

    def norm_transpose(src, r_src, nt, xnT, r_xnT, col0, gsrc, sq_junk, xs_bf, stat, r_tmp, nkc=16):
        width = nkc * 128
        P.op("act", lambda e: e.activation(out=sq_junk[:nt, :width], in_=src, func=AF.Square,
                                           accum_out=stat[:nt, 0:1]),
             reads=list(r_src), writes=[r_tmp])
        P.op("act", lambda e: e.activation(out=stat[:nt, 1:2], in_=stat[:nt, 0:1], func=AF.Sqrt,
                                           scale=1.0 / width, bias=epsb[:nt, 0:1]),
             reads=[r_tmp, r_const], writes=[r_tmp])
        P.op("dve", lambda e: e.reciprocal(out=stat[:nt, 2:3], in_=stat[:nt, 1:2]),
             reads=[r_tmp], writes=[r_tmp])
        P.op("act", lambda e: e.activation(out=xs_bf[:nt, :width], in_=src, func=AF.Copy,
                                           scale=stat[:nt, 2:3]),
             reads=list(r_src) + [r_tmp], writes=[r_tmp])
        for k0 in range(0, nkc, 8):
            kn = min(8, nkc - k0)
            pb = tr_banks[tr_state[0] % len(tr_banks)]
            tr_state[0] += 1
            pv = psum[pb][:].bitcast(BF16)

            def tr(e, k0=k0, kn=kn, pv=pv):
                ins = None
                for k in range(kn):
                    ins = e.transpose(pv[:, k * 128:k * 128 + nt], xs_bf[:nt, (k0 + k) * 128:(k0 + k + 1) * 128],
                                      ident[:nt, :nt])
                return ins
            P.op("pe", tr, reads=[r_tmp, r_const], writes=[psr[pb]])
            P.op("dve", lambda e, k0=k0, kn=kn, pv=pv: e.tensor_tensor(
                out=xnT[:, k0:k0 + kn, col0:col0 + nt],
                in0=pv[:, :kn * 128].rearrange("p (k t) -> p k t", k=kn)[:, :, :nt],
                in1=gsrc[:, k0:k0 + kn].unsqueeze(2).to_broadcast([128, kn, nt]),
                op=ALU.mult),
                reads=[psr[pb], r_const], writes=[r_xnT])

    tr_banks = [6, 7]
    tr_state = [0]

    GROUPS = [(0, 12), (12, 12), (24, 12), (36, 8)]

    def alloc_ffn(ph, ntile, pfx, T=528, nxst=1):
        def sb_t(name, shape, dt):
            return ph.enter_context(nc.sbuf_tensor(pfx + name, shape, dt))
        sb = {}
        sb["xnT"] = sb_t("xnT", [128, 16, T], BF16)
        sb["r_xnT"] = Res("xnT")
        sb["actT"] = [sb_t(f"actT{i}", [128, 12, T], BF16) for i in range(2)]
        sb["r_act"] = [Res("act0"), Res("act1")]
        sb["ysb"] = [sb_t(f"ysb{i}", [128, D], F32) for i in range(ntile)]
        sb["r_ysb"] = [[Res(f"y{i}q{q}") for q in range(4)] for i in range(ntile)]
        sb["wg"] = [sb_t(f"wg{i}", [128, 16, 256], BF16) for i in range(2)]
        sb["wu"] = [sb_t(f"wu{i}", [128, 16, 256], BF16) for i in range(2)]
        sb["r_wgu"] = [Res(f"wgu{i}") for i in range(2)]
        sb["wd"] = [sb_t(f"wd{i}", [128, 12, 512], BF16) for i in range(2)]
        sb["r_wd"] = [Res("wd0"), Res("wd1")]
        sb["sg"] = [sb_t(f"sg{i}", [128, 512], F32) for i in range(2)]
        sb["r_sg"] = [Res("sg0"), Res("sg1")]
        sb["junk"] = sb_t("junk", [128, D], BF16)
        sb["xs_bf"] = sb_t("xs_bf", [128, D], BF16)
        sb["stat"] = sb_t("stat", [128, 4], F32)
        sb["stat2"] = sb_t("stat2", [128, 4], F32)
        sb["r_tmp"] = Res("tmp")
        sb["r_tmp2"] = Res("tmp2")
        sb["grow"] = sb_t("grow", [128, D], F32)
        sb["r_grow"] = Res("grow")
        sb["xst"] = [sb_t(f"xst{i}", [128, D], F32) for i in range(nxst)]
        sb["r_xst"] = [Res(f"xst{i}") for i in range(nxst)]
        sb["cnt"] = dict(gu=0, bank=0, sg=0, wd=0, dn=0, xst=0)
        sb["gu_banks"] = [(0, 1), (2, 3)]
        sb["dn_banks"] = [4, 5, 6, 7]
        return sb

    def load_grow(sb, g_post):
        P.dma("sp", sb["grow"][:, :], gains[g_post:g_post + 1, :].partition_broadcast(128), writes=[sb["r_grow"]])
        P.op("dve", lambda e: e.tensor_scalar(out=sb["grow"][:, :], in0=sb["grow"][:, :], scalar1=0.5, scalar2=None,
                                              op0=ALU.mult), reads=[sb["r_grow"]], writes=[sb["r_grow"]])

    def ffn_pass(which, tiles, sb):
        wgu, wdn = w_gu[which], w_dn[which]
        g_pre = 0 if which == 0 else 4
        xnT, actT, ysb = sb["xnT"], sb["actT"], sb["ysb"]
        r_xnT = sb["r_xnT"]
        cnt = sb["cnt"]
        T = sum(t["nt"] for t in tiles)
        cols = []
        c = 0
        for t in tiles:
            cols.append(c)
            c += t["nt"]
        for ti, (t, c0) in enumerate(zip(tiles, cols)):
            nt = t["nt"]
            if t.get("x_ap") is None:
                t["load"](ysb[ti][:nt, :], sb["r_ysb"][ti])
                src, rs = ysb[ti][:nt, :], sb["r_ysb"][ti]
            else:
                src, rs = t["x_ap"], [t["r_x"]]
            norm_transpose(src, rs, nt, xnT, r_xnT, c0, gT[:, g_pre, :], sb["junk"], sb["xs_bf"],
                           sb["stat"], sb["r_tmp"])
        nb_ = -(-T // 512)
        bsz = -(-T // (nb_ * 16)) * 16
        nblocks = [(b0, min(bsz, T - b0)) for b0 in range(0, T, bsz)]
        for gi, (f0, gn) in enumerate(GROUPS):
            ab = gi % 2
            r_act = sb["r_act"][ab]
            for fp in range(gn // 2):
                st = cnt["gu"] % 2
                cnt["gu"] += 1
                wg_t, wu_t, r_w = sb["wg"][st], sb["wu"][st], sb["r_wgu"][st]
                fcol = (f0 + 2 * fp) * 128
                P.dma("pool", wg_t[:], wview(wgu, 0, D, fcol, 256), writes=[r_w])
                P.dma("pool", wu_t[:], wview(wgu, 0, D, DFF + fcol, 256), writes=[r_w], append=True)
                for sub in range(2):
                    fi = 2 * fp + sub
                    for (b0, bn) in nblocks:
                        bg, bu = sb["gu_banks"][cnt["bank"] % 2]
                        cnt["bank"] += 1

                        def mm(e, wt, bank, b0=b0, bn=bn, sub=sub):
                            ins = None
                            for kc in range(16):
                                ins = e.matmul(psum[bank][:, :bn], wt[:, kc, sub * 128:(sub + 1) * 128],
                                               xnT[:, kc, b0:b0 + bn], start=(kc == 0), stop=(kc == 15))
                            return ins
                        P.op("pe", lambda e, wt=wg_t, bank=bg, mm=mm: mm(e, wt, bank), reads=[r_w, r_xnT],
                             writes=[psr[bg]])
                        P.op("pe", lambda e, wt=wu_t, bank=bu, mm=mm: mm(e, wt, bank), reads=[r_w, r_xnT],
                             writes=[psr[bu]])
                        sgi = cnt["sg"] % 2
                        cnt["sg"] += 1
                        sg, r_sg = sb["sg"][sgi], sb["r_sg"][sgi]
                        P.op("act", lambda e, bg=bg, bn=bn, sg=sg: e.activation(
                            out=sg[:, :bn], in_=psum[bg][:, :bn], func=AF.Silu),
                            reads=[psr[bg]], writes=[r_sg])
                        P.op("dve", lambda e, bu=bu, b0=b0, bn=bn, sg=sg, fi=fi, ab=ab: e.tensor_tensor(
                            out=actT[ab][:, fi, b0:b0 + bn], in0=psum[bu][:, :bn], in1=sg[:, :bn], op=ALU.mult),
                            reads=[psr[bu], r_sg], writes=[r_act])
            for q in range(4):
                st = cnt["wd"] % 2
                cnt["wd"] += 1
                wd_t, r_wd = sb["wd"][st], sb["r_wd"][st]
                P.dma("pool", wd_t[:, :gn, :], wview(wdn, f0 * 128, gn * 128, q * 512, 512), writes=[r_wd])
                for ti, (t, c0) in enumerate(zip(tiles, cols)):
                    nt = t["nt"]
                    pd = sb["dn_banks"][cnt["dn"] % 4]
                    cnt["dn"] += 1

                    def mmd(e, pd=pd, c0=c0, nt=nt, wd_t=wd_t, ab=ab, gn=gn):
                        ins = None
                        for fi in range(gn):
                            ins = e.matmul(psum[pd][:nt, :], actT[ab][:, fi, c0:c0 + nt], wd_t[:, fi, :],
                                           start=(fi == 0), stop=(fi == gn - 1))
                        return ins
                    P.op("pe", mmd, reads=[r_act, r_wd], writes=[psr[pd]])
                    r_y = sb["r_ysb"][ti][q]
                    dst = ysb[ti][:nt, q * 512:(q + 1) * 512]
                    if gi == 0:
                        P.op("act", lambda e, dst=dst, pd=pd, nt=nt: e.activation(out=dst, in_=psum[pd][:nt, :],
                                                                                 func=AF.Copy),
                             reads=[psr[pd]], writes=[r_y])
                    else:
                        P.op("dve", lambda e, dst=dst, pd=pd, nt=nt: e.tensor_tensor(
                            out=dst, in0=psum[pd][:nt, :], in1=dst, op=ALU.add),
                            reads=[psr[pd], r_y], writes=[r_y])
        for ti, t in enumerate(tiles):
            nt = t["nt"]
            stat = sb["stat2"]
            r_t2 = sb["r_tmp2"]
            ry = sb["r_ysb"][ti]
            if t.get("x_ap") is None:
                k = cnt["xst"] % len(sb["xst"])
                cnt["xst"] += 1
                t["load"](sb["xst"][k][:nt, :], [sb["r_xst"][k]])
                xap, rx = sb["xst"][k][:nt, :], sb["r_xst"][k]
            else:
                xap, rx = t["x_ap"], t["r_x"]
            P.op("act", lambda e, ti=ti, nt=nt: e.activation(out=sb["junk"][:nt, :], in_=ysb[ti][:nt, :],
                                                            func=AF.Square, accum_out=stat[:nt, 0:1]),
                 reads=ry, writes=[r_t2])
            P.op("act", lambda e, nt=nt: e.activation(out=stat[:nt, 1:2], in_=stat[:nt, 0:1], func=AF.Sqrt,
                                                      scale=1.0 / D, bias=epsb[:nt, 0:1]),
                 reads=[r_t2, r_const], writes=[r_t2])
            P.op("dve", lambda e, nt=nt: e.reciprocal(out=stat[:nt, 2:3], in_=stat[:nt, 1:2]),
                 reads=[r_t2], writes=[r_t2])
            P.op("dve", lambda e, ti=ti, nt=nt: e.scalar_tensor_tensor(
                out=ysb[ti][:nt, :], in0=ysb[ti][:nt, :], scalar=stat[:nt, 2:3], in1=sb["grow"][:nt, :],
                op0=ALU.mult, op1=ALU.mult),
                reads=[r_t2, sb["r_grow"]] + ry, writes=ry)
            P.op("dve", lambda e, ti=ti, nt=nt, xap=xap: e.tensor_tensor(out=ysb[ti][:nt, :], in0=ysb[ti][:nt, :],
                                                                         in1=xap, op=ALU.add),
                 reads=ry + [rx], writes=ry)
            t["done"](ti, ysb[ti][:nt, :], ry)

    r_h1d = [Res(f"h1d{i}") for i in range(NSLOT + 1)]
    with contextlib.ExitStack() as ph:
        sb = alloc_ffn(ph, 6, 'a1_', T=768)
        load_grow(sb, 1)
        pass_slots = [list(range(0, 6)), list(range(6, 12)), list(range(12, 17)), list(range(17, 22)),
                      list(range(22, 27)), list(range(27, 32)) + [NSLOT]]
        if debug is not None and stop_after in ('a1', 'a2'):
            pass_slots = [[0, 1, 2, 3, NSLOT]]
        if only is not None or "a1" not in phases:
            pass_slots = []
        for slots_ in pass_slots:
            tiles = []
            for s in slots_:
                if s == NSLOT:
                    tiles.append(dict(nt=NMETA, slot=NSLOT, load=(lambda dst, rd: P.dma(
                        "sp", dst, meta[:, :], writes=rd))))
                else:
                    tiles.append(dict(nt=128, slot=s, load=(lambda dst, rd, s=s: P.dma(
                        "sp", dst, xs[s * 128:(s + 1) * 128, :], writes=rd))))

            def done(ti, y_ap, ry, tiles=tiles):
                s = tiles[ti]["slot"]
                nt = tiles[ti]["nt"]
                P.dma("sp", h1_d[s * 128:s * 128 + nt, :], y_ap, reads=ry, writes=[r_h1d[s]])
                if dbg is not None and stop_after == "a1":
                    P.dma("sp", dbg[ti * 128:ti * 128 + nt, :], y_ap, reads=ry, writes=[Res()])
            for t in tiles:
                t["done"] = done
            ffn_pass(0, tiles, sb)
        P.barrier()


    def dump(ap_sb, reads, r0, nrow, c0, ncol):
        P.dma("sp", dbg[r0:r0 + nrow, c0:c0 + ncol], ap_sb, reads=reads, writes=[Res()])

    r_kvn_d = [Res(f"kvnd{i}") for i in range(NSLOT + 1)]
    r_kpe_d = [Res(f"kped{i}") for i in range(NSLOT + 1)]
    r_cqn_d = [Res(f"cqnd{i}") for i in range(NT_OWN)]
    if stop_after not in ("a1",) and only is None and "a2" in phases:
      with contextlib.ExitStack() as ph:
        def sb_t(name, shape, dt):
            return ph.enter_context(nc.sbuf_tensor('a2_' + name, shape, dt))
        w_lat = sb_t("w_lat", [128, 16, 1152], BF16)
        r_wlat = Res("wlat")
        P.dma("pool", w_lat[:, :, 0:544], wview(w_in, 0, D, 1024, 544), writes=[r_wlat])
        P.dma("pool", w_lat[:, :, 544:1088], wview(w_in, 0, D, 1568, 544), writes=[r_wlat], append=True)
        P.op("dve", lambda e: e.tensor_scalar(out=w_lat[:, :, 1088:1120], in0=w_lat[:, :, 1056:1088], scalar1=-1.0,
                                              scalar2=None, op0=ALU.mult), reads=[r_wlat], writes=[r_wlat])
        P.op("dve", lambda e: e.tensor_copy(out=w_lat[:, :, 1120:1152], in_=w_lat[:, :, 1024:1056]),
             reads=[r_wlat], writes=[r_wlat])
        cos_sb = sb_t("cos_sb", [ROPE, LK], F32)
        sin_sb = sb_t("sin_sb", [ROPE, LK], F32)
        latg = sb_t("latg", [128, 8], F32)
        r_tab = Res("tab")
        P.dma("sp", cos_sb[:, :], cosT[:, :], writes=[r_tab])
        P.dma("sp", sin_sb[:, :], sinT[:, :], writes=[r_tab], append=True)
        P.dma("sp", latg[:, :], latgT[:, :], writes=[r_tab], append=True)
        hbuf = [sb_t(f"hbuf{i}", [128, D], F32) for i in range(2)]
        r_hbuf = [Res("hb0"), Res("hb1")]
        xn2 = [sb_t(f"xn2_{i}", [128, 16, 128], BF16) for i in range(2)]
        r_xn2 = [Res("xn2_0"), Res("xn2_1")]
        junk = sb_t("junk", [128, D], BF16)
        xs_bf = sb_t("xs_bf", [128, D], BF16)
        stat = sb_t("stat", [128, 4], F32)
        r_tmp = Res("tmp")
        junk2 = sb_t("junk2", [128, 512], BF16)
        xs_bf2 = sb_t("xs_bf2", [128, 512], BF16)
        stat2 = sb_t("stat2", [128, 4], F32)
        r_tmp2 = Res("tmp2")
        t1 = sb_t("t1", [ROPE, 128], F32)
        t2 = sb_t("t2", [ROPE, 128], F32)
        r_t12 = Res("t12")
        kpe_t = [sb_t(f"kpe_t{i}", [ROPE, 128], BF16) for i in range(2)]
        r_kpe_t = [Res("kpet0"), Res("kpet1")]
        kvn_t = [sb_t(f"kvn_t{i}", [128, 4, 128], BF16) for i in range(2)]
        r_kvn_t = [Res("kvnt0"), Res("kvnt1")]
        cqn_t = [sb_t(f"cqn_t{i}", [128, 4, 128], BF16) for i in range(2)]
        r_cqn_t = [Res("cqnt0"), Res("cqnt1")]
        ntile_a2 = NSLOT + 1 if debug is None or stop_after != "a2" else 5
        for tix in range(ntile_a2):
            pb = tix % 2
            if tix == 0:
                nt, row0, kcol, own, s = NMETA, SEQ, 0, False, NSLOT
            else:
                s = tix - 1
                nt, row0, kcol, own = 128, s * 128, NMETA + s * 128, s < NT_OWN
            hb = hbuf[pb]
            P.dma("sp", hb[:nt, :], h1_d[row0:row0 + nt, :], reads=[r_h1d[s]], writes=[r_hbuf[pb]])
            norm_transpose(hb[:nt, :], [r_hbuf[pb]], nt, xn2[pb], r_xn2[pb], 0, gT[:, 2, :], junk, xs_bf, stat, r_tmp)

            def mm_tok(e, bank, c0, nt=nt, pb=pb):
                ins = None
                for kc in range(16):
                    ins = e.matmul(psum[bank][:nt, :], xn2[pb][:, kc, :nt], w_lat[:, kc, c0:c0 + 512],
                                   start=(kc == 0), stop=(kc == 15))
                return ins

            def mm_feat(e, bank, c0, nt=nt, pb=pb):
                ins = None
                for kc in range(16):
                    ins = e.matmul(psum[bank][:ROPE, :nt], w_lat[:, kc, c0:c0 + ROPE], xn2[pb][:, kc, :nt],
                                   start=(kc == 0), stop=(kc == 15))
                return ins
            P.op("pe", lambda e, f=mm_tok: f(e, 1, 512), reads=[r_xn2[pb], r_wlat], writes=[psr[1]])
            P.op("pe", lambda e, f=mm_feat: f(e, 2, 1024), reads=[r_xn2[pb], r_wlat], writes=[psr[2]])
            P.op("pe", lambda e, f=mm_feat: f(e, 3, 1088), reads=[r_xn2[pb], r_wlat], writes=[psr[3]])
            if own:
                P.op("pe", lambda e, f=mm_tok: f(e, 0, 0), reads=[r_xn2[pb], r_wlat], writes=[psr[0]])
            P.op("dve", lambda e, nt=nt, kcol=kcol: e.tensor_tensor(out=t1[:, :nt], in0=psum[2][:ROPE, :nt],
                                                                  in1=cos_sb[:, kcol:kcol + nt], op=ALU.mult),
                 reads=[psr[2], r_tab], writes=[r_t12])
            P.op("dve", lambda e, nt=nt, kcol=kcol: e.tensor_tensor(out=t2[:, :nt], in0=psum[3][:ROPE, :nt],
                                                                  in1=sin_sb[:, kcol:kcol + nt], op=ALU.mult),
                 reads=[psr[3], r_tab], writes=[r_t12])
            P.op("dve", lambda e, nt=nt, pb=pb: e.tensor_tensor(out=kpe_t[pb][:, :nt], in0=t1[:, :nt], in1=t2[:, :nt],
                                                              op=ALU.add),
                 reads=[r_t12], writes=[r_kpe_t[pb]])
            P.dma("sp", kpeT_d[:, kcol:kcol + nt], kpe_t[pb][:, :nt], reads=[r_kpe_t[pb]], writes=[r_kpe_d[tix]])
            norm_transpose(psum[1][:nt, :], [psr[1]], nt, kvn_t[pb], r_kvn_t[pb], 0, latg[:, 4:8], junk2, xs_bf2,
                           stat2, r_tmp2, nkc=4)
            P.dma("sp", kvnT_d[:, :, kcol:kcol + nt], kvn_t[pb][:, :, :nt], reads=[r_kvn_t[pb]], writes=[r_kvn_d[tix]])
            if stop_after == "a2":
                for kc in range(4):
                    P.dma("pool", dbg[0:128, kc * 1024 + kcol:kc * 1024 + kcol + nt], kvn_t[pb][:, kc, :nt],
                          reads=[r_kvn_t[pb]], writes=[Res()])
                P.dma("pool", dbg[128:192, kcol:kcol + nt], kpe_t[pb][:, :nt], reads=[r_kpe_t[pb]], writes=[Res()])
                if own:
                    for kc in range(4):
                        P.dma("pool", dbg[256:384, kc * 1024 + s * 128:kc * 1024 + s * 128 + nt], cqn_t[pb][:, kc, :nt],
                              reads=[r_cqn_t[pb]], writes=[Res()])
            if own:
                norm_transpose(psum[0][:nt, :], [psr[0]], nt, cqn_t[pb], r_cqn_t[pb], 0, latg[:, 0:4], junk2, xs_bf2,
                               stat2, r_tmp2, nkc=4)
                P.dma("sp", cqnT_d[:, :, s * 128:(s + 1) * 128], cqn_t[pb][:, :, :], reads=[r_cqn_t[pb]],
                      writes=[r_cqn_d[s]])
        P.barrier()

    r_oT_d = [Res(f"oTd{h}") for h in range(NH)]
    if stop_after not in ("a1", "a2", "a2f") and "b" in phases:
      with contextlib.ExitStack() as ph:
        def sb_t(name, shape, dt):
            return ph.enter_context(nc.sbuf_tensor('b_' + name, shape, dt))
        NQ = NT_OWN * 128
        kvnT = sb_t("kvnT", [128, 4, LK], BF16)
        kpeT = sb_t("kpeT", [ROPE, LK], BF16)
        cqnT = sb_t("cqnT", [128, 4, NQ], BF16)
        cosq = sb_t("cosq", [ROPE, NQ], F32)
        sinq = sb_t("sinq", [ROPE, NQ], F32)
        tri = sb_t("tri", [128, 128], BF16)
        ones = sb_t("ones", [128, 128], BF16)
        sel = sb_t("sel", [128, 4], F32)
        r_lat = Res("lat")
        P.dma("sp", kvnT[:, :, :], kvnT_d[:, :, :], reads=r_kvn_d, writes=[r_lat])
        P.dma("sp", kpeT[:, :], kpeT_d[:, :], reads=r_kpe_d, writes=[r_lat], append=True)
        P.dma("sp", cqnT[:, :, :], cqnT_d[:, :, :], reads=r_cqn_d, writes=[r_lat], append=True)
        P.dma("sp", cosq[:, :], cosT[:, NMETA:NMETA + NQ], writes=[r_lat], append=True)
        P.dma("sp", sinq[:, :], sinT[:, NMETA:NMETA + NQ], writes=[r_lat], append=True)
        P.dma("sp", sel[:, :], sel_d[:, :], writes=[r_lat], append=True)
        r_msk = Res("msk")
        P.dma("pool", tri[:, :], tri_d[:, :], writes=[r_msk])
        P.op("dve", lambda e: e.memset(ones[:], 1.0), writes=[r_msk])
        hbufs = []
        for i in range(2):
            hbufs.append(dict(
                wkv=sb_t(f"wkv{i}", [128, 4, 256], BF16), wq=sb_t(f"wq{i}", [128, 4, 256], BF16),
                KT=sb_t(f"KT{i}", [128, LK], BF16), V=sb_t(f"V{i}", [128, NSLOT + 1, 128], BF16),
                qT=sb_t(f"qT{i}", [128, NQ], BF16), qpe=sb_t(f"qpe{i}", [ROPE, NQ], BF16),
                oT=sb_t(f"oT{i}", [128, NQ], BF16),
                r_w=Res(f"hw{i}"), r_KT=Res(f"KT{i}"), r_V=Res(f"V{i}"), r_qT=Res(f"qT{i}"), r_qpe=Res(f"qpe{i}"),
                r_oT=Res(f"oT{i}")))
        pT = [sb_t(f"pT{i}", [128, 512], BF16) for i in range(3)]
        r_pT = [Res(f"pT{i}") for i in range(3)]
        rec = sb_t("rec", [128, 512], F32)
        r_rec = Res("rec")
        rt1 = sb_t("rt1", [ROPE, 512], F32)
        rt2 = sb_t("rt2", [ROPE, 512], F32)
        r_rt = Res("rt")
        pj = [6, 7, 0, 1, 2, 3, 4]
        pjc = [0]

        def pjbank():
            b = pj[pjc[0] % len(pj)]
            pjc[0] += 1
            return b

        def proj(h, hb):
            B = hbufs[hb]
            wkv, wq = B["wkv"], B["wq"]
            P.dma("pool", wkv[:, :, :], wview(w_kv_b, 0, KVL, h * 256, 256), writes=[B["r_w"]])
            P.dma("pool", wq[:, :, 0:QKD], wview(w_q_b, 0, QL, h * QKD, QKD), writes=[B["r_w"]], append=True)
            P.op("dve", lambda e: e.tensor_scalar(out=wq[:, :, 192:224], in0=wq[:, :, 160:192], scalar1=-1.0,
                                                  scalar2=None, op0=ALU.mult), reads=[B["r_w"]], writes=[B["r_w"]])
            P.op("dve", lambda e: e.tensor_copy(out=wq[:, :, 224:256], in_=wq[:, :, 128:160]),
                 reads=[B["r_w"]], writes=[B["r_w"]])
            for b0 in range(0, LK, 512):
                bn = min(512, LK - b0)
                bk = pjbank()

                def mmk(e, bk=bk, b0=b0, bn=bn):
                    ins = None
                    for kc in range(4):
                        ins = e.matmul(psum[bk][:, :bn], wkv[:, kc, 0:128], kvnT[:, kc, b0:b0 + bn],
                                       start=(kc == 0), stop=(kc == 3))
                    return ins
                P.op("pe", mmk, reads=[B["r_w"], r_lat], writes=[psr[bk]])
                P.op("act", lambda e, bk=bk, b0=b0, bn=bn: e.activation(out=B["KT"][:, b0:b0 + bn],
                                                                       in_=psum[bk][:, :bn], func=AF.Copy),
                     reads=[psr[bk]], writes=[B["r_KT"]])
            for g0 in range(0, NSLOT + 1, 4):
                gn = min(4, NSLOT + 1 - g0)
                bk = pjbank()

                def mmv(e, bk=bk, g0=g0, gn=gn):
                    ins = None
                    for j in range(gn):
                        kt = g0 + j
                        nk = NMETA if kt == 0 else 128
                        kc0 = 0 if kt == 0 else NMETA + (kt - 1) * 128
                        for kc in range(4):
                            ins = e.matmul(psum[bk][:nk, j * 128:(j + 1) * 128], kvnT[:, kc, kc0:kc0 + nk],
                                           wkv[:, kc, 128:256], start=(kc == 0), stop=(kc == 3))
                    return ins
                P.op("pe", mmv, reads=[B["r_w"], r_lat], writes=[psr[bk]])
                if g0 == 0:
                    P.op("dve", lambda e, bk=bk: e.tensor_copy(out=B["V"][:NMETA, 0, :], in_=psum[bk][:NMETA, 0:128]),
                         reads=[psr[bk]], writes=[B["r_V"]])
                    P.op("dve", lambda e, bk=bk, gn=gn: e.tensor_copy(
                        out=B["V"][:, 1:gn, :], in_=psum[bk][:, 128:gn * 128].rearrange("p (k c) -> p k c", c=128)),
                        reads=[psr[bk]], writes=[B["r_V"]])
                else:
                    P.op("dve", lambda e, bk=bk, g0=g0, gn=gn: e.tensor_copy(
                        out=B["V"][:, g0:g0 + gn, :], in_=psum[bk][:, :gn * 128].rearrange("p (k c) -> p k c", c=128)),
                        reads=[psr[bk]], writes=[B["r_V"]])
            for qb in range(NQ // 512):
                bk = pjbank()

                def mmq(e, bk=bk, qb=qb):
                    ins = None
                    for kc in range(4):
                        ins = e.matmul(psum[bk][:, :], wq[:, kc, 0:128], cqnT[:, kc, qb * 512:(qb + 1) * 512],
                                       start=(kc == 0), stop=(kc == 3))
                    return ins
                P.op("pe", mmq, reads=[B["r_w"], r_lat], writes=[psr[bk]])
                P.op("act", lambda e, bk=bk, qb=qb: e.activation(out=B["qT"][:, qb * 512:(qb + 1) * 512],
                                                               in_=psum[bk][:, :], func=AF.Copy),
                     reads=[psr[bk]], writes=[B["r_qT"]])
                ba, bb = pjbank(), pjbank()

                def mmp(e, bank, c0, qb=qb):
                    ins = None
                    for kc in range(4):
                        ins = e.matmul(psum[bank][:ROPE, :], wq[:, kc, c0:c0 + ROPE], cqnT[:, kc, qb * 512:(qb + 1) * 512],
                                       start=(kc == 0), stop=(kc == 3))
                    return ins
                P.op("pe", lambda e, f=mmp, ba=ba: f(e, ba, 128), reads=[B["r_w"], r_lat], writes=[psr[ba]])
                P.op("pe", lambda e, f=mmp, bb=bb: f(e, bb, 192), reads=[B["r_w"], r_lat], writes=[psr[bb]])
                P.op("dve", lambda e, ba=ba, qb=qb: e.tensor_tensor(out=rt1[:, :], in0=psum[ba][:ROPE, :],
                                                                  in1=cosq[:, qb * 512:(qb + 1) * 512], op=ALU.mult),
                     reads=[psr[ba], r_lat], writes=[r_rt])
                P.op("dve", lambda e, bb=bb, qb=qb: e.tensor_tensor(out=rt2[:, :], in0=psum[bb][:ROPE, :],
                                                                  in1=sinq[:, qb * 512:(qb + 1) * 512], op=ALU.mult),
                     reads=[psr[bb], r_lat], writes=[r_rt])
                P.op("dve", lambda e, qb=qb: e.tensor_tensor(out=B["qpe"][:, qb * 512:(qb + 1) * 512], in0=rt1[:, :],
                                                            in1=rt2[:, :], op=ALU.add),
                     reads=[r_rt], writes=[B["r_qpe"]])

        sbank = [0]
        qcount = [0]

        def attn(h, hb):
            B = hbufs[hb]
            for qb in range(NQ // 512):
                tl = [(0, NMETA, 0, 0, False, False)]
                for ip in range(4 * qb + 4):
                    q0 = max(0, ip - 4 * qb) * 128
                    tl.append((1 + ip, 128, NMETA + ip * 128, q0, ip >= 4 * qb, False))
                    tl.append((1 + NT_OWN + ip, 128, NMETA + (NT_OWN + ip) * 128, q0, False, ip == 0))
                tl = [tl[0]] + [x for x in tl[1:] if x[3] > 0] + [x for x in tl[1:] if x[3] == 0]
                ob = 3 + (qcount[0] % 2)
                db = 5 + (qcount[0] % 2)
                qcount[0] += 1
                n = len(tl)
                sb_of = {}

                def qk(j):
                    kt, nk, kc0, q0, diag, b16 = tl[j]
                    s = sbank[0] % 3
                    sbank[0] += 1
                    sb_of[j] = s

                    def f(e, s=s, nk=nk, kc0=kc0, q0=q0, qb=qb):
                        e.matmul(psum[s][:nk, q0:512], B["KT"][:, kc0:kc0 + nk],
                                 B["qT"][:, qb * 512 + q0:(qb + 1) * 512], start=True, stop=False)
                        return e.matmul(psum[s][:nk, q0:512], kpeT[:, kc0:kc0 + nk],
                                        B["qpe"][:, qb * 512 + q0:(qb + 1) * 512], start=False, stop=True)
                    P.op("pe", f, reads=[B["r_KT"], B["r_qT"], B["r_qpe"], r_lat], writes=[psr[s]])
                    bias = sel[:nk, 2:3] if b16 else 0.0
                    P.op("act", lambda e, s=s, nk=nk, q0=q0, bias=bias: e.activation(
                        out=pT[s][:nk, q0:512], in_=psum[s][:nk, q0:512], func=AF.Exp, bias=bias, scale=SCALE),
                        reads=[psr[s], r_lat], writes=[r_pT[s]])
                    if diag:
                        P.op("dve", lambda e, s=s, q0=q0: e.tensor_tensor(out=pT[s][:, q0:q0 + 128],
                                                                        in0=pT[s][:, q0:q0 + 128], in1=tri[:, :],
                                                                        op=ALU.mult),
                             reads=[r_pT[s], r_msk], writes=[r_pT[s]])

                def pv(j):
                    kt, nk, kc0, q0, diag, b16 = tl[j]
                    s = sb_of[j]

                    def f(e, s=s, kt=kt, nk=nk, q0=q0, j=j, ob=ob, n=n, db=db):
                        e.matmul(psum[ob][:, q0:512], B["V"][:nk, kt, :], pT[s][:nk, q0:512],
                                 start=(j == 0), stop=(j == n - 1))
                        return e.matmul(psum[db][:, q0:512], ones[:nk, :], pT[s][:nk, q0:512],
                                        start=(j == 0), stop=(j == n - 1))
                    P.op("pe", f, reads=[B["r_V"], r_pT[s], r_msk], writes=[psr[ob], psr[db]])
                qk(0)
                qk(1)
                for j in range(n):
                    if j + 2 < n:
                        qk(j + 2)
                    pv(j)
                P.op("dve", lambda e, db=db: e.reciprocal(out=rec[:, :], in_=psum[db][:, :]), reads=[psr[db]],
                     writes=[r_rec])
                P.op("dve", lambda e, ob=ob, qb=qb: e.tensor_tensor(out=B["oT"][:, qb * 512:(qb + 1) * 512],
                                                                  in0=psum[ob][:, :], in1=rec[:, :], op=ALU.mult),
                     reads=[psr[ob], r_rec], writes=[B["r_oT"]])
            P.dma("sp", oT_d[:, h, :], B["oT"][:, :], reads=[B["r_oT"]], writes=[r_oT_d[h]])
            if stop_after == "b":
                P.dma("pool", dbg[h * 128:(h + 1) * 128, :], B["oT"][:, :], reads=[B["r_oT"]], writes=[Res()])

        nheads = NH if stop_after != "b" else (2 if only is None else 1)
        if debug is not None and len(debug) > 4:
            nheads = debug[4]
        import os
        if not os.environ.get("K_PIPE"):
            for h in range(nheads):
                proj(h, h % 2)
                attn(h, h % 2)
        else:
            proj(0, 0)
            for h in range(nheads):
                if h + 1 < nheads:
                    proj(h + 1, (h + 1) % 2)
                attn(h, h % 2)
        P.barrier()

    r_h2d = [Res(f"h2d{i}") for i in range(NT_OWN)]
    if stop_after not in ("a1", "a2", "a2f", "b") and "c1" in phases:
      with contextlib.ExitStack() as ph:
        def sb_t(name, shape, dt):
            return ph.enter_context(nc.sbuf_tensor('c1_' + name, shape, dt))
        xn2m = sb_t("xn2m", [128, 16, 512], BF16)
        r_xn2m = Res("xn2m")
        xn2h = sb_t("xn2h", [128, 16, 64], BF16)
        r_xn2h = Res("xn2h")
        hst = sb_t("hst", [128, D], F32)
        r_hst = Res("hst")
        halo = sb_t("halo", [NMETA, D], F32)
        r_halo = Res("halo")
        junk = sb_t("junk", [128, D], BF16)
        xs_bf = sb_t("xs_bf", [128, D], BF16)
        stat = sb_t("stat", [128, 4], F32)
        r_tmp = Res("tmp")
        stat2 = sb_t("stat2", [128, 4], F32)
        r_tmp2 = Res("tmp2")
        uext = sb_t("uext", [128, 2, 576], F32)
        r_uext = [Res("uext0"), Res("uext1")]
        ptmp = [sb_t(f"ptmp{i}", [128, 576], F32) for i in range(2)]
        r_ptmp = Res("ptmp")
        P.op("dve", lambda e: e.memset(ptmp[0][:, :], 0.0), writes=[r_ptmp])
        P.op("dve", lambda e: e.memset(ptmp[1][:, :], 0.0), writes=[r_ptmp])
        dext = sb_t("dext", [128, 2, 576], BF16)
        r_dext = Res("dext")
        ypT = sb_t("ypT", [128, 8, 512], BF16)
        r_ypT = Res("ypT")
        oTs = sb_t("oTs", [128, NH, 512], BF16)
        r_oTs = Res("oTs")
        yT = sb_t("yT", [128, 16, 512], BF16)
        r_yT = Res("yT")
        stg = [sb_t(f"stg{i}", [128, 14336], BF16) for i in range(2)]
        r_stg = [Res("stg0"), Res("stg1")]
        wst = []
        for i in range(2):
            wst.append(dict(gp=stg[i][:, 0:4096].rearrange("p (k c) -> p k c", c=256),
                            mo=stg[i][:, 4096:8192].rearrange("p (k c) -> p k c", c=256),
                            gm=stg[i][:, 8192:12288].rearrange("p (k c) -> p k c", c=256),
                            po=stg[i][:, 12288:14336].rearrange("p (k c) -> p k c", c=256),
                            r=r_stg[i]))
        wup = [stg[i][:, 0:4096].rearrange("p (k c) -> p k c", c=256) for i in range(2)]
        r_wup = r_stg
        wo = wup
        r_wo = r_stg
        sgt = [sb_t(f"sgt{i}", [128, 512], F32) for i in range(4)]
        r_sgt = [Res(f"sgt{i}") for i in range(4)]
        m_sb = [sb_t(f"m_sb{i}", [128, D], F32) for i in range(4)]
        r_m = [Res(f"m{i}") for i in range(4)]
        growm = sb_t("growm", [128, D], F32)
        pw = sb_t("pw", [128, 4, 2, 256], BF16)
        psc = sb_t("psc", [128, 8], F32)
        selc = sb_t("selc", [128, 4], F32)
        r_cc = Res("cc")
        P.dma("sp", growm[:, :], gains[3:4, :].partition_broadcast(128), writes=[r_cc])
        P.dma("sp", psc[:, :], pscT[:, :], writes=[r_cc], append=True)
        P.dma("sp", selc[:, :], sel_d[:, :], writes=[r_cc], append=True)
        P.dma("pool", pw[:, :, :, :], pool_w.rearrange("g (k p) o -> p g k o", p=128), writes=[r_cc], append=True)
        stc = [0]
        npass_c = 4 if stop_after != "c1" else 1
        for ps_i in range(npass_c):
            own_tiles = list(range(ps_i * 4, ps_i * 4 + 4))
            P.dma("sp", oTs[:, :, :], oT_d[:, :, ps_i * 512:(ps_i + 1) * 512], reads=r_oT_d, writes=[r_oTs])
            for t, i in enumerate(own_tiles):
                P.dma("sp", m_sb[t][:, :], h1_d[i * 128:(i + 1) * 128, :], reads=[r_h1d[i]], writes=[r_m[t]])
                norm_transpose(m_sb[t][:, :], [r_m[t]], 128, xn2m, r_xn2m, t * 128, gT[:, 2, :], junk, xs_bf, stat, r_tmp)
                if i == 0:
                    P.dma("sp", hst[:NMETA, :], h1_d[SEQ:SEQ + NMETA, :], reads=[r_h1d[NSLOT]], writes=[r_hst])
                    P.dma("sp", halo[:, :], h1_d[NT_OWN * 128 + 112:NT_OWN * 128 + 128, :], reads=[r_h1d[NT_OWN]],
                          writes=[r_halo])
                    P.op("dve", lambda e: e.tensor_scalar(out=hst[:NMETA, :], in0=hst[:NMETA, :],
                                                          scalar1=selc[:NMETA, 0:1], scalar2=None, op0=ALU.mult),
                         reads=[r_hst, r_cc], writes=[r_hst])
                    P.op("dve", lambda e: e.scalar_tensor_tensor(out=halo[:, :], in0=halo[:, :],
                                                                 scalar=selc[:NMETA, 1:2], in1=hst[:NMETA, :],
                                                                 op0=ALU.mult, op1=ALU.add),
                         reads=[r_hst, r_halo, r_cc], writes=[r_halo])
                else:
                    sl = NT_OWN + i
                    P.dma("sp", halo[:, :], h1_d[sl * 128 + 112:sl * 128 + 128, :], reads=[r_h1d[sl]], writes=[r_halo])
                norm_transpose(halo[:, :], [r_halo], NMETA, xn2h, r_xn2h, t * 16, gT[:, 2, :], junk, xs_bf, stat, r_tmp)
            for g in range(4):
                wi = stc[0] % 2
                stc[0] += 1
                P.dma("pool", wup[wi], wview(w_in, 0, D, g * 256, 256), writes=[r_wup[wi]])
                nsteps = g + 1
                for sub in range(2):
                    def mmu(e, bank, rhs, n, sub=sub, wi=wi):
                        ins = None
                        for kc in range(16):
                            ins = e.matmul(psum[bank][:, :n], wup[wi][:, kc, sub * 128:(sub + 1) * 128], rhs(kc),
                                           start=(kc == 0), stop=(kc == 15))
                        return ins
                    ba, bb = (0, 1) if sub == 0 else (2, 3)
                    P.op("pe", lambda e, f=mmu, ba=ba: f(e, ba, lambda kc: xn2m[:, kc, :], 512),
                         reads=[r_wup[wi], r_xn2m], writes=[psr[ba]])
                    P.op("pe", lambda e, f=mmu, bb=bb: f(e, bb, lambda kc: xn2h[:, kc, :], 64),
                         reads=[r_wup[wi], r_xn2h], writes=[psr[bb]])
                    uv = uext[:, sub, :].rearrange("p (s c) -> p s c", c=144)
                    P.op("act", lambda e, ba=ba, uv=uv: e.activation(
                        out=uv[:, :, 16:144], in_=psum[ba][:, :].rearrange("p (s c) -> p s c", c=128), func=AF.Copy),
                        reads=[psr[ba]], writes=[r_uext[sub]])
                    P.op("act", lambda e, bb=bb, uv=uv: e.activation(
                        out=uv[:, :, 0:16], in_=psum[bb][:, :64].rearrange("p (s c) -> p s c", c=16), func=AF.Copy),
                        reads=[psr[bb], r_uext[sub]], writes=[r_uext[sub]])
                    cur = uext[:, sub, :]
                    rcur = r_uext[sub]
                    for k in range(nsteps):
                        sh = 1 << k
                        nx = ptmp[k % 2]
                        P.op("dve", lambda e, cur=cur, nx=nx, sh=sh: e.tensor_tensor(
                            out=nx[:, sh:576], in0=cur[:, sh:576], in1=cur[:, 0:576 - sh], op=ALU.add),
                            reads=[rcur, r_ptmp], writes=[r_ptmp])
                        cur = nx[:, :]
                        rcur = r_ptmp
                    wnd = float(1 << nsteps)
                    P.op("dve", lambda e, cur=cur, sub=sub, wnd=wnd: e.scalar_tensor_tensor(
                        out=dext[:, sub, 16:576], in0=cur[:, 16:576], scalar=1.0 / wnd, in1=uext[:, sub, 16:576],
                        op0=ALU.mult, op1=ALU.subtract),
                        reads=[r_ptmp, r_uext[sub]], writes=[r_dext])
                P.op("dve", lambda e: e.memset(dext[:, :, 0:16], 0.0), reads=[], writes=[r_dext])
                for oc2 in range(2):
                    for blk in range(2):
                        bk = 4 + (2 * oc2 + blk) % 4

                        def mmpw(e, bk=bk, oc2=oc2, blk=blk, g=g):
                            ins = None
                            for k2 in range(2):
                                ins = e.matmul(psum[bk][:, :288], pw[:, g, k2, oc2 * 128:(oc2 + 1) * 128],
                                               dext[:, k2, blk * 288:(blk + 1) * 288], start=(k2 == 0), stop=(k2 == 1))
                            return ins
                        P.op("pe", mmpw, reads=[r_dext, r_cc], writes=[psr[bk]])
                        c = 2 * g + oc2
                        P.op("act", lambda e, bk=bk, c=c, blk=blk: e.activation(
                            out=ypT[:, c, blk * 256:(blk + 1) * 256].rearrange("p (s c) -> p s c", c=128),
                            in_=psum[bk][:, :288].rearrange("p (s c) -> p s c", c=144)[:, :, 16:144],
                            func=AF.Copy, scale=psc[:, c:c + 1]),
                            reads=[psr[bk], r_cc], writes=[r_ypT])
            for oc in range(16):
                sub = oc % 2
                if sub == 0:
                    wi = stc[0] % 2
                    stc[0] += 1
                    W = wst[wi]
                    P.dma("pool", W["po"], wview(w_pool_o, 0, POOLW, oc * 128, 256), writes=[W["r"]])
                    P.dma("pool", W["gp"], wview(w_in, 0, D, 2112 + oc * 128, 256), writes=[W["r"]], append=True)
                    P.dma("pool", W["mo"], wview(w_mla_o, 0, D, oc * 128, 256), writes=[W["r"]], append=True)
                    P.dma("pool", W["gm"], wview(w_in, 0, D, 4160 + oc * 128, 256), writes=[W["r"]], append=True)
                bs = (0, 1, 2, 3) if oc % 2 == 0 else (4, 5, 6, 7)

                def mmc(e, bank, wt, rhs, nk, sub=sub):
                    ins = None
                    for kc in range(nk):
                        ins = e.matmul(psum[bank][:, :], wt[:, kc, sub * 128:(sub + 1) * 128], rhs(kc),
                                       start=(kc == 0), stop=(kc == nk - 1))
                    return ins
                P.op("pe", lambda e, f=mmc, W=W, b=bs[0]: f(e, b, W["po"], lambda kc: ypT[:, kc, :], 8),
                     reads=[W["r"], r_ypT], writes=[psr[bs[0]]])
                P.op("pe", lambda e, f=mmc, W=W, b=bs[1]: f(e, b, W["gp"], lambda kc: xn2m[:, kc, :], 16),
                     reads=[W["r"], r_xn2m], writes=[psr[bs[1]]])
                P.op("pe", lambda e, f=mmc, W=W, b=bs[2]: f(e, b, W["mo"], lambda kc: oTs[:, kc, :], 16),
                     reads=[W["r"], r_oTs], writes=[psr[bs[2]]])
                P.op("pe", lambda e, f=mmc, W=W, b=bs[3]: f(e, b, W["gm"], lambda kc: xn2m[:, kc, :], 16),
                     reads=[W["r"], r_xn2m], writes=[psr[bs[3]]])
                P.op("act", lambda e, b=bs[1]: e.activation(out=sgt[0][:, :], in_=psum[b][:, :], func=AF.Sigmoid),
                     reads=[psr[bs[1]]], writes=[r_sgt[0]])
                P.op("act", lambda e, b=bs[3]: e.activation(out=sgt[1][:, :], in_=psum[b][:, :], func=AF.Sigmoid),
                     reads=[psr[bs[3]]], writes=[r_sgt[1]])
                P.op("dve", lambda e, b=bs[0]: e.tensor_tensor(out=sgt[2][:, :], in0=psum[b][:, :], in1=sgt[0][:, :],
                                                              op=ALU.mult),
                     reads=[psr[bs[0]], r_sgt[0]], writes=[r_sgt[2]])
                P.op("dve", lambda e, b=bs[2]: e.tensor_tensor(out=sgt[3][:, :], in0=psum[b][:, :], in1=sgt[1][:, :],
                                                              op=ALU.mult),
                     reads=[psr[bs[2]], r_sgt[1]], writes=[r_sgt[3]])
                P.op("dve", lambda e, oc=oc: e.tensor_tensor(out=yT[:, oc, :], in0=sgt[2][:, :], in1=sgt[3][:, :],
                                                            op=ALU.add),
                     reads=[r_sgt[2], r_sgt[3]], writes=[r_yT])
            for e8 in range(8):
                wi = stc[0] % 2
                stc[0] += 1
                P.dma("pool", wo[wi], wview(w_out, 0, D, e8 * 256, 256), writes=[r_wo[wi]])
                for t in range(4):
                    bk = (e8 * 4 + t) % 8

                    def mmo(e, bk=bk, t=t, wi=wi):
                        ins = None
                        for kc in range(16):
                            ins = e.matmul(psum[bk][:, :256], yT[:, kc, t * 128:(t + 1) * 128], wo[wi][:, kc, :],
                                           start=(kc == 0), stop=(kc == 15))
                        return ins
                    P.op("pe", mmo, reads=[r_yT, r_wo[wi]], writes=[psr[bk]])
                    P.op("act", lambda e, bk=bk, t=t, e8=e8: e.activation(out=m_sb[t][:, e8 * 256:(e8 + 1) * 256],
                                                                         in_=psum[bk][:, :256], func=AF.Copy),
                         reads=[psr[bk]], writes=[r_m[t]])
            for t, i in enumerate(own_tiles):
                P.op("act", lambda e, t=t: e.activation(out=junk[:, :], in_=m_sb[t][:, :], func=AF.Square,
                                                        accum_out=stat2[:, 0:1]),
                     reads=[r_m[t]], writes=[r_tmp2])
                P.op("act", lambda e: e.activation(out=stat2[:, 1:2], in_=stat2[:, 0:1], func=AF.Sqrt, scale=1.0 / D,
                                                   bias=epsb[:, 0:1]),
                     reads=[r_tmp2, r_const], writes=[r_tmp2])
                P.op("dve", lambda e: e.reciprocal(out=stat2[:, 2:3], in_=stat2[:, 1:2]), reads=[r_tmp2], writes=[r_tmp2])
                P.dma("sp", hst[:, :], h1_d[i * 128:(i + 1) * 128, :], reads=[r_h1d[i]], writes=[r_hst])
                P.op("dve", lambda e, t=t: e.scalar_tensor_tensor(out=m_sb[t][:, :], in0=m_sb[t][:, :],
                                                                 scalar=stat2[:, 2:3], in1=growm[:, :],
                                                                 op0=ALU.mult, op1=ALU.mult),
                     reads=[r_tmp2, r_cc, r_m[t]], writes=[r_m[t]])
                P.op("dve", lambda e, t=t: e.tensor_tensor(out=m_sb[t][:, :], in0=m_sb[t][:, :], in1=hst[:, :], op=ALU.add),
                     reads=[r_m[t], r_hst], writes=[r_m[t]])
                P.dma("sp", h2_d[i * 128:(i + 1) * 128, :], m_sb[t][:, :], reads=[r_m[t]], writes=[r_h2d[i]])
                if stop_after == "c1":
                    dump(m_sb[t][:, :], [r_m[t]], t * 128, 128, 0, D)
        P.barrier()

    if stop_after == "all" and "c2" in phases:
      with contextlib.ExitStack() as ph:
        sb = alloc_ffn(ph, 6, 'c2_', T=768)
        load_grow(sb, 5)
        for own_ in ([0, 1, 2, 3, 4, 5], [6, 7, 8, 9, 10], [11, 12, 13, 14, 15]):
            tiles = []
            for i in own_:
                tiles.append(dict(nt=128, slot=i, load=(lambda dst, rd, i=i: P.dma(
                    "sp", dst, h2_d[i * 128:(i + 1) * 128, :], reads=[r_h2d[i]], writes=rd))))

            def done2(ti, y_ap, ry, tiles=tiles):
                i = tiles[ti]["slot"]
                P.dma("sp", out_d[i * 128:(i + 1) * 128, :], y_ap, reads=ry, writes=[Res()])
            for t in tiles:
                t["done"] = done2
            ffn_pass(1, tiles, sb)
        P.barrier()

    P.barrier()
    with nc.Block() as block:
        P.emit(block)
    es.close()
    return nc


def _prep_inputs(inputs):
    x = np.asarray(inputs["x"], dtype=np.float32)
    B = x.shape[0]
    gains = np.stack([np.asarray(inputs[k], np.float32)[0] for k in
                      ("norm_ffn1_pre", "norm_ffn1_post", "norm_mix_pre", "norm_mix_post",
                       "norm_ffn2_pre", "norm_ffn2_post")], axis=0)
    common = {
        "meta": np.ascontiguousarray(inputs["meta_tokens"], dtype=np.float32),
        "gains": np.ascontiguousarray(gains),
        "gainsT": np.ascontiguousarray(gains.reshape(6, 16, 128).transpose(2, 0, 1).reshape(128, 96)),
        "ffn1_w_gu": np.asarray(inputs["ffn1_w_gu"], np.float32)[0],
        "ffn2_w_gu": np.asarray(inputs["ffn2_w_gu"], np.float32)[0],
        "ffn1_w_down": np.asarray(inputs["ffn1_w_down"], np.float32)[0],
        "ffn2_w_down": np.asarray(inputs["ffn2_w_down"], np.float32)[0],
        "w_in": np.asarray(inputs["w_in"], np.float32)[0],
        "pool_w": np.asarray(inputs["pool_w"], np.float32)[0],
        "pool_scale": np.asarray(inputs["pool_scale"], np.float32)[0],
        "w_pool_o": np.asarray(inputs["w_pool_o"], np.float32)[0],
        "q_a_norm": np.asarray(inputs["q_a_norm"], np.float32)[0],
        "w_q_b": np.asarray(inputs["w_q_b"], np.float32)[0],
        "kv_a_norm": np.asarray(inputs["kv_a_norm"], np.float32)[0],
        "w_kv_b": np.asarray(inputs["w_kv_b"], np.float32)[0],
        "w_mla_o": np.asarray(inputs["w_mla_o"], np.float32)[0],
        "w_out": np.asarray(inputs["w_out"], np.float32)[0],
        "ident": np.eye(128, dtype=np.float32),
        "latgT": np.ascontiguousarray(np.concatenate([
            np.asarray(inputs["q_a_norm"], np.float32)[0].reshape(4, 128).T,
            np.asarray(inputs["kv_a_norm"], np.float32)[0].reshape(4, 128).T], axis=1)),
        "pscT": np.ascontiguousarray(np.asarray(inputs["pool_scale"], np.float32)[0].reshape(8, 128).T),
        "tri": np.triu(np.ones((128, 128), np.float32)),
    }
    inv = (10000.0 ** (-np.arange(0, ROPE, 2, dtype=np.float32) / ROPE)).astype(np.float32)
    in_maps = []
    tile_maps = []
    for core in range(8):
        b, c = core // 2, core % 2
        own = [2 * i + c for i in range(16)]
        if c == 1:
            oth = [2 * i for i in range(16)]
        else:
            oth = [31] + [2 * i - 1 for i in range(1, 16)]
        order = own + oth
        xt = x[b].reshape(32, 128, D)[order].reshape(SEQ, D)
        pos = np.concatenate([np.arange(NMETA)] + [NMETA + j * 128 + np.arange(128) for j in order]).astype(np.float32)
        ang = pos[None, :] * np.concatenate([inv, inv])[:, None]
        sel = np.zeros((128, 4), np.float32)
        if c == 0:
            sel[:, 0] = 1.0
            sel[:, 2] = -30000.0
        else:
            sel[:, 1] = 1.0
        m = dict(common)
        m["xs"] = np.ascontiguousarray(xt)
        m["cosT"] = np.cos(ang).astype(np.float32)
        m["sinT"] = np.sin(ang).astype(np.float32)
        m["sel"] = sel
        in_maps.append(m)
        tile_maps.append(own)
    return in_maps, tile_maps


_NC_CACHE = {}


def kernel(**inputs):
    in_maps, tile_maps = _prep_inputs(inputs)
    if "nc" not in _NC_CACHE:
        _NC_CACHE["nc"] = build_program()
    nc = _NC_CACHE["nc"]
    res = run_bass_kernel_spmd(nc, in_maps, core_ids=list(range(8)))
    out = np.empty((4, SEQ, D), np.float32)
    for core in range(8):
        b = core // 2
        o = res.results[core]["out"].reshape(16, 128, D)
        for i, j in enumerate(tile_maps[core]):
            out[b, j * 128:(j + 1) * 128] = o[i]
    return out
```

```python
import contextlib
import numpy as np
import concourse.bass as bass
import concourse.mybir as mybir
from concourse.bass_utils import run_bass_kernel_spmd

F32 = mybir.dt.float32
BF16 = mybir.dt.bfloat16
AF = mybir.ActivationFunctionType
ALU = mybir.AluOpType

D = 2048
SEQ = 4096
NMETA = 16
DFF = 5632
NFC = DFF // 128
QL = 512
KVL = 512
ROPE = 64
NOPE = 128
VD = 128
NH = 16
QKD = NOPE + ROPE
POOLW = 1024
IN_COLS = 6208
EPS = 1e-6
SCALE = QKD ** -0.5
NT_OWN = 16
NSLOT = 32
LK = NMETA + SEQ
ENGS = ("pe", "act", "dve", "pool", "sp")


class Res:
    __slots__ = ("name", "w_eng", "w_dma", "r_eng", "r_dma")

    def __init__(self, name=""):
        self.name = name
        self.w_eng = {}
        self.w_dma = []
        self.r_eng = {}
        self.r_dma = []


class Prog:
    NRING = 12

    def __init__(self, nc, es):
        self.nc = nc
        self.lists = {e: [] for e in ENGS}
        self.cnt = {e: 0 for e in ENGS}
        self.known = {e: {e2: 0 for e2 in ENGS} for e in ENGS}
        self.kdma = {e: {} for e in ENGS}
        self.snaps = {e: [None] for e in ENGS}
        self.sem = {e: es.enter_context(nc.semaphore("c_" + e)) for e in ENGS}
        self.ring = {}
        self.ring_val = {}
        self.ring_pos = {}
        for q in ("sp", "pool"):
            self.ring[q] = [es.enter_context(nc.semaphore(f"d_{q}{i}")) for i in range(self.NRING)]
            self.ring_val[q] = [0] * self.NRING
            self.ring_pos[q] = 0

    def _deps(self, eng, reads, writes):
        waits_e = {}
        waits_d = {}

        def need_e(e2, idx, raw):
            if e2 == eng and eng == "pe":
                return
            if self.known[eng][e2] >= idx:
                return
            if waits_e.get(e2, 0) < idx:
                waits_e[e2] = idx

        def need_d(tok):
            key, val = tok
            if self.kdma[eng].get(key, 0) >= val:
                return
            if waits_d.get(key, 0) < val:
                waits_d[key] = val

        for r in reads:
            for e2, idx in r.w_eng.items():
                need_e(e2, idx, True)
            for t in r.w_dma:
                need_d(t)
        for w in writes:
            for e2, idx in w.w_eng.items():
                need_e(e2, idx, False)
            for t in w.w_dma:
                need_d(t)
            for e2, idx in w.r_eng.items():
                need_e(e2, idx, False)
            for t in w.r_dma:
                need_d(t)
        out = []
        for e2, idx in waits_e.items():
            out.append((self.sem[e2], idx))
            kn = self.known[eng]
            if kn[e2] < idx:
                kn[e2] = idx
            sn = self.snaps[e2][idx]
            if sn is not None:
                for e3, v in zip(ENGS, sn):
                    if kn[e3] < v:
                        kn[e3] = v
        for key, val in waits_d.items():
            out.append((key, val))
            self.kdma[eng][key] = val
        return out

    def op(self, eng, fn, reads=(), writes=()):
        waits = self._deps(eng, reads, writes)
        self.cnt[eng] += 1
        idx = self.cnt[eng]
        self.snaps[eng].append(tuple(self.known[eng][e] for e in ENGS))
        self.lists[eng].append((waits, fn, self.sem[eng], 1))
        for w in writes:
            w.w_eng = {eng: idx}
            w.w_dma = []
            w.r_eng = {}
            w.r_dma = []
        for r in reads:
            if r.r_eng.get(eng, 0) < idx:
                r.r_eng[eng] = idx
        return idx

    def dma(self, q, out, in_, reads=(), writes=(), append=False):
        waits = [] if append else self._deps(q, reads, writes)
        pos = self.ring_pos[q]
        self.ring_pos[q] = (pos + 1) % self.NRING
        sem = self.ring[q][pos]
        prev = self.ring_val[q][pos]
        if prev and self.kdma[q].get(sem, 0) < prev:
            waits.append((sem, prev))
            self.kdma[q][sem] = prev
        val = prev + 16
        self.ring_val[q][pos] = val
        tok = (sem, val)
        self.lists[q].append((waits, lambda e: e.dma_start(out=out, in_=in_), sem, 16))
        for w in writes:
            if append:
                w.w_dma.append(tok)
                continue
            w.w_eng = {}
            w.w_dma = [tok]
            w.r_eng = {}
            w.r_dma = []
        for r in reads:
            r.r_dma.append(tok)
        return tok

    def barrier(self):
        tgt = dict(self.cnt)
        dm = []
        for q in self.ring:
            for s, v in zip(self.ring[q], self.ring_val[q]):
                if v:
                    dm.append((s, v))
        for e in ENGS:
            waits = []
            for e2 in ENGS:
                if e2 != e and self.known[e][e2] < tgt[e2]:
                    waits.append((self.sem[e2], tgt[e2]))
                    self.known[e][e2] = tgt[e2]
            for s, v in dm:
                if self.kdma[e].get(s, 0) < v:
                    waits.append((s, v))
                    self.kdma[e][s] = v
            if waits:
                self.lists[e].append((waits, None, None, 0))

    def emit(self, block):
        def mk(e):
            lst = self.lists[e]

            def body(eng):
                for waits, fn, sem, inc in lst:
                    for s, v in waits:
                        eng.wait_ge(s, v)
                    if fn is not None:
                        ins = fn(eng)
                        ins.then_inc(sem, inc)
            return body
        block.tensor(mk("pe"))
        block.scalar(mk("act"))
        block.vector(mk("dve"))
        block.gpsimd(mk("pool"))
        block.sync(mk("sp"))


def build_program(debug=None):
    nc = bass.Bass("TRN2", target_bir_lowering=False)
    es = contextlib.ExitStack()

    def din(name, shape, dt=F32):
        return nc.dram_tensor(name, list(shape), dt, kind="ExternalInput").ap()

    xs = din("xs", [SEQ, D])
    meta = din("meta", [NMETA, D])
    gains = din("gains", [6, D])
    gainsT = din("gainsT", [128, 6 * 16])
    w_gu = [din("ffn1_w_gu", [D, 2 * DFF]), din("ffn2_w_gu", [D, 2 * DFF])]
    w_dn = [din("ffn1_w_down", [DFF, D]), din("ffn2_w_down", [DFF, D])]
    w_in = din("w_in", [D, IN_COLS])
    pool_w = din("pool_w", [4, 256, 256])
    pool_scale = din("pool_scale", [POOLW])
    w_pool_o = din("w_pool_o", [POOLW, D])
    q_a_norm = din("q_a_norm", [QL])
    w_q_b = din("w_q_b", [QL, NH * QKD])
    kv_a_norm = din("kv_a_norm", [KVL])
    w_kv_b = din("w_kv_b", [KVL, NH * (NOPE + VD)])
    w_mla_o = din("w_mla_o", [NH * VD, D])
    w_out = din("w_out", [D, D])
    cosT = din("cosT", [ROPE, LK])
    sinT = din("sinT", [ROPE, LK])
    ident_d = din("ident", [128, 128])
    tri_d = din("tri", [128, 128])
    sel_d = din("sel", [128, 4])
    latgT = din("latgT", [128, 8])
    pscT = din("pscT", [128, 8])
    out_d = nc.dram_tensor("out", [NT_OWN * 128, D], F32, kind="ExternalOutput").ap()

    h1_d = nc.dram_tensor("h1_d", [SEQ + NMETA, D], F32).ap()
    kvnT_d = nc.dram_tensor("kvnT_d", [128, 4, LK], BF16).ap()
    kpeT_d = nc.dram_tensor("kpeT_d", [ROPE, LK], BF16).ap()
    cqnT_d = nc.dram_tensor("cqnT_d", [128, 4, NT_OWN * 128], BF16).ap()
    oT_d = nc.dram_tensor("oT_d", [128, NH, NT_OWN * 128], BF16).ap()
    h2_d = nc.dram_tensor("h2_d", [NT_OWN * 128, D], F32).ap()
    dbg = None
    if debug is not None:
        dbg = nc.dram_tensor("dbg", list(debug[:2]), F32, kind="ExternalOutput").ap()

    P = Prog(nc, es)
    stop_after = debug[2] if (debug is not None and len(debug) > 2) else "all"
    only = debug[3] if (debug is not None and len(debug) > 3) else None
    import os
    phases = set((os.environ.get("K_PHASES") or "a1,a2,b,c1,c2").split(","))
    psum = [es.enter_context(nc.psum_tensor(f"ps{i}", [128, 512], F32)) for i in range(8)]
    psr = [Res(f"ps{i}") for i in range(8)]

    ident = es.enter_context(nc.sbuf_tensor("identb", [128, 128], BF16))
    gT = es.enter_context(nc.sbuf_tensor("gT", [128, 6, 16], F32))
    r_const = Res("const")
    epsb = es.enter_context(nc.sbuf_tensor("epsb", [128, 1], F32))
    P.op("dve", lambda e: e.memset(epsb[:], EPS), writes=[r_const])
    P.dma("pool", ident[:], ident_d[:, :], writes=[r_const])
    P.dma("sp", gT[:].rearrange("p g k -> p (g k)"), gainsT[:, :], writes=[r_const])

    def wview(w, r0, nr, c0, ncol):
        return w[r0:r0 + nr, c0:c0 + ncol].rearrange("(kc p) f -> p kc f", p=128)

    def norm_transpose(src, r_src, nt, xnT, r_xnT, col0, gsrc, sq_junk, xs_bf, stat, r_tmp, nkc=16):
        width = nkc * 128
        P.op("act", lambda e: e.activation(out=sq_junk[:nt, :width], in_=src, func=AF.Square,
                                           accum_out=stat[:nt, 0:1]),
             reads=list(r_src), writes=[r_tmp])
        P.op("act", lambda e: e.activation(out=stat[:nt, 1:2], in_=stat[:nt, 0:1], func=AF.Sqrt,
                                           scale=1.0 / width, bias=epsb[:nt, 0:1]),
             reads=[r_tmp, r_const], writes=[r_tmp])
        P.op("dve", lambda e: e.reciprocal(out=stat[:nt, 2:3], in_=stat[:nt, 1:2]),
             reads=[r_tmp], writes=[r_tmp])
        P.op("act", lambda e: e.activation(out=xs_bf[:nt, :width], in_=src, func=AF.Copy,
                                           scale=stat[:nt, 2:3]),
             reads=list(r_src) + [r_tmp], writes=[r_tmp])
        for k0 in range(0, nkc, 8):
            kn = min(8, nkc - k0)
            pb = tr_banks[tr_state[0] % len(tr_banks)]
            tr_state[0] += 1
            pv = psum[pb][:].bitcast(BF16)

            def tr(e, k0=k0, kn=kn, pv=pv):
                ins = None
                for k in range(kn):
                    ins = e.transpose(pv[:, k * 128:k * 128 + nt], xs_bf[:nt, (k0 + k) * 128:(k0 + k + 1) * 128],
                                      ident[:nt, :nt])
                return ins
            P.op("pe", tr, reads=[r_tmp, r_const], writes=[psr[pb]])
            P.op("dve", lambda e, k0=k0, kn=kn, pv=pv: e.tensor_tensor(
                out=xnT[:, k0:k0 + kn, col0:col0 + nt],
                in0=pv[:, :kn * 128].rearrange("p (k t) -> p k t", k=kn)[:, :, :nt],
                in1=gsrc[:, k0:k0 + kn].unsqueeze(2).to_broadcast([128, kn, nt]),
                op=ALU.mult),
                reads=[psr[pb], r_const], writes=[r_xnT])

    tr_banks = [6, 7]
    tr_state = [0]

    GROUPS = [(0, 12), (12, 12), (24, 12), (36, 8)]

    def alloc_ffn(ph, ntile, pfx, T=528, nxst=1):
        def sb_t(name, shape, dt):
            return ph.enter_context(nc.sbuf_tensor(pfx + name, shape, dt))
        sb = {}
        sb["xnT"] = sb_t("xnT", [128, 16, T], BF16)
        sb["r_xnT"] = Res("xnT")
        sb["actT"] = [sb_t(f"actT{i}", [128, 12, T], BF16) for i in range(2)]
        sb["r_act"] = [Res("act0"), Res("act1")]
        sb["ysb"] = [sb_t(f"ysb{i}", [128, D], F32) for i in range(ntile)]
        sb["r_ysb"] = [[Res(f"y{i}q{q}") for q in range(4)] for i in range(ntile)]
        sb["wg"] = [sb_t(f"wg{i}", [128, 16, 256], BF16) for i in range(2)]
        sb["wu"] = [sb_t(f"wu{i}", [128, 16, 256], BF16) for i in range(2)]
        sb["r_wgu"] = [Res(f"wgu{i}") for i in range(2)]
        sb["wd"] = [sb_t(f"wd{i}", [128, 12, 512], BF16) for i in range(2)]
        sb["r_wd"] = [Res("wd0"), Res("wd1")]
        sb["sg"] = [sb_t(f"sg{i}", [128, 512], F32) for i in range(2)]
        sb["r_sg"] = [Res("sg0"), Res("sg1")]
        sb["junk"] = sb_t("junk", [128, D], BF16)
        sb["xs_bf"] = sb_t("xs_bf", [128, D], BF16)
        sb["stat"] = sb_t("stat", [128, 4], F32)
        sb["stat2"] = sb_t("stat2", [128, 4], F32)
        sb["r_tmp"] = Res("tmp")
        sb["r_tmp2"] = Res("tmp2")
        sb["grow"] = sb_t("grow", [128, D], F32)
        sb["r_grow"] = Res("grow")
        sb["xst"] = [sb_t(f"xst{i}", [128, D], F32) for i in range(nxst)]
        sb["r_xst"] = [Res(f"xst{i}") for i in range(nxst)]
        sb["cnt"] = dict(gu=0, bank=0, sg=0, wd=0, dn=0, xst=0)
        sb["gu_banks"] = [(0, 1), (2, 3)]
        sb["dn_banks"] = [4, 5, 6, 7]
        return sb

    def load_grow(sb, g_post):
        P.dma("sp", sb["grow"][:, :], gains[g_post:g_post + 1, :].partition_broadcast(128), writes=[sb["r_grow"]])
        P.op("dve", lambda e: e.tensor_scalar(out=sb["grow"][:, :], in0=sb["grow"][:, :], scalar1=0.5, scalar2=None,
                                              op0=ALU.mult), reads=[sb["r_grow"]], writes=[sb["r_grow"]])

    def ffn_pass(which, tiles, sb):
        wgu, wdn = w_gu[which], w_dn[which]
        g_pre = 0 if which == 0 else 4
        xnT, actT, ysb = sb["xnT"], sb["actT"], sb["ysb"]
        r_xnT = sb["r_xnT"]
        cnt = sb["cnt"]
        T = sum(t["nt"] for t in tiles)
        cols = []
        c = 0
        for t in tiles:
            cols.append(c)
            c += t["nt"]
        for ti, (t, c0) in enumerate(zip(tiles, cols)):
            nt = t["nt"]
            if t.get("x_ap") is None:
                t["load"](ysb[ti][:nt, :], sb["r_ysb"][ti])
                src, rs = ysb[ti][:nt, :], sb["r_ysb"][ti]
            else:
                src, rs = t["x_ap"], [t["r_x"]]
            norm_transpose(src, rs, nt, xnT, r_xnT, c0, gT[:, g_pre, :], sb["junk"], sb["xs_bf"],
                           sb["stat"], sb["r_tmp"])
        nb_ = -(-T // 512)
        bsz = -(-T // (nb_ * 16)) * 16
        nblocks = [(b0, min(bsz, T - b0)) for b0 in range(0, T, bsz)]
        for gi, (f0, gn) in enumerate(GROUPS):
            ab = gi % 2
            r_act = sb["r_act"][ab]
            for fp in range(gn // 2):
                st = cnt["gu"] % 2
                cnt["gu"] += 1
                wg_t, wu_t, r_w = sb["wg"][st], sb["wu"][st], sb["r_wgu"][st]
                fcol = (f0 + 2 * fp) * 128
                P.dma("pool", wg_t[:], wview(wgu, 0, D, fcol, 256), writes=[r_w])
                P.dma("pool", wu_t[:], wview(wgu, 0, D, DFF + fcol, 256), writes=[r_w], append=True)
                for sub in range(2):
                    fi = 2 * fp + sub
                    for (b0, bn) in nblocks:
                        bg, bu = sb["gu_banks"][cnt["bank"] % 2]
                        cnt["bank"] += 1

                        def mm(e, wt, bank, b0=b0, bn=bn, sub=sub):
                            ins = None
                            for kc in range(16):
                                ins = e.matmul(psum[bank][:, :bn], wt[:, kc, sub * 128:(sub + 1) * 128],
                                               xnT[:, kc, b0:b0 + bn], start=(kc == 0), stop=(kc == 15))
                            return ins
                        P.op("pe", lambda e, wt=wg_t, bank=bg, mm=mm: mm(e, wt, bank), reads=[r_w, r_xnT],
                             writes=[psr[bg]])
                        P.op("pe", lambda e, wt=wu_t, bank=bu, mm=mm: mm(e, wt, bank), reads=[r_w, r_xnT],
                             writes=[psr[bu]])
                        sgi = cnt["sg"] % 2
                        cnt["sg"] += 1
                        sg, r_sg = sb["sg"][sgi], sb["r_sg"][sgi]
                        P.op("act", lambda e, bg=bg, bn=bn, sg=sg: e.activation(
                            out=sg[:, :bn], in_=psum[bg][:, :bn], func=AF.Silu),
                            reads=[psr[bg]], writes=[r_sg])
                        P.op("dve", lambda e, bu=bu, b0=b0, bn=bn, sg=sg, fi=fi, ab=ab: e.tensor_tensor(
                            out=actT[ab][:, fi, b0:b0 + bn], in0=psum[bu][:, :bn], in1=sg[:, :bn], op=ALU.mult),
                            reads=[psr[bu], r_sg], writes=[r_act])
            for q in range(4):
                st = cnt["wd"] % 2
                cnt["wd"] += 1
                wd_t, r_wd = sb["wd"][st], sb["r_wd"][st]
                P.dma("pool", wd_t[:, :gn, :], wview(wdn, f0 * 128, gn * 128, q * 512, 512), writes=[r_wd])
                for ti, (t, c0) in enumerate(zip(tiles, cols)):
                    nt = t["nt"]
                    pd = sb["dn_banks"][cnt["dn"] % 4]
                    cnt["dn"] += 1

                    def mmd(e, pd=pd, c0=c0, nt=nt, wd_t=wd_t, ab=ab, gn=gn):
                        ins = None
                        for fi in range(gn):
                            ins = e.matmul(psum[pd][:nt, :], actT[ab][:, fi, c0:c0 + nt], wd_t[:, fi, :],
                                           start=(fi == 0), stop=(fi == gn - 1))
                        return ins
                    P.op("pe", mmd, reads=[r_act, r_wd], writes=[psr[pd]])
                    r_y = sb["r_ysb"][ti][q]
                    dst = ysb[ti][:nt, q * 512:(q + 1) * 512]
                    if gi == 0:
                        P.op("act", lambda e, dst=dst, pd=pd, nt=nt: e.activation(out=dst, in_=psum[pd][:nt, :],
                                                                                 func=AF.Copy),
                             reads=[psr[pd]], writes=[r_y])
                    else:
                        P.op("dve", lambda e, dst=dst, pd=pd, nt=nt: e.tensor_tensor(
                            out=dst, in0=psum[pd][:nt, :], in1=dst, op=ALU.add),
                            reads=[psr[pd], r_y], writes=[r_y])
        for ti, t in enumerate(tiles):
            nt = t["nt"]
            stat = sb["stat2"]
            r_t2 = sb["r_tmp2"]
            ry = sb["r_ysb"][ti]
            if t.get("x_ap") is None:
                k = cnt["xst"] % len(sb["xst"])
                cnt["xst"] += 1
                t["load"](sb["xst"][k][:nt, :], [sb["r_xst"][k]])
                xap, rx = sb["xst"][k][:nt, :], sb["r_xst"][k]
            else:
                xap, rx = t["x_ap"], t["r_x"]
            P.op("act", lambda e, ti=ti, nt=nt: e.activation(out=sb["junk"][:nt, :], in_=ysb[ti][:nt, :],
                                                            func=AF.Square, accum_out=stat[:nt, 0:1]),
                 reads=ry, writes=[r_t2])
            P.op("act", lambda e, nt=nt: e.activation(out=stat[:nt, 1:2], in_=stat[:nt, 0:1], func=AF.Sqrt,
                                                      scale=1.0 / D, bias=epsb[:nt, 0:1]),
                 reads=[r_t2, r_const], writes=[r_t2])
            P.op("dve", lambda e, nt=nt: e.reciprocal(out=stat[:nt, 2:3], in_=stat[:nt, 1:2]),
                 reads=[r_t2], writes=[r_t2])
            P.op("dve", lambda e, ti=ti, nt=nt: e.scalar_tensor_tensor(
                out=ysb[ti][:nt, :], in0=ysb[ti][:nt, :], scalar=stat[:nt, 2:3], in1=sb["grow"][:nt, :],
                op0=ALU.mult, op1=ALU.mult),
                reads=[r_t2, sb["r_grow"]] + ry, writes=ry)
            P.op("dve", lambda e, ti=ti, nt=nt, xap=xap: e.tensor_tensor(out=ysb[ti][:nt, :], in0=ysb[ti][:nt, :],
                                                                         in1=xap, op=ALU.add),
                 reads=ry + [rx], writes=ry)
            t["done"](ti, ysb[ti][:nt, :], ry)

    r_h1d = [Res(f"h1d{i}") for i in range(NSLOT + 1)]
    with contextlib.ExitStack() as ph:
        sb = alloc_ffn(ph, 6, 'a1_', T=768)
        load_grow(sb, 1)
        pass_slots = [list(range(0, 6)), list(range(6, 12)), list(range(12, 17)), list(range(17, 22)),
                      list(range(22, 27)), list(range(27, 32)) + [NSLOT]]
        if debug is not None and stop_after in ('a1', 'a2'):
            pass_slots = [[0, 1, 2, 3, NSLOT]]
        if only is not None or "a1" not in phases:
            pass_slots = []
        for slots_ in pass_slots:
            tiles = []
            for s in slots_:
                if s == NSLOT:
                    tiles.append(dict(nt=NMETA, slot=NSLOT, load=(lambda dst, rd: P.dma(
                        "sp", dst, meta[:, :], writes=rd))))
                else:
                    tiles.append(dict(nt=128, slot=s, load=(lambda dst, rd, s=s: P.dma(
                        "sp", dst, xs[s * 128:(s + 1) * 128, :], writes=rd))))

            def done(ti, y_ap, ry, tiles=tiles):
                s = tiles[ti]["slot"]
                nt = tiles[ti]["nt"]
                P.dma("sp", h1_d[s * 128:s * 128 + nt, :], y_ap, reads=ry, writes=[r_h1d[s]])
                if dbg is not None and stop_after == "a1":
                    P.dma("sp", dbg[ti * 128:ti * 128 + nt, :], y_ap, reads=ry, writes=[Res()])
            for t in tiles:
                t["done"] = done
            ffn_pass(0, tiles, sb)
        P.barrier()


    def dump(ap_sb, reads, r0, nrow, c0, ncol):
        P.dma("sp", dbg[r0:r0 + nrow, c0:c0 + ncol], ap_sb, reads=reads, writes=[Res()])

    r_kvn_d = [Res(f"kvnd{i}") for i in range(NSLOT + 1)]
    r_kpe_d = [Res(f"kped{i}") for i in range(NSLOT + 1)]
    r_cqn_d = [Res(f"cqnd{i}") for i in range(NT_OWN)]
    if stop_after not in ("a1",) and only is None and "a2" in phases:
      with contextlib.ExitStack() as ph:
        def sb_t(name, shape, dt):
            return ph.enter_context(nc.sbuf_tensor('a2_' + name, shape, dt))
        w_lat = sb_t("w_lat", [128, 16, 1152], BF16)
        r_wlat = Res("wlat")
        P.dma("pool", w_lat[:, :, 0:544], wview(w_in, 0, D, 1024, 544), writes=[r_wlat])
        P.dma("pool", w_lat[:, :, 544:1088], wview(w_in, 0, D, 1568, 544), writes=[r_wlat], append=True)
        P.op("dve", lambda e: e.tensor_scalar(out=w_lat[:, :, 1088:1120], in0=w_lat[:, :, 1056:1088], scalar1=-1.0,
                                              scalar2=None, op0=ALU.mult), reads=[r_wlat], writes=[r_wlat])
        P.op("dve", lambda e: e.tensor_copy(out=w_lat[:, :, 1120:1152], in_=w_lat[:, :, 1024:1056]),
             reads=[r_wlat], writes=[r_wlat])
        cos_sb = sb_t("cos_sb", [ROPE, LK], F32)
        sin_sb = sb_t("sin_sb", [ROPE, LK], F32)
        latg = sb_t("latg", [128, 8], F32)
        r_tab = Res("tab")
        P.dma("sp", cos_sb[:, :], cosT[:, :], writes=[r_tab])
        P.dma("sp", sin_sb[:, :], sinT[:, :], writes=[r_tab], append=True)
        P.dma("sp", latg[:, :], latgT[:, :], writes=[r_tab], append=True)
        hbuf = [sb_t(f"hbuf{i}", [128, D], F32) for i in range(2)]
        r_hbuf = [Res("hb0"), Res("hb1")]
        xn2 = [sb_t(f"xn2_{i}", [128, 16, 128], BF16) for i in range(2)]
        r_xn2 = [Res("xn2_0"), Res("xn2_1")]
        junk = sb_t("junk", [128, D], BF16)
        xs_bf = sb_t("xs_bf", [128, D], BF16)
        stat = sb_t("stat", [128, 4], F32)
        r_tmp = Res("tmp")
        junk2 = sb_t("junk2", [128, 512], BF16)
        xs_bf2 = sb_t("xs_bf2", [128, 512], BF16)
        stat2 = sb_t("stat2", [128, 4], F32)
        r_tmp2 = Res("tmp2")
        t1 = sb_t("t1", [ROPE, 128], F32)
        t2 = sb_t("t2", [ROPE, 128], F32)
        r_t12 = Res("t12")
        kpe_t = [sb_t(f"kpe_t{i}", [ROPE, 128], BF16) for i in range(2)]
        r_kpe_t = [Res("kpet0"), Res("kpet1")]
        kvn_t = [sb_t(f"kvn_t{i}", [128, 4, 128], BF16) for i in range(2)]
        r_kvn_t = [Res("kvnt0"), Res("kvnt1")]
        cqn_t = [sb_t(f"cqn_t{i}", [128, 4, 128], BF16) for i in range(2)]
        r_cqn_t = [Res("cqnt0"), Res("cqnt1")]
        ntile_a2 = NSLOT + 1 if debug is None or stop_after != "a2" else 5
        for tix in range(ntile_a2):
            pb = tix % 2
            if tix == 0:
                nt, row0, kcol, own, s = NMETA, SEQ, 0, False, NSLOT
            else:
                s = tix - 1
                nt, row0, kcol, own = 128, s * 128, NMETA + s * 128, s < NT_OWN
            hb = hbuf[pb]
            P.dma("sp", hb[:nt, :], h1_d[row0:row0 + nt, :], reads=[r_h1d[s]], writes=[r_hbuf[pb]])
            norm_transpose(hb[:nt, :], [r_hbuf[pb]], nt, xn2[pb], r_xn2[pb], 0, gT[:, 2, :], junk, xs_bf, stat, r_tmp)

            def mm_tok(e, bank, c0, nt=nt, pb=pb):
                ins = None
                for kc in range(16):
                    ins = e.matmul(psum[bank][:nt, :], xn2[pb][:, kc, :nt], w_lat[:, kc, c0:c0 + 512],
                                   start=(kc == 0), stop=(kc == 15))
                return ins

            def mm_feat(e, bank, c0, nt=nt, pb=pb):
                ins = None
                for kc in range(16):
                    ins = e.matmul(psum[bank][:ROPE, :nt], w_lat[:, kc, c0:c0 + ROPE], xn2[pb][:, kc, :nt],
                                   start=(kc == 0), stop=(kc == 15))
                return ins
            P.op("pe", lambda e, f=mm_tok: f(e, 1, 512), reads=[r_xn2[pb], r_wlat], writes=[psr[1]])
            P.op("pe", lambda e, f=mm_feat: f(e, 2, 1024), reads=[r_xn2[pb], r_wlat], writes=[psr[2]])
            P.op("pe", lambda e, f=mm_feat: f(e, 3, 1088), reads=[r_xn2[pb], r_wlat], writes=[psr[3]])
            if own:
                P.op("pe", lambda e, f=mm_tok: f(e, 0, 0), reads=[r_xn2[pb], r_wlat], writes=[psr[0]])
            P.op("dve", lambda e, nt=nt, kcol=kcol: e.tensor_tensor(out=t1[:, :nt], in0=psum[2][:ROPE, :nt],
                                                                  in1=cos_sb[:, kcol:kcol + nt], op=ALU.mult),
                 reads=[psr[2], r_tab], writes=[r_t12])
            P.op("dve", lambda e, nt=nt, kcol=kcol: e.tensor_tensor(out=t2[:, :nt], in0=psum[3][:ROPE, :nt],
                                                                  in1=sin_sb[:, kcol:kcol + nt], op=ALU.mult),
                 reads=[psr[3], r_tab], writes=[r_t12])
            P.op("dve", lambda e, nt=nt, pb=pb: e.tensor_tensor(out=kpe_t[pb][:, :nt], in0=t1[:, :nt], in1=t2[:, :nt],
                                                              op=ALU.add),
                 reads=[r_t12], writes=[r_kpe_t[pb]])
            P.dma("sp", kpeT_d[:, kcol:kcol + nt], kpe_t[pb][:, :nt], reads=[r_kpe_t[pb]], writes=[r_kpe_d[tix]])
            norm_transpose(psum[1][:nt, :], [psr[1]], nt, kvn_t[pb], r_kvn_t[pb], 0, latg[:, 4:8], junk2, xs_bf2,
                           stat2, r_tmp2, nkc=4)
            P.dma("sp", kvnT_d[:, :, kcol:kcol + nt], kvn_t[pb][:, :, :nt], reads=[r_kvn_t[pb]], writes=[r_kvn_d[tix]])
            if stop_after == "a2":
                for kc in range(4):
                    P.dma("pool", dbg[0:128, kc * 1024 + kcol:kc * 1024 + kcol + nt], kvn_t[pb][:, kc, :nt],
                          reads=[r_kvn_t[pb]], writes=[Res()])
                P.dma("pool", dbg[128:192, kcol:kcol + nt], kpe_t[pb][:, :nt], reads=[r_kpe_t[pb]], writes=[Res()])
                if own:
                    for kc in range(4):
                        P.dma("pool", dbg[256:384, kc * 1024 + s * 128:kc * 1024 + s * 128 + nt], cqn_t[pb][:, kc, :nt],
                              reads=[r_cqn_t[pb]], writes=[Res()])
            if own:
                norm_transpose(psum[0][:nt, :], [psr[0]], nt, cqn_t[pb], r_cqn_t[pb], 0, latg[:, 0:4], junk2, xs_bf2,
                               stat2, r_tmp2, nkc=4)
                P.dma("sp", cqnT_d[:, :, s * 128:(s + 1) * 128], cqn_t[pb][:, :, :], reads=[r_cqn_t[pb]],
                      writes=[r_cqn_d[s]])
        P.barrier()

    r_oT_d = [Res(f"oTd{h}") for h in range(NH)]
    if stop_after not in ("a1", "a2", "a2f") and "b" in phases:
      with contextlib.ExitStack() as ph:
        def sb_t(name, shape, dt):
            return ph.enter_context(nc.sbuf_tensor('b_' + name, shape, dt))
        NQ = NT_OWN * 128
        kvnT = sb_t("kvnT", [128, 4, LK], BF16)
        kpeT = sb_t("kpeT", [ROPE, LK], BF16)
        cqnT = sb_t("cqnT", [128, 4, NQ], BF16)
        cosq = sb_t("cosq", [ROPE, NQ], F32)
        sinq = sb_t("sinq", [ROPE, NQ], F32)
        tri = sb_t("tri", [128, 128], BF16)
        ones = sb_t("ones", [128, 128], BF16)
        sel = sb_t("sel", [128, 4], F32)
        r_lat = Res("lat")
        P.dma("sp", kvnT[:, :, :], kvnT_d[:, :, :], reads=r_kvn_d, writes=[r_lat])
        P.dma("sp", kpeT[:, :], kpeT_d[:, :], reads=r_kpe_d, writes=[r_lat], append=True)
        P.dma("sp", cqnT[:, :, :], cqnT_d[:, :, :], reads=r_cqn_d, writes=[r_lat], append=True)
        P.dma("sp", cosq[:, :], cosT[:, NMETA:NMETA + NQ], writes=[r_lat], append=True)
        P.dma("sp", sinq[:, :], sinT[:, NMETA:NMETA + NQ], writes=[r_lat], append=True)
        P.dma("sp", sel[:, :], sel_d[:, :], writes=[r_lat], append=True)
        r_msk = Res("msk")
        P.dma("pool", tri[:, :], tri_d[:, :], writes=[r_msk])
        P.op("dve", lambda e: e.memset(ones[:], 1.0), writes=[r_msk])
        hbufs = []
        for i in range(2):
            hbufs.append(dict(
                wkv=sb_t(f"wkv{i}", [128, 4, 256], BF16), wq=sb_t(f"wq{i}", [128, 4, 256], BF16),
                KT=sb_t(f"KT{i}", [128, LK], BF16), V=sb_t(f"V{i}", [128, NSLOT + 1, 128], BF16),
                qT=sb_t(f"qT{i}", [128, NQ], BF16), qpe=sb_t(f"qpe{i}", [ROPE, NQ], BF16),
                oT=sb_t(f"oT{i}", [128, NQ], BF16),
                r_w=Res(f"hw{i}"), r_KT=Res(f"KT{i}"), r_V=Res(f"V{i}"), r_qT=Res(f"qT{i}"), r_qpe=Res(f"qpe{i}"),
                r_oT=Res(f"oT{i}")))
        pT = [sb_t(f"pT{i}", [128, 512], BF16) for i in range(4)]
        r_pT = [Res(f"pT{i}") for i in range(4)]
        SBK = [0, 1, 2, 7]
        rec = sb_t("rec", [128, 512], F32)
        r_rec = Res("rec")
        rt1 = sb_t("rt1", [ROPE, 512], F32)
        rt2 = sb_t("rt2", [ROPE, 512], F32)
        r_rt = Res("rt")
        pj = [6, 7, 0, 1, 2, 3, 4]
        pjc = [0]

        def pjbank():
            b = pj[pjc[0] % len(pj)]
            pjc[0] += 1
            return b

        def proj(h, hb):
            B = hbufs[hb]
            wkv, wq = B["wkv"], B["wq"]
            P.dma("pool", wkv[:, :, :], wview(w_kv_b, 0, KVL, h * 256, 256), writes=[B["r_w"]])
            P.dma("pool", wq[:, :, 0:QKD], wview(w_q_b, 0, QL, h * QKD, QKD), writes=[B["r_w"]], append=True)
            P.op("dve", lambda e: e.tensor_scalar(out=wq[:, :, 192:224], in0=wq[:, :, 160:192], scalar1=-1.0,
                                                  scalar2=None, op0=ALU.mult), reads=[B["r_w"]], writes=[B["r_w"]])
            P.op("dve", lambda e: e.tensor_copy(out=wq[:, :, 224:256], in_=wq[:, :, 128:160]),
                 reads=[B["r_w"]], writes=[B["r_w"]])
            for b0 in range(0, LK, 512):
                bn = min(512, LK - b0)
                bk = pjbank()

                def mmk(e, bk=bk, b0=b0, bn=bn):
                    ins = None
                    for kc in range(4):
                        ins = e.matmul(psum[bk][:, :bn], wkv[:, kc, 0:128], kvnT[:, kc, b0:b0 + bn],
                                       start=(kc == 0), stop=(kc == 3))
                    return ins
                P.op("pe", mmk, reads=[B["r_w"], r_lat], writes=[psr[bk]])
                P.op("act", lambda e, bk=bk, b0=b0, bn=bn: e.activation(out=B["KT"][:, b0:b0 + bn],
                                                                       in_=psum[bk][:, :bn], func=AF.Copy),
                     reads=[psr[bk]], writes=[B["r_KT"]])
            for g0 in range(0, NSLOT + 1, 4):
                gn = min(4, NSLOT + 1 - g0)
                bk = pjbank()

                def mmv(e, bk=bk, g0=g0, gn=gn):
                    ins = None
                    for j in range(gn):
                        kt = g0 + j
                        nk = NMETA if kt == 0 else 128
                        kc0 = 0 if kt == 0 else NMETA + (kt - 1) * 128
                        for kc in range(4):
                            ins = e.matmul(psum[bk][:nk, j * 128:(j + 1) * 128], kvnT[:, kc, kc0:kc0 + nk],
                                           wkv[:, kc, 128:256], start=(kc == 0), stop=(kc == 3))
                    return ins
                P.op("pe", mmv, reads=[B["r_w"], r_lat], writes=[psr[bk]])
                if g0 == 0:
                    P.op("dve", lambda e, bk=bk: e.tensor_copy(out=B["V"][:NMETA, 0, :], in_=psum[bk][:NMETA, 0:128]),
                         reads=[psr[bk]], writes=[B["r_V"]])
                    P.op("dve", lambda e, bk=bk, gn=gn: e.tensor_copy(
                        out=B["V"][:, 1:gn, :], in_=psum[bk][:, 128:gn * 128].rearrange("p (k c) -> p k c", c=128)),
                        reads=[psr[bk]], writes=[B["r_V"]])
                else:
                    P.op("dve", lambda e, bk=bk, g0=g0, gn=gn: e.tensor_copy(
                        out=B["V"][:, g0:g0 + gn, :], in_=psum[bk][:, :gn * 128].rearrange("p (k c) -> p k c", c=128)),
                        reads=[psr[bk]], writes=[B["r_V"]])
            for qb in range(NQ // 512):
                bk = pjbank()

                def mmq(e, bk=bk, qb=qb):
                    ins = None
                    for kc in range(4):
                        ins = e.matmul(psum[bk][:, :], wq[:, kc, 0:128], cqnT[:, kc, qb * 512:(qb + 1) * 512],
                                       start=(kc == 0), stop=(kc == 3))
                    return ins
                P.op("pe", mmq, reads=[B["r_w"], r_lat], writes=[psr[bk]])
                P.op("act", lambda e, bk=bk, qb=qb: e.activation(out=B["qT"][:, qb * 512:(qb + 1) * 512],
                                                               in_=psum[bk][:, :], func=AF.Copy),
                     reads=[psr[bk]], writes=[B["r_qT"]])
                ba, bb = pjbank(), pjbank()

                def mmp(e, bank, c0, qb=qb):
                    ins = None
                    for kc in range(4):
                        ins = e.matmul(psum[bank][:ROPE, :], wq[:, kc, c0:c0 + ROPE], cqnT[:, kc, qb * 512:(qb + 1) * 512],
                                       start=(kc == 0), stop=(kc == 3))
                    return ins
                P.op("pe", lambda e, f=mmp, ba=ba: f(e, ba, 128), reads=[B["r_w"], r_lat], writes=[psr[ba]])
                P.op("pe", lambda e, f=mmp, bb=bb: f(e, bb, 192), reads=[B["r_w"], r_lat], writes=[psr[bb]])
                P.op("dve", lambda e, ba=ba, qb=qb: e.tensor_tensor(out=rt1[:, :], in0=psum[ba][:ROPE, :],
                                                                  in1=cosq[:, qb * 512:(qb + 1) * 512], op=ALU.mult),
                     reads=[psr[ba], r_lat], writes=[r_rt])
                P.op("dve", lambda e, bb=bb, qb=qb: e.tensor_tensor(out=rt2[:, :], in0=psum[bb][:ROPE, :],
                                                                  in1=sinq[:, qb * 512:(qb + 1) * 512], op=ALU.mult),
                     reads=[psr[bb], r_lat], writes=[r_rt])
                P.op("dve", lambda e, qb=qb: e.tensor_tensor(out=B["qpe"][:, qb * 512:(qb + 1) * 512], in0=rt1[:, :],
                                                            in1=rt2[:, :], op=ALU.add),
                     reads=[r_rt], writes=[B["r_qpe"]])

        sbank = [0]
        qcount = [0]

        def attn(h, hb):
            B = hbufs[hb]
            for qb in range(NQ // 512):
                tl = [(0, NMETA, 0, 0, False, False)]
                for ip in range(4 * qb + 4):
                    q0 = max(0, ip - 4 * qb) * 128
                    tl.append((1 + ip, 128, NMETA + ip * 128, q0, ip >= 4 * qb, False))
                    tl.append((1 + NT_OWN + ip, 128, NMETA + (NT_OWN + ip) * 128, q0, False, ip == 0))
                tl = [tl[0]] + [x for x in tl[1:] if x[3] > 0] + [x for x in tl[1:] if x[3] == 0]
                ob = 3 + (qcount[0] % 2)
                db = 5 + (qcount[0] % 2)
                qcount[0] += 1
                n = len(tl)
                sb_of = {}

                def qk(j):
                    kt, nk, kc0, q0, diag, b16 = tl[j]
                    s = sbank[0] % 4
                    sbank[0] += 1
                    sb_of[j] = s
                    sk = SBK[s]

                    def f(e, sk=sk, nk=nk, kc0=kc0, q0=q0, qb=qb):
                        e.matmul(psum[sk][:nk, q0:512], B["KT"][:, kc0:kc0 + nk],
                                 B["qT"][:, qb * 512 + q0:(qb + 1) * 512], start=True, stop=False)
                        return e.matmul(psum[sk][:nk, q0:512], kpeT[:, kc0:kc0 + nk],
                                        B["qpe"][:, qb * 512 + q0:(qb + 1) * 512], start=False, stop=True)
                    P.op("pe", f, reads=[B["r_KT"], B["r_qT"], B["r_qpe"], r_lat], writes=[psr[sk]])
                    bias = sel[:nk, 2:3] if b16 else 0.0
                    P.op("act", lambda e, s=s, sk=sk, nk=nk, q0=q0, bias=bias: e.activation(
                        out=pT[s][:nk, q0:512], in_=psum[sk][:nk, q0:512], func=AF.Exp, bias=bias, scale=SCALE),
                        reads=[psr[sk], r_lat], writes=[r_pT[s]])
                    if diag:
                        P.op("dve", lambda e, s=s, q0=q0: e.tensor_tensor(out=pT[s][:, q0:q0 + 128],
                                                                        in0=pT[s][:, q0:q0 + 128], in1=tri[:, :],
                                                                        op=ALU.mult),
                             reads=[r_pT[s], r_msk], writes=[r_pT[s]])

                def pv(j):
                    kt, nk, kc0, q0, diag, b16 = tl[j]
                    s = sb_of[j]

                    def f(e, s=s, kt=kt, nk=nk, q0=q0, j=j, ob=ob, n=n, db=db):
                        e.matmul(psum[ob][:, q0:512], B["V"][:nk, kt, :], pT[s][:nk, q0:512],
                                 start=(j == 0), stop=(j == n - 1))
                        return e.matmul(psum[db][:, q0:512], ones[:nk, :], pT[s][:nk, q0:512],
                                        start=(j == 0), stop=(j == n - 1))
                    P.op("pe", f, reads=[B["r_V"], r_pT[s], r_msk], writes=[psr[ob], psr[db]])
                qk(0)
                qk(1)
                qk(2)
                for j in range(n):
                    if j + 3 < n:
                        qk(j + 3)
                    pv(j)
                P.op("dve", lambda e, db=db: e.reciprocal(out=rec[:, :], in_=psum[db][:, :]), reads=[psr[db]],
                     writes=[r_rec])
                P.op("dve", lambda e, ob=ob, qb=qb: e.tensor_tensor(out=B["oT"][:, qb * 512:(qb + 1) * 512],
                                                                  in0=psum[ob][:, :], in1=rec[:, :], op=ALU.mult),
                     reads=[psr[ob], r_rec], writes=[B["r_oT"]])
            P.dma("sp", oT_d[:, h, :], B["oT"][:, :], reads=[B["r_oT"]], writes=[r_oT_d[h]])
            if stop_after == "b":
                P.dma("pool", dbg[h * 128:(h + 1) * 128, :], B["oT"][:, :], reads=[B["r_oT"]], writes=[Res()])

        nheads = NH if stop_after != "b" else (2 if only is None else 1)
        if debug is not None and len(debug) > 4:
            nheads = debug[4]
        import os
        if not os.environ.get("K_PIPE"):
            for h in range(nheads):
                proj(h, h % 2)
                attn(h, h % 2)
        else:
            proj(0, 0)
            for h in range(nheads):
                if h + 1 < nheads:
                    proj(h + 1, (h + 1) % 2)
                attn(h, h % 2)
        P.barrier()

    r_h2d = [Res(f"h2d{i}") for i in range(NT_OWN)]
    if stop_after not in ("a1", "a2", "a2f", "b") and "c1" in phases:
      with contextlib.ExitStack() as ph:
        def sb_t(name, shape, dt):
            return ph.enter_context(nc.sbuf_tensor('c1_' + name, shape, dt))
        xn2m = sb_t("xn2m", [128, 16, 512], BF16)
        r_xn2m = Res("xn2m")
        xn2h = sb_t("xn2h", [128, 16, 64], BF16)
        r_xn2h = Res("xn2h")
        hst = sb_t("hst", [128, D], F32)
        r_hst = Res("hst")
        halo = sb_t("halo", [NMETA, D], F32)
        r_halo = Res("halo")
        junk = sb_t("junk", [128, D], BF16)
        xs_bf = sb_t("xs_bf", [128, D], BF16)
        stat = sb_t("stat", [128, 4], F32)
        r_tmp = Res("tmp")
        stat2 = sb_t("stat2", [128, 4], F32)
        r_tmp2 = Res("tmp2")
        uext = sb_t("uext", [128, 2, 576], F32)
        r_uext = [Res("uext0"), Res("uext1")]
        ptmp = [sb_t(f"ptmp{i}", [128, 576], F32) for i in range(2)]
        r_ptmp = Res("ptmp")
        P.op("dve", lambda e: e.memset(ptmp[0][:, :], 0.0), writes=[r_ptmp])
        P.op("dve", lambda e: e.memset(ptmp[1][:, :], 0.0), writes=[r_ptmp])
        dext = sb_t("dext", [128, 2, 576], BF16)
        r_dext = Res("dext")
        ypT = sb_t("ypT", [128, 8, 512], BF16)
        r_ypT = Res("ypT")
        oTs = sb_t("oTs", [128, NH, 512], BF16)
        r_oTs = Res("oTs")
        yT = sb_t("yT", [128, 16, 512], BF16)
        r_yT = Res("yT")
        stg = [sb_t(f"stg{i}", [128, 14336], BF16) for i in range(2)]
        r_stg = [Res("stg0"), Res("stg1")]
        wst = []
        for i in range(2):
            wst.append(dict(gp=stg[i][:, 0:4096].rearrange("p (k c) -> p k c", c=256),
                            mo=stg[i][:, 4096:8192].rearrange("p (k c) -> p k c", c=256),
                            gm=stg[i][:, 8192:12288].rearrange("p (k c) -> p k c", c=256),
                            po=stg[i][:, 12288:14336].rearrange("p (k c) -> p k c", c=256),
                            r=r_stg[i]))
        wup = [stg[i][:, 0:4096].rearrange("p (k c) -> p k c", c=256) for i in range(2)]
        r_wup = r_stg
        wo = wup
        r_wo = r_stg
        sgt = [sb_t(f"sgt{i}", [128, 512], F32) for i in range(4)]
        r_sgt = [Res(f"sgt{i}") for i in range(4)]
        m_sb = [sb_t(f"m_sb{i}", [128, D], F32) for i in range(4)]
        r_m = [Res(f"m{i}") for i in range(4)]
        growm = sb_t("growm", [128, D], F32)
        pw = sb_t("pw", [128, 4, 2, 256], BF16)
        psc = sb_t("psc", [128, 8], F32)
        selc = sb_t("selc", [128, 4], F32)
        r_cc = Res("cc")
        P.dma("sp", growm[:, :], gains[3:4, :].partition_broadcast(128), writes=[r_cc])
        P.dma("sp", psc[:, :], pscT[:, :], writes=[r_cc], append=True)
        P.dma("sp", selc[:, :], sel_d[:, :], writes=[r_cc], append=True)
        P.dma("pool", pw[:, :, :, :], pool_w.rearrange("g (k p) o -> p g k o", p=128), writes=[r_cc], append=True)
        stc = [0]
        npass_c = 4 if stop_after != "c1" else 1
        for ps_i in range(npass_c):
            own_tiles = list(range(ps_i * 4, ps_i * 4 + 4))
            P.dma("sp", oTs[:, :, :], oT_d[:, :, ps_i * 512:(ps_i + 1) * 512], reads=r_oT_d, writes=[r_oTs])
            for t, i in enumerate(own_tiles):
                P.dma("sp", m_sb[t][:, :], h1_d[i * 128:(i + 1) * 128, :], reads=[r_h1d[i]], writes=[r_m[t]])
                norm_transpose(m_sb[t][:, :], [r_m[t]], 128, xn2m, r_xn2m, t * 128, gT[:, 2, :], junk, xs_bf, stat, r_tmp)
                if i == 0:
                    P.dma("sp", hst[:NMETA, :], h1_d[SEQ:SEQ + NMETA, :], reads=[r_h1d[NSLOT]], writes=[r_hst])
                    P.dma("sp", halo[:, :], h1_d[NT_OWN * 128 + 112:NT_OWN * 128 + 128, :], reads=[r_h1d[NT_OWN]],
                          writes=[r_halo])
                    P.op("dve", lambda e: e.tensor_scalar(out=hst[:NMETA, :], in0=hst[:NMETA, :],
                                                          scalar1=selc[:NMETA, 0:1], scalar2=None, op0=ALU.mult),
                         reads=[r_hst, r_cc], writes=[r_hst])
                    P.op("dve", lambda e: e.scalar_tensor_tensor(out=halo[:, :], in0=halo[:, :],
                                                                 scalar=selc[:NMETA, 1:2], in1=hst[:NMETA, :],
                                                                 op0=ALU.mult, op1=ALU.add),
                         reads=[r_hst, r_halo, r_cc], writes=[r_halo])
                else:
                    sl = NT_OWN + i
                    P.dma("sp", halo[:, :], h1_d[sl * 128 + 112:sl * 128 + 128, :], reads=[r_h1d[sl]], writes=[r_halo])
                norm_transpose(halo[:, :], [r_halo], NMETA, xn2h, r_xn2h, t * 16, gT[:, 2, :], junk, xs_bf, stat, r_tmp)
            for g in range(4):
                wi = stc[0] % 2
                stc[0] += 1
                P.dma("pool", wup[wi], wview(w_in, 0, D, g * 256, 256), writes=[r_wup[wi]])
                nsteps = g + 1
                for sub in range(2):
                    def mmu(e, bank, rhs, n, sub=sub, wi=wi):
                        ins = None
                        for kc in range(16):
                            ins = e.matmul(psum[bank][:, :n], wup[wi][:, kc, sub * 128:(sub + 1) * 128], rhs(kc),
                                           start=(kc == 0), stop=(kc == 15))
                        return ins
                    ba, bb = (0, 1) if sub == 0 else (2, 3)
                    P.op("pe", lambda e, f=mmu, ba=ba: f(e, ba, lambda kc: xn2m[:, kc, :], 512),
                         reads=[r_wup[wi], r_xn2m], writes=[psr[ba]])
                    P.op("pe", lambda e, f=mmu, bb=bb: f(e, bb, lambda kc: xn2h[:, kc, :], 64),
                         reads=[r_wup[wi], r_xn2h], writes=[psr[bb]])
                    uv = uext[:, sub, :].rearrange("p (s c) -> p s c", c=144)
                    P.op("act", lambda e, ba=ba, uv=uv: e.activation(
                        out=uv[:, :, 16:144], in_=psum[ba][:, :].rearrange("p (s c) -> p s c", c=128), func=AF.Copy),
                        reads=[psr[ba]], writes=[r_uext[sub]])
                    P.op("act", lambda e, bb=bb, uv=uv: e.activation(
                        out=uv[:, :, 0:16], in_=psum[bb][:, :64].rearrange("p (s c) -> p s c", c=16), func=AF.Copy),
                        reads=[psr[bb], r_uext[sub]], writes=[r_uext[sub]])
                    cur = uext[:, sub, :]
                    rcur = r_uext[sub]
                    for k in range(nsteps):
                        sh = 1 << k
                        nx = ptmp[k % 2]
                        P.op("dve", lambda e, cur=cur, nx=nx, sh=sh: e.tensor_tensor(
                            out=nx[:, sh:576], in0=cur[:, sh:576], in1=cur[:, 0:576 - sh], op=ALU.add),
                            reads=[rcur, r_ptmp], writes=[r_ptmp])
                        cur = nx[:, :]
                        rcur = r_ptmp
                    wnd = float(1 << nsteps)
                    P.op("dve", lambda e, cur=cur, sub=sub, wnd=wnd: e.scalar_tensor_tensor(
                        out=dext[:, sub, 16:576], in0=cur[:, 16:576], scalar=1.0 / wnd, in1=uext[:, sub, 16:576],
                        op0=ALU.mult, op1=ALU.subtract),
                        reads=[r_ptmp, r_uext[sub]], writes=[r_dext])
                P.op("dve", lambda e: e.memset(dext[:, :, 0:16], 0.0), reads=[], writes=[r_dext])
                for oc2 in range(2):
                    for blk in range(2):
                        bk = 4 + (2 * oc2 + blk) % 4

                        def mmpw(e, bk=bk, oc2=oc2, blk=blk, g=g):
                            ins = None
                            for k2 in range(2):
                                ins = e.matmul(psum[bk][:, :288], pw[:, g, k2, oc2 * 128:(oc2 + 1) * 128],
                                               dext[:, k2, blk * 288:(blk + 1) * 288], start=(k2 == 0), stop=(k2 == 1))
                            return ins
                        P.op("pe", mmpw, reads=[r_dext, r_cc], writes=[psr[bk]])
                        c = 2 * g + oc2
                        P.op("act", lambda e, bk=bk, c=c, blk=blk: e.activation(
                            out=ypT[:, c, blk * 256:(blk + 1) * 256].rearrange("p (s c) -> p s c", c=128),
                            in_=psum[bk][:, :288].rearrange("p (s c) -> p s c", c=144)[:, :, 16:144],
                            func=AF.Copy, scale=psc[:, c:c + 1]),
                            reads=[psr[bk], r_cc], writes=[r_ypT])
            for oc in range(16):
                sub = oc % 2
                if sub == 0:
                    wi = stc[0] % 2
                    stc[0] += 1
                    W = wst[wi]
                    P.dma("pool", W["po"], wview(w_pool_o, 0, POOLW, oc * 128, 256), writes=[W["r"]])
                    P.dma("pool", W["gp"], wview(w_in, 0, D, 2112 + oc * 128, 256), writes=[W["r"]], append=True)
                    P.dma("pool", W["mo"], wview(w_mla_o, 0, D, oc * 128, 256), writes=[W["r"]], append=True)
                    P.dma("pool", W["gm"], wview(w_in, 0, D, 4160 + oc * 128, 256), writes=[W["r"]], append=True)
                bs = (0, 1, 2, 3) if oc % 2 == 0 else (4, 5, 6, 7)

                def mmc(e, bank, wt, rhs, nk, sub=sub):
                    ins = None
                    for kc in range(nk):
                        ins = e.matmul(psum[bank][:, :], wt[:, kc, sub * 128:(sub + 1) * 128], rhs(kc),
                                       start=(kc == 0), stop=(kc == nk - 1))
                    return ins
                P.op("pe", lambda e, f=mmc, W=W, b=bs[0]: f(e, b, W["po"], lambda kc: ypT[:, kc, :], 8),
                     reads=[W["r"], r_ypT], writes=[psr[bs[0]]])
                P.op("pe", lambda e, f=mmc, W=W, b=bs[1]: f(e, b, W["gp"], lambda kc: xn2m[:, kc, :], 16),
                     reads=[W["r"], r_xn2m], writes=[psr[bs[1]]])
                P.op("pe", lambda e, f=mmc, W=W, b=bs[2]: f(e, b, W["mo"], lambda kc: oTs[:, kc, :], 16),
                     reads=[W["r"], r_oTs], writes=[psr[bs[2]]])
                P.op("pe", lambda e, f=mmc, W=W, b=bs[3]: f(e, b, W["gm"], lambda kc: xn2m[:, kc, :], 16),
                     reads=[W["r"], r_xn2m], writes=[psr[bs[3]]])
                P.op("act", lambda e, b=bs[1]: e.activation(out=sgt[0][:, :], in_=psum[b][:, :], func=AF.Sigmoid),
                     reads=[psr[bs[1]]], writes=[r_sgt[0]])
                P.op("act", lambda e, b=bs[3]: e.activation(out=sgt[1][:, :], in_=psum[b][:, :], func=AF.Sigmoid),
                     reads=[psr[bs[3]]], writes=[r_sgt[1]])
                P.op("dve", lambda e, b=bs[0]: e.tensor_tensor(out=sgt[2][:, :], in0=psum[b][:, :], in1=sgt[0][:, :],
                                                              op=ALU.mult),
                     reads=[psr[bs[0]], r_sgt[0]], writes=[r_sgt[2]])
                P.op("dve", lambda e, b=bs[2]: e.tensor_tensor(out=sgt[3][:, :], in0=psum[b][:, :], in1=sgt[1][:, :],
                                                              op=ALU.mult),
                     reads=[psr[bs[2]], r_sgt[1]], writes=[r_sgt[3]])
                P.op("dve", lambda e, oc=oc: e.tensor_tensor(out=yT[:, oc, :], in0=sgt[2][:, :], in1=sgt[3][:, :],
                                                            op=ALU.add),
                     reads=[r_sgt[2], r_sgt[3]], writes=[r_yT])
            for e8 in range(8):
                wi = stc[0] % 2
                stc[0] += 1
                P.dma("pool", wo[wi], wview(w_out, 0, D, e8 * 256, 256), writes=[r_wo[wi]])
                for t in range(4):
                    bk = (e8 * 4 + t) % 8

                    def mmo(e, bk=bk, t=t, wi=wi):
                        ins = None
                        for kc in range(16):
                            ins = e.matmul(psum[bk][:, :256], yT[:, kc, t * 128:(t + 1) * 128], wo[wi][:, kc, :],
                                           start=(kc == 0), stop=(kc == 15))
                        return ins
                    P.op("pe", mmo, reads=[r_yT, r_wo[wi]], writes=[psr[bk]])
                    P.op("act", lambda e, bk=bk, t=t, e8=e8: e.activation(out=m_sb[t][:, e8 * 256:(e8 + 1) * 256],
                                                                         in_=psum[bk][:, :256], func=AF.Copy),
                         reads=[psr[bk]], writes=[r_m[t]])
            for t, i in enumerate(own_tiles):
                P.op("act", lambda e, t=t: e.activation(out=junk[:, :], in_=m_sb[t][:, :], func=AF.Square,
                                                        accum_out=stat2[:, 0:1]),
                     reads=[r_m[t]], writes=[r_tmp2])
                P.op("act", lambda e: e.activation(out=stat2[:, 1:2], in_=stat2[:, 0:1], func=AF.Sqrt, scale=1.0 / D,
                                                   bias=epsb[:, 0:1]),
                     reads=[r_tmp2, r_const], writes=[r_tmp2])
                P.op("dve", lambda e: e.reciprocal(out=stat2[:, 2:3], in_=stat2[:, 1:2]), reads=[r_tmp2], writes=[r_tmp2])
                P.dma("sp", hst[:, :], h1_d[i * 128:(i + 1) * 128, :], reads=[r_h1d[i]], writes=[r_hst])
                P.op("dve", lambda e, t=t: e.scalar_tensor_tensor(out=m_sb[t][:, :], in0=m_sb[t][:, :],
                                                                 scalar=stat2[:, 2:3], in1=growm[:, :],
                                                                 op0=ALU.mult, op1=ALU.mult),
                     reads=[r_tmp2, r_cc, r_m[t]], writes=[r_m[t]])
                P.op("dve", lambda e, t=t: e.tensor_tensor(out=m_sb[t][:, :], in0=m_sb[t][:, :], in1=hst[:, :], op=ALU.add),
                     reads=[r_m[t], r_hst], writes=[r_m[t]])
                P.dma("sp", h2_d[i * 128:(i + 1) * 128, :], m_sb[t][:, :], reads=[r_m[t]], writes=[r_h2d[i]])
                if stop_after == "c1":
                    dump(m_sb[t][:, :], [r_m[t]], t * 128, 128, 0, D)
        P.barrier()

    if stop_after == "all" and "c2" in phases:
      with contextlib.ExitStack() as ph:
        sb = alloc_ffn(ph, 6, 'c2_', T=768)
        load_grow(sb, 5)
        for own_ in ([0, 1, 2, 3, 4, 5], [6, 7, 8, 9, 10], [11, 12, 13, 14, 15]):
            tiles = []
            for i in own_:
                tiles.append(dict(nt=128, slot=i, load=(lambda dst, rd, i=i: P.dma(
                    "sp", dst, h2_d[i * 128:(i + 1) * 128, :], reads=[r_h2d[i]], writes=rd))))

            def done2(ti, y_ap, ry, tiles=tiles):
                i = tiles[ti]["slot"]
                P.dma("sp", out_d[i * 128:(i + 1) * 128, :], y_ap, reads=ry, writes=[Res()])
            for t in tiles:
                t["done"] = done2
            ffn_pass(1, tiles, sb)
        P.barrier()

    P.barrier()
    with nc.Block() as block:
        P.emit(block)
    es.close()
    return nc


def _prep_inputs(inputs):
    x = np.asarray(inputs["x"], dtype=np.float32)
    B = x.shape[0]
    gains = np.stack([np.asarray(inputs[k], np.float32)[0] for k in
                      ("norm_ffn1_pre", "norm_ffn1_post", "norm_mix_pre", "norm_mix_post",
                       "norm_ffn2_pre", "norm_ffn2_post")], axis=0)
    common = {
        "meta": np.ascontiguousarray(inputs["meta_tokens"], dtype=np.float32),
        "gains": np.ascontiguousarray(gains),
        "gainsT": np.ascontiguousarray(gains.reshape(6, 16, 128).transpose(2, 0, 1).reshape(128, 96)),
        "ffn1_w_gu": np.asarray(inputs["ffn1_w_gu"], np.float32)[0],
        "ffn2_w_gu": np.asarray(inputs["ffn2_w_gu"], np.float32)[0],
        "ffn1_w_down": np.asarray(inputs["ffn1_w_down"], np.float32)[0],
        "ffn2_w_down": np.asarray(inputs["ffn2_w_down"], np.float32)[0],
        "w_in": np.asarray(inputs["w_in"], np.float32)[0],
        "pool_w": np.asarray(inputs["pool_w"], np.float32)[0],
        "pool_scale": np.asarray(inputs["pool_scale"], np.float32)[0],
        "w_pool_o": np.asarray(inputs["w_pool_o"], np.float32)[0],
        "q_a_norm": np.asarray(inputs["q_a_norm"], np.float32)[0],
        "w_q_b": np.asarray(inputs["w_q_b"], np.float32)[0],
        "kv_a_norm": np.asarray(inputs["kv_a_norm"], np.float32)[0],
        "w_kv_b": np.asarray(inputs["w_kv_b"], np.float32)[0],
        "w_mla_o": np.asarray(inputs["w_mla_o"], np.float32)[0],
        "w_out": np.asarray(inputs["w_out"], np.float32)[0],
        "ident": np.eye(128, dtype=np.float32),
        "latgT": np.ascontiguousarray(np.concatenate([
            np.asarray(inputs["q_a_norm"], np.float32)[0].reshape(4, 128).T,
            np.asarray(inputs["kv_a_norm"], np.float32)[0].reshape(4, 128).T], axis=1)),
        "pscT": np.ascontiguousarray(np.asarray(inputs["pool_scale"], np.float32)[0].reshape(8, 128).T),
        "tri": np.triu(np.ones((128, 128), np.float32)),
    }
    inv = (10000.0 ** (-np.arange(0, ROPE, 2, dtype=np.float32) / ROPE)).astype(np.float32)
    in_maps = []
    tile_maps = []
    for core in range(8):
        b, c = core // 2, core % 2
        own = [2 * i + c for i in range(16)]
        if c == 1:
            oth = [2 * i for i in range(16)]
        else:
            oth = [31] + [2 * i - 1 for i in range(1, 16)]
        order = own + oth
        xt = x[b].reshape(32, 128, D)[order].reshape(SEQ, D)
        pos = np.concatenate([np.arange(NMETA)] + [NMETA + j * 128 + np.arange(128) for j in order]).astype(np.float32)
        ang = pos[None, :] * np.concatenate([inv, inv])[:, None]
        sel = np.zeros((128, 4), np.float32)
        if c == 0:
            sel[:, 0] = 1.0
            sel[:, 2] = -30000.0
        else:
            sel[:, 1] = 1.0
        m = dict(common)
        m["xs"] = np.ascontiguousarray(xt)
        m["cosT"] = np.cos(ang).astype(np.float32)
        m["sinT"] = np.sin(ang).astype(np.float32)
        m["sel"] = sel
        in_maps.append(m)
        tile_maps.append(own)
    return in_maps, tile_maps


_NC_CACHE = {}


def kernel(**inputs):
    in_maps, tile_maps = _prep_inputs(inputs)
    if "nc" not in _NC_CACHE:
        _NC_CACHE["nc"] = build_program()
    nc = _NC_CACHE["nc"]
    res = run_bass_kernel_spmd(nc, in_maps, core_ids=list(range(8)))
    out = np.empty((4, SEQ, D), np.float32)
    for core in range(8):
        b = core // 2
        o = res.results[core]["out"].reshape(16, 128, D)
        for i, j in enumerate(tile_maps[core]):
            out[b, j * 128:(j + 1) * 128] = o[i]
    return out
```
